# Optimizing a Trainium2 kernel written in Bass

```python
import jax, jax.numpy as jnp
from jax import lax
import numpy as np

D_MODEL = 1024
BATCH = 4
SEQ = 8192
DEPTH = 2

N_EVEN = (DEPTH + 1) // 2
N_ODD = DEPTH // 2

DN_ALPHA = (2.0 * DEPTH) ** 0.25
DN_BETA = (8.0 * DEPTH) ** -0.25
LN_EPS = 1e-5

D_FF = 2816
FFN_RES = 0.5

CONV_DIM = D_MODEL // 2
CONV_WIDTH = 31
SB_HEADS = 8
SB_HEAD_DIM = 64
SB_DIM = SB_HEADS * SB_HEAD_DIM
Q_BLOCK = 128
IN0_DIM = 2 * CONV_DIM + 3 * SB_DIM
MIX0_DIM = CONV_DIM + SB_DIM

RW_HEADS = 8
RW_HEAD_DIM = 64
RW_DIM = RW_HEADS * RW_HEAD_DIM
DECAY_LORA = 64
ICLR_LORA = 64
GATE_LORA = 128
RW_IN_DIM = 3 * RW_DIM + DECAY_LORA + ICLR_LORA + GATE_LORA
RW_SPLITS = (RW_DIM, 2 * RW_DIM, 3 * RW_DIM, 3 * RW_DIM + DECAY_LORA, 3 * RW_DIM + DECAY_LORA + ICLR_LORA)
GN_EPS = RW_HEAD_DIM * 1e-5
POOL_WINDOWS = (2, 4, 8, 16)
POOL_GROUPS = 4
POOL_GROUP_DIM = 128
POOL_DIM = POOL_GROUPS * POOL_GROUP_DIM
IN1_DIM = RW_IN_DIM + POOL_DIM
MIX1_DIM = RW_DIM + POOL_DIM

kernel_name = "hybrid_conv_stickbreak_rwkv7_pool_deepnorm"


def layer_norm(x, g, b, eps=LN_EPS):
    xf = x.astype(jnp.float32)
    mu = jnp.mean(xf, axis=-1, keepdims=True)
    var = jnp.mean(jnp.square(xf - mu), axis=-1, keepdims=True)
    y = (xf - mu) * lax.rsqrt(var + eps)
    return (y * g.astype(jnp.float32) + b.astype(jnp.float32)).astype(x.dtype)


def swiglu(x, w_in, w_out):
    gate, up = jnp.split(x @ w_in, 2, axis=-1)
    return (jax.nn.silu(gate) * up) @ w_out


def conformer_conv(u, w_dw, b_dw, g, b):
    a, gate = jnp.split(u, 2, axis=-1)
    h = a * jax.nn.sigmoid(gate)
    h = lax.conv_general_dilated(h, w_dw[:, None, :], window_strides=(1,),
                                 padding=[(CONV_WIDTH - 1, 0)],
                                 dimension_numbers=('NWC', 'WIO', 'NWC'),
                                 feature_group_count=CONV_DIM) + b_dw
    return jax.nn.silu(layer_norm(h, g, b))


def stick_breaking_attention(q, k, v):
    seq = q.shape[2]
    scale = SB_HEAD_DIM ** -0.5
    outs = []
    for blk in range(seq // Q_BLOCK):
        start = blk * Q_BLOCK
        end = start + Q_BLOCK
        z = jnp.einsum('bhqd,bhkd->bhqk', q[:, :, start:end], k[:, :, :end],
                       preferred_element_type=jnp.float32) * scale
        mask = jnp.arange(end)[None, :] < jnp.arange(start, end)[:, None]
        log_keep = jnp.where(mask, jax.nn.log_sigmoid(-z), 0.0)
        log_tail = lax.cumsum(log_keep, axis=3, reverse=True) - log_keep
        att = jnp.where(mask, jnp.exp(jax.nn.log_sigmoid(z) + log_tail), 0.0)
        outs.append(jnp.einsum('bhqk,bhkd->bhqd', att.astype(v.dtype), v[:, :, :end]))
    return jnp.concatenate(outs, axis=2)


def even_mixer(x, w_in, w_dw, b_dw, conv_g, conv_b, w_out):
    bsz, seq, _ = x.shape
    h = x @ w_in
    y_conv = conformer_conv(h[..., :2 * CONV_DIM], w_dw, b_dw, conv_g, conv_b)
    qkv = h[..., 2 * CONV_DIM:].reshape(bsz, seq, 3, SB_HEADS, SB_HEAD_DIM)
    qkv = jnp.transpose(qkv, (2, 0, 3, 1, 4))
    y_att = stick_breaking_attention(qkv[0], qkv[1], qkv[2])
    y_att = jnp.transpose(y_att, (0, 2, 1, 3)).reshape(bsz, seq, SB_DIM)
    return jnp.concatenate([y_conv, y_att], axis=-1) @ w_out


def token_shift(p, mu):
    p_prev = jnp.pad(p, ((0, 0), (1, 0), (0, 0)))[:, :-1]
    return p + (p_prev - p) * mu


def rwkv7_recurrence(r, w, k, v, a, b):
    bsz, seq, nh, nd = r.shape

    def step(state, inp):
        r_t, w_t, k_t, v_t, a_t, b_t = inp
        sa = jnp.einsum('bhij,bhj->bhi', state, a_t)
        state = (state * w_t[:, :, None, :] + sa[..., :, None] * b_t[..., None, :]
                 + v_t[..., :, None] * k_t[..., None, :])
        return state, jnp.einsum('bhij,bhj->bhi', state, r_t)

    xs = (jnp.moveaxis(r, 1, 0), jnp.moveaxis(w, 1, 0), jnp.moveaxis(k, 1, 0),
          jnp.moveaxis(v, 1, 0), jnp.moveaxis(a, 1, 0), jnp.moveaxis(b, 1, 0))
    state0 = jnp.zeros((bsz, nh, nd, nd), jnp.float32)
    _, y = lax.scan(step, state0, xs)
    return jnp.moveaxis(y, 0, 1)


def rwkv7_time_mix(p, mu, w0, w2, a0, a2, g2, k_k, k_a, r_k, lnx_g, lnx_b):
    bsz, seq, _ = p.shape
    f32 = jnp.float32
    p = token_shift(p, mu)
    r, k, v, w_lr, a_lr, g_lr = jnp.split(p, RW_SPLITS, axis=-1)
    log_w = -jax.nn.softplus(-(w0 + jnp.tanh(w_lr) @ w2).astype(f32)) - 0.5
    decay = jnp.exp(-jnp.exp(log_w))
    iclr = jax.nn.sigmoid((a0 + a_lr @ a2).astype(f32))
    gate = (jax.nn.sigmoid(g_lr) @ g2).astype(f32)

    def heads(t):
        return t.astype(f32).reshape(bsz, seq, RW_HEADS, RW_HEAD_DIM)

    r_h, k_h, v_h, w_h, a_h = heads(r), heads(k), heads(v), heads(decay), heads(iclr)
    kk = k_h * k_k.astype(f32)
    kk = kk * lax.rsqrt(jnp.maximum(jnp.sum(kk * kk, axis=-1, keepdims=True), 1e-24))
    k_h = k_h * (1.0 + (a_h - 1.0) * k_a.astype(f32))
    y = rwkv7_recurrence(r_h, w_h, k_h, v_h, -kk, kk * a_h)
    y = layer_norm(y, lnx_g, lnx_b, GN_EPS)
    y = y + jnp.sum(r_h * k_h * r_k.astype(f32), axis=-1, keepdims=True) * v_h
    return (y.reshape(bsz, seq, RW_DIM) * gate).astype(p.dtype)


def multiscale_pool(u, w_pool, b_pool, scale):
    bsz, seq, _ = u.shape
    ug = u.astype(jnp.float32).reshape(bsz, seq, POOL_GROUPS, POOL_GROUP_DIM)
    csum = jnp.pad(jnp.cumsum(ug, axis=1), ((0, 0), (1, 0), (0, 0), (0, 0)))
    t = jnp.arange(seq)
    outs = []
    for gi, win in enumerate(POOL_WINDOWS):
        c = csum[:, :, gi]
        lo = jnp.pad(c, ((0, 0), (win - 1, 0), (0, 0)))[:, :seq]
        count = jnp.minimum(t + 1, win).astype(jnp.float32)[None, :, None]
        outs.append((c[:, 1:] - lo) / count)
    pooled = (jnp.stack(outs, axis=2) - ug).astype(u.dtype)
    y = jnp.einsum('bsgc,gcd->bsgd', pooled, w_pool) + b_pool
    return y.reshape(bsz, seq, POOL_DIM) * scale


def odd_mixer(x, w_in, mu, w0, w2, a0, a2, g2, k_k, k_a, r_k, lnx_g, lnx_b,
              w_pool, b_pool, pool_scale, w_out):
    h = x @ w_in
    y_rw = rwkv7_time_mix(h[..., :RW_IN_DIM], mu, w0, w2, a0, a2, g2, k_k, k_a, r_k,
                          lnx_g, lnx_b)
    y_pool = multiscale_pool(h[..., RW_IN_DIM:], w_pool, b_pool, pool_scale)
    return jnp.concatenate([y_rw, y_pool], axis=-1) @ w_out


def setup_inputs(seed: int = 0) -> dict:
    key = jax.random.key(seed)
    ks = jax.random.split(key, 27)
    f32 = jnp.float32

    def nrm(k, shape, std):
        return std * jax.random.normal(k, shape, f32)

    return {
        "x": jax.random.normal(ks[0], (BATCH, SEQ, D_MODEL), f32),
        "ffn_in": nrm(ks[1], (DEPTH, 2, D_MODEL, 2 * D_FF), D_MODEL ** -0.5),
        "ffn_out": nrm(ks[2], (DEPTH, 2, D_FF, D_MODEL), DN_BETA * D_FF ** -0.5),
        "ln_g": 1.0 + nrm(ks[3], (DEPTH, 3, D_MODEL), 0.02),
        "ln_b": nrm(ks[4], (DEPTH, 3, D_MODEL), 0.02),
        "e_w_in": nrm(ks[5], (N_EVEN, D_MODEL, IN0_DIM), D_MODEL ** -0.5),
        "e_w_dw": nrm(ks[6], (N_EVEN, CONV_WIDTH, CONV_DIM), CONV_WIDTH ** -0.5),
        "e_b_dw": nrm(ks[7], (N_EVEN, CONV_DIM), 0.02),
        "e_conv_g": 1.0 + nrm(ks[8], (N_EVEN, CONV_DIM), 0.02),
        "e_conv_b": nrm(ks[9], (N_EVEN, CONV_DIM), 0.02),
        "e_w_out": nrm(ks[10], (N_EVEN, MIX0_DIM, D_MODEL), DN_BETA * MIX0_DIM ** -0.5),
        "o_w_in": nrm(ks[11], (N_ODD, D_MODEL, IN1_DIM), D_MODEL ** -0.5),
        "o_mu": jax.random.uniform(ks[12], (N_ODD, RW_IN_DIM), f32, 0.0, 1.0),
        "o_w0": jax.random.uniform(ks[13], (N_ODD, RW_DIM), f32, -6.0, -1.0),
        "o_w2": nrm(ks[14], (N_ODD, DECAY_LORA, RW_DIM), 0.1 * DECAY_LORA ** -0.5),
        "o_a0": nrm(ks[15], (N_ODD, RW_DIM), 0.1),
        "o_a2": nrm(ks[16], (N_ODD, ICLR_LORA, RW_DIM), 0.1 * ICLR_LORA ** -0.5),
        "o_g2": nrm(ks[17], (N_ODD, GATE_LORA, RW_DIM), GATE_LORA ** -0.5),
        "o_k_k": 0.85 + nrm(ks[18], (N_ODD, RW_HEADS, RW_HEAD_DIM), 0.02),
        "o_k_a": 1.0 + nrm(ks[19], (N_ODD, RW_HEADS, RW_HEAD_DIM), 0.02),
        "o_r_k": nrm(ks[20], (N_ODD, RW_HEADS, RW_HEAD_DIM), 0.1),
        "o_lnx_g": 1.0 + nrm(ks[21], (N_ODD, RW_HEADS, RW_HEAD_DIM), 0.02),
        "o_lnx_b": nrm(ks[22], (N_ODD, RW_HEADS, RW_HEAD_DIM), 0.02),
        "o_w_pool": nrm(ks[23], (N_ODD, POOL_GROUPS, POOL_GROUP_DIM, POOL_GROUP_DIM), POOL_GROUP_DIM ** -0.5),
        "o_b_pool": nrm(ks[24], (N_ODD, POOL_GROUPS, POOL_GROUP_DIM), 0.02),
        "o_pool_scale": 0.5 + nrm(ks[25], (N_ODD, POOL_DIM), 0.1),
        "o_w_out": nrm(ks[26], (N_ODD, MIX1_DIM, D_MODEL), DN_BETA * MIX1_DIM ** -0.5),
    }


def reference(x, ffn_in, ffn_out, ln_g, ln_b,
              e_w_in, e_w_dw, e_b_dw, e_conv_g, e_conv_b, e_w_out,
              o_w_in, o_mu, o_w0, o_w2, o_a0, o_a2, o_g2, o_k_k, o_k_a, o_r_k,
              o_lnx_g, o_lnx_b, o_w_pool, o_b_pool, o_pool_scale, o_w_out):
    for layer in range(DEPTH):
        f = swiglu(x, ffn_in[layer, 0], ffn_out[layer, 0])
        x = layer_norm(DN_ALPHA * x + FFN_RES * f, ln_g[layer, 0], ln_b[layer, 0])
        if layer % 2 == 0:
            i = layer // 2
            m = even_mixer(x, e_w_in[i], e_w_dw[i], e_b_dw[i], e_conv_g[i], e_conv_b[i], e_w_out[i])
        else:
            i = layer // 2
            m = odd_mixer(x, o_w_in[i], o_mu[i], o_w0[i], o_w2[i], o_a0[i], o_a2[i], o_g2[i],
                          o_k_k[i], o_k_a[i], o_r_k[i], o_lnx_g[i], o_lnx_b[i],
                          o_w_pool[i], o_b_pool[i], o_pool_scale[i], o_w_out[i])
        x = layer_norm(DN_ALPHA * x + m, ln_g[layer, 1], ln_b[layer, 1])
        f = swiglu(x, ffn_in[layer, 1], ffn_out[layer, 1])
        x = layer_norm(DN_ALPHA * x + FFN_RES * f, ln_g[layer, 2], ln_b[layer, 2])
    return x
```

```python
import numpy as np
from contextlib import ExitStack
import concourse.bass as bass
import concourse.mybir as mybir
from concourse.bass_utils import run_bass_kernel_spmd

F32 = mybir.dt.float32
BF16 = mybir.dt.bfloat16
AF = mybir.ActivationFunctionType
ALU = mybir.AluOpType
AX = mybir.AxisListType

D_MODEL = 1024
D_FF = 2816
DEPTH = 2
DN_ALPHA = (2.0 * DEPTH) ** 0.25
LN_EPS = 1e-5
ARENA_BYTES = 207 * 1024


class Op:
    __slots__ = ("eng", "fn", "idx", "deps", "dma", "dslot", "dval", "signal",
                 "sig", "waits")

    def __init__(self, eng, fn, idx, dma):
        self.eng = eng
        self.fn = fn
        self.idx = idx
        self.dma = dma
        self.deps = set()
        self.dslot = -1
        self.dval = 0
        self.signal = False
        self.sig = 0
        self.waits = []


class Tile:
    def __init__(self, ap, name=""):
        self.ap = ap
        self.name = name

    def __getitem__(self, k):
        return self.ap[k]


class Prog:
    CENG = ("pe", "act", "dve", "pool")

    def __init__(self, nc, es, n_dma_sems=48):
        self.nc = nc
        self.eng = {"pe": nc.tensor, "act": nc.scalar, "dve": nc.vector,
                    "pool": nc.gpsimd, "sp": nc.sync}
        self.sem = {e: es.enter_context(nc.semaphore("sem_" + e)) for e in self.CENG}
        self.dsem = [es.enter_context(nc.semaphore("dsem%d" % i)) for i in range(n_dma_sems)]
        self.dsem_val = [0] * n_dma_sems
        self.dsem_last = [None] * n_dma_sems
        self.dnext = 0
        self.ops = []
        self.last_w = {}
        self.readers = {}
        self.count = {e: 0 for e in self.eng}
        self.last_op = {e: None for e in self.eng}
        self.arena = es.enter_context(nc.sbuf_tensor("arena", [128, ARENA_BYTES // 4], F32))
        self.cur = 0
        self.psum = [es.enter_context(nc.psum_tensor("ps%d" % i, [128, 512], F32)) for i in range(8)]
        self.ps_tiles = [Tile(p, "ps%d" % i) for i, p in enumerate(self.psum)]

    def alloc(self, free_shape, dtype, name=""):
        n = 1
        for s in free_shape:
            n *= s
        esz = 4 if dtype == F32 else 2
        nbytes = (n * esz + 63) // 64 * 64
        off = self.cur
        self.cur += nbytes
        assert self.cur <= ARENA_BYTES, "arena overflow %s: %d" % (name, self.cur)
        ap = self.arena[:, off // 4:(off + nbytes) // 4]
        if dtype != F32:
            ap = ap.bitcast(dtype)
        ap = ap[:, 0:n]
        if len(free_shape) == 2:
            ap = ap.rearrange("p (a b) -> p a b", a=free_shape[0])
        elif len(free_shape) == 3:
            ap = ap.rearrange("p (a b c) -> p a b c", a=free_shape[0], b=free_shape[1])
        return Tile(ap, name)

    def arena_reset(self, to=0):
        self.cur = to

    def op(self, eng, fn, reads=(), writes=(), dma=False):
        o = Op(eng, fn, self.count[eng], dma)
        self.count[eng] += 1
        deps = o.deps
        for t in reads:
            w = self.last_w.get(t)
            if w is not None:
                deps.add(w)
        for t in writes:
            w = self.last_w.get(t)
            if w is not None:
                deps.add(w)
            rd = self.readers.get(t)
            if rd:
                deps.update(rd.values())
        if dma:
            slot = self.dnext
            self.dnext = (self.dnext + 1) % len(self.dsem)
            prev = self.dsem_last[slot]
            if prev is not None:
                deps.add(prev)
            self.dsem_val[slot] += 16
            o.dslot = slot
            o.dval = self.dsem_val[slot]
            self.dsem_last[slot] = o
        for t in reads:
            key = ("d", id(o)) if dma else eng
            self.readers.setdefault(t, {})[key] = o
        for t in writes:
            self.last_w[t] = o
            self.readers[t] = {}
        self.ops.append(o)
        self.last_op[eng] = o
        return o

    def dma(self, eng, out, in_, reads=(), writes=()):
        return self.op(eng, lambda e: e.dma_start(out=out, in_=in_), reads, writes, dma=True)

    def mm(self, out, lhsT, rhs, start, stop, reads, writes):
        return self.op("pe", lambda e: e.matmul(out, lhsT, rhs, start=start, stop=stop), reads, writes)

    def tr(self, out, in_, ident, reads, writes):
        return self.op("pe", lambda e: e.transpose(out, in_, ident), reads, writes)

    def act(self, out, in_, func, reads, writes, bias=None, scale=None, eng="act", accum_out=None):
        kw = {}
        if bias is not None:
            kw["bias"] = bias
        if scale is not None:
            kw["scale"] = scale
        if accum_out is not None:
            kw["accum_out"] = accum_out
        return self.op(eng, lambda e: e.activation(out=out, in_=in_, func=func, **kw), reads, writes)

    def tt(self, eng, out, in0, in1, op, reads, writes):
        return self.op(eng, lambda e: e.tensor_tensor(out, in0, in1, op), reads, writes)

    def stt(self, eng, out, in0, scalar, in1, op0, op1, reads, writes):
        return self.op(eng, lambda e: e.scalar_tensor_tensor(out, in0, scalar, in1, op0, op1), reads, writes)

    def ts(self, eng, out, in0, s1, s2, op0, op1, reads, writes):
        if s2 is None:
            return self.op(eng, lambda e: e.tensor_scalar(out, in0, s1, None, op0), reads, writes)
        return self.op(eng, lambda e: e.tensor_scalar(out, in0, s1, s2, op0, op1), reads, writes)

    def copy(self, eng, out, in_, reads, writes):
        if eng == "act":
            return self.op(eng, lambda e: e.activation(out=out, in_=in_, func=AF.Copy), reads, writes)
        return self.op(eng, lambda e: e.tensor_copy(out, in_), reads, writes)

    def barrier(self):
        deps = set(o for o in self.last_op.values() if o is not None)
        deps.update(o for o in self.dsem_last if o is not None)
        for e in self.eng:
            o = Op(e, None, self.count[e], False)
            self.count[e] += 1
            o.deps = set(deps)
            self.ops.append(o)

    def finish(self):
        deps = set(o for o in self.last_op.values() if o is not None)
        deps.update(o for o in self.dsem_last if o is not None)
        o = Op("sp", None, self.count["sp"], False)
        o.deps = deps
        self.ops.append(o)
        self.emit()

    def emit(self):
        waited = {e: {} for e in self.eng}
        for o in self.ops:
            need = {}
            for d in o.deps:
                if d is o:
                    continue
                if d.dma:
                    key, val = ("d", d.dslot), d.dval
                else:
                    if d.eng == "pe" and o.eng == "pe":
                        continue
                    key, val = d.eng, d.idx
                if waited[o.eng].get(key, -1) >= val:
                    continue
                if key not in need or need[key][0] < val:
                    need[key] = (val, d)
            for key, (val, d) in need.items():
                waited[o.eng][key] = val
                if not d.dma:
                    d.signal = True
                o.waits.append(d)
        signum = {e: 0 for e in self.eng}
        n_inst = 0
        for o in self.ops:
            e = self.eng[o.eng]
            if o.signal and not o.dma:
                signum[o.eng] += 1
                o.sig = signum[o.eng]
            for d in o.waits:
                if d.dma:
                    e.wait_ge(self.dsem[d.dslot], d.dval)
                else:
                    e.wait_ge(self.sem[d.eng], d.sig)
                n_inst += 1
            if o.fn is None:
                continue
            inst = o.fn(e)
            n_inst += 1
            if o.dma:
                inst.then_inc(self.dsem[o.dslot], 16)
            elif o.signal:
                inst.then_inc(self.sem[o.eng], 1)
        self.n_inst = n_inst
        self.n_sig = dict(signum)
        self.ops = []


def load_w_bf16(P, dst, src_ap, kchunks, ncols, split=1):
    v = src_ap.rearrange("(k p) n -> p k n", p=128)
    step = max(1, kchunks // split)
    for k0 in range(0, kchunks, step):
        k1 = min(kchunks, k0 + step)
        P.dma("pool", dst[:, k0:k1, :], v[:, k0:k1, :], writes=[(dst, k) for k in range(k0, k1)])


def load_bcast(P, dst, vec_ap, n):
    P.dma("sp", dst[:, 0:n], vec_ap.partition_broadcast(128), writes=[dst])


def transpose_block(P, C, x_f32, xbf, xT, tcol, ps_t, nchunks=8, cast_eng="pool"):
    P.copy(cast_eng, xbf[:, 0:nchunks * 128], x_f32[:, 0:nchunks * 128], [x_f32], [xbf])
    psv = ps_t.ap.bitcast(BF16)
    for c in range(nchunks):
        P.tr(psv[:, c * 128:(c + 1) * 128], xbf[:, c * 128:(c + 1) * 128], C["ident"][:, :],
             [xbf, C["ident"]], [ps_t])
    P.copy("dve", xT[:, 0:nchunks, tcol:tcol + 128],
           psv[:, 0:nchunks * 128].rearrange("p (c t) -> p c t", c=nchunks), [ps_t], [xT])


def layer_norm_block(P, r, out, gbc, bbc, stats, mv, eps, D=1024):
    nch = D // 512
    for h in range(nch):
        P.op("dve", lambda e, h=h: e.bn_stats(stats[:, h, :], r[:, h * 512:(h + 1) * 512]),
             reads=[r], writes=[stats])
    P.op("dve", lambda e: e.bn_aggr(mv[:, 0:2], stats[:, 0:nch, :]), reads=[stats], writes=[mv])
    P.ts("dve", mv[:, 2:3], mv[:, 1:2], eps, None, ALU.add, None, [mv], [mv])
    P.act(mv[:, 2:3], mv[:, 2:3], AF.Sqrt, [mv], [mv])
    P.op("dve", lambda e: e.reciprocal(mv[:, 2:3], mv[:, 2:3]), reads=[mv], writes=[mv])
    P.stt("dve", mv[:, 3:4], mv[:, 0:1], -1.0, mv[:, 2:3], ALU.mult, ALU.mult, [mv], [mv])
    P.act(r[:, 0:D], r[:, 0:D], AF.Identity, [r, mv], [r], bias=mv[:, 3:4], scale=mv[:, 2:3])
    P.tt("pool", r[:, 0:D], r[:, 0:D], gbc[:, 0:D], ALU.mult, [r, gbc], [r])
    P.tt("pool", out[:, 0:D], r[:, 0:D], bbc[:, 0:D], ALU.add, [r, bbc], [out])


def tiles_of(nblocks, per):
    out = []
    b = 0
    while b < nblocks:
        n = min(per, nblocks - b)
        out.append((b, n))
        b += n
    return out


def phase_ffn(P, C, x_in, in_blk0, x_out, out_blk0, nblocks, w_in, w_out, ln_g, ln_b):
    P.barrier()
    P.arena_reset(C["arena_base"])
    NF = D_FF // 128
    Win = P.alloc([8, 2 * D_FF], BF16, "Win")
    Wout = P.alloc([NF, D_MODEL], BF16, "Wout")
    gbc = P.alloc([D_MODEL], F32, "gbc")
    bbc = P.alloc([D_MODEL], F32, "bbc")
    TB = C.get('TB', 4)
    xin1 = P.alloc([D_MODEL], F32, "xin")
    xbf1 = P.alloc([D_MODEL], BF16, "xbf")
    xres1 = P.alloc([D_MODEL], F32, "xres")
    xin = [xin1, xin1]
    xbf = [xbf1, xbf1]
    xres = [xres1, xres1]
    xT = [P.alloc([8, TB * 128], BF16, "xT%d" % i) for i in range(2)]
    gT = P.alloc([NF, TB * 128], BF16, "gT")
    sg = [P.alloc([TB * 128], F32, "sg%d" % i) for i in range(2)]
    rr = [P.alloc([D_MODEL], F32, "r%d" % i) for i in range(2)]
    stats = [P.alloc([2, 6], F32, "st%d" % i) for i in range(2)]
    mv = [P.alloc([4], F32, "mv%d" % i) for i in range(2)]

    load_w_bf16(P, Win, w_in, 8, 2 * D_FF, split=8)
    load_w_bf16(P, Wout, w_out, NF, D_MODEL, split=2)
    load_bcast(P, gbc, ln_g, D_MODEL)
    load_bcast(P, bbc, ln_b, D_MODEL)

    psG = [P.ps_tiles[0], P.ps_tiles[1]]
    psU = [P.ps_tiles[2], P.ps_tiles[3]]
    psY = [P.ps_tiles[4], P.ps_tiles[5]]
    psT = P.ps_tiles[6]
    cres = 0.5 / DN_ALPHA
    eps = LN_EPS / (DN_ALPHA * DN_ALPHA)

    tl = tiles_of(nblocks, TB)
    grp = 0
    for ti, (b0, nb) in enumerate(tl):
        NT = nb * 128
        xt = xT[ti % 2]
        for j in range(nb):
            g = b0 + j
            s = g % 2
            P.dma("sp", xin[s][:, :], x_in[(in_blk0 + g) * 128:(in_blk0 + g + 1) * 128, :],
                  writes=[xin[s]])
            transpose_block(P, C, xin[s], xbf[s], xt, j * 128, psT)
        for fp in range(NF):
            pg = psG[grp % 2]
            pu = psU[grp % 2]
            sgt = sg[grp % 2]
            grp += 1
            for kc in range(8):
                P.mm(pg[:, 0:NT], Win[:, kc, fp * 128:(fp + 1) * 128], xt[:, kc, 0:NT],
                     kc == 0, kc == 7, [(Win, kc), xt], [pg])
            for kc in range(8):
                P.mm(pu[:, 0:NT], Win[:, kc, D_FF + fp * 128:D_FF + (fp + 1) * 128], xt[:, kc, 0:NT],
                     kc == 0, kc == 7, [(Win, kc), xt], [pu])
            P.act(sgt[:, 0:NT], pg[:, 0:NT], AF.Silu, [pg], [sgt])
            P.tt("dve", gT[:, fp, 0:NT], sgt[:, 0:NT], pu[:, 0:NT], ALU.mult, [sgt, pu], [(gT, fp)])
        for j in range(nb):
            g = b0 + j
            s = g % 2
            P.dma("sp", xres[s][:, :], x_in[(in_blk0 + g) * 128:(in_blk0 + g + 1) * 128, :],
                  writes=[xres[s]])
            for half in range(2):
                py = psY[half]
                for fc in range(NF):
                    P.mm(py[:, 0:512], gT[:, fc, j * 128:(j + 1) * 128],
                         Wout[:, fc, half * 512:(half + 1) * 512], fc == 0, fc == NF - 1,
                         [(gT, fc), (Wout, fc)], [py])
                P.stt("dve", rr[s][:, half * 512:(half + 1) * 512], py[:, 0:512], cres,
                      xres[s][:, half * 512:(half + 1) * 512], ALU.mult, ALU.add,
                      [py, xres[s]], [rr[s]])
            layer_norm_block(P, rr[s], rr[s], gbc, bbc, stats[s], mv[s], eps)
            P.dma("sp", x_out[(out_blk0 + g) * 128:(out_blk0 + g + 1) * 128, :], rr[s][:, :],
                  reads=[rr[s]])


def setup_consts(P, consts_dram):
    C = {}
    ident = P.alloc([128], BF16, "ident")
    P.dma("pool", ident[:, :], consts_dram["ident"][:, :], writes=[ident])
    C["ident"] = ident
    onesF = P.alloc([128], F32, "onesF")
    P.op("pool", lambda e: e.memset(onesF[:, :], 1.0), writes=[onesF])
    C["onesF"] = onesF
    onesB = P.alloc([128], BF16, "onesB")
    P.op("pool", lambda e: e.memset(onesB[:, :], 1.0), writes=[onesB])
    C["onesB"] = onesB
    trim = P.alloc([128], BF16, "trim")
    P.dma("pool", trim[:, :], consts_dram["trim"][:, :], writes=[trim])
    C["trim"] = trim
    dmask = P.alloc([512], F32, "dmask")
    P.dma("sp", dmask[:, :], consts_dram["dmask"][:, :], writes=[dmask])
    C["dmask"] = dmask
    C["arena_base"] = P.cur
    return C


def host_consts():
    k = np.arange(128)
    trim = (k[:, None] >= k[None, :]).astype(np.float32)
    dm = (k[:, None] < k[None, :]).astype(np.float32)
    return {"ident": np.eye(128, dtype=np.float32), "trim": trim,
            "dmask": np.ascontiguousarray(np.tile(dm, (1, 4)))}


class PsRot:
    def __init__(self, P, banks):
        self.t = [P.ps_tiles[b] for b in banks]
        self.i = 0

    def next(self):
        t = self.t[self.i % len(self.t)]
        self.i += 1
        return t


def load_xT_tile(P, C, x_dram, blk0, nb, xin, xbf, xt, psT, ctr):
    for j in range(nb):
        s = ctr[0] % 2
        ctr[0] += 1
        P.dma("sp", xin[s][:, :], x_dram[(blk0 + j) * 128:(blk0 + j + 1) * 128, :], writes=[xin[s]])
        transpose_block(P, C, xin[s], xbf[s], xt, j * 128, psT)


def phase_inproj_even(P, C, x_in, nblocks, w_in, hcT, qT, kT, vtm):
    P.barrier()
    P.arena_reset(C["arena_base"])
    NCOL = 2560
    We = P.alloc([8, NCOL], BF16, "We")
    wv = w_in.rearrange("(k p) n -> p k n", p=128)
    for k0 in range(0, 8, 2):
        P.dma("pool", We[:, k0:k0 + 2, 0:2048], wv[:, k0:k0 + 2, 0:2048],
              writes=[(We, k) for k in range(k0, k0 + 2)])
    wvv = wv[:, :, 2048:2560].rearrange("p k (h two d) -> p k two h d", two=2, d=64)
    for par in range(2):
        for k in range(8):
            P.dma("pool", We[:, k, 2048 + par * 256:2048 + (par + 1) * 256].rearrange(
                "p (h d) -> p h d", d=64), wvv[:, k, par], writes=[(We, ("v", par, k))])
    TB = 4
    xin = [P.alloc([D_MODEL], F32, "xin%d" % i) for i in range(2)]
    xbf = [P.alloc([D_MODEL], BF16, "xbf%d" % i) for i in range(2)]
    xT = [P.alloc([8, TB * 128], BF16, "xT%d" % i) for i in range(2)]
    sgm = [P.alloc([TB * 128], F32, "sgm%d" % i) for i in range(2)]
    ob = [P.alloc([TB * 128], BF16, "ob%d" % i) for i in range(4)]
    psr = PsRot(P, [0, 1, 2, 3, 4, 5])
    psT = P.ps_tiles[6]
    ctr = [0]
    oi = 0
    for ti, (b0, nb) in enumerate(tiles_of(nblocks, TB)):
        NT = nb * 128
        t0 = b0 * 128
        xt = xT[ti % 2]
        load_xT_tile(P, C, x_in, b0, nb, xin, xbf, xt, psT, ctr)
        for c in range(4):
            pa = psr.next()
            pg = psr.next()
            for kc in range(8):
                P.mm(pa[:, 0:NT], We[:, kc, c * 128:(c + 1) * 128], xt[:, kc, 0:NT],
                     kc == 0, kc == 7, [(We, kc), xt], [pa])
            for kc in range(8):
                P.mm(pg[:, 0:NT], We[:, kc, 512 + c * 128:512 + (c + 1) * 128], xt[:, kc, 0:NT],
                     kc == 0, kc == 7, [(We, kc), xt], [pg])
            sg = sgm[c % 2]
            P.act(sg[:, 0:NT], pg[:, 0:NT], AF.Sigmoid, [pg], [sg])
            o = ob[oi % 4]
            oi += 1
            P.tt("dve", o[:, 0:NT], sg[:, 0:NT], pa[:, 0:NT], ALU.mult, [sg, pa], [o])
            P.dma("sp", hcT[c * 128:(c + 1) * 128, t0:t0 + NT], o[:, 0:NT], reads=[o])
        for c in range(8):
            pq = psr.next()
            col = 1024 + c * 128
            for kc in range(8):
                P.mm(pq[:, 0:NT], We[:, kc, col:col + 128], xt[:, kc, 0:NT],
                     kc == 0, kc == 7, [(We, kc), xt], [pq])
            o = ob[oi % 4]
            oi += 1
            if c < 4:
                P.act(o[:, 0:NT], pq[:, 0:NT], AF.Copy, [pq], [o], scale=0.125)
                P.dma("sp", qT[c * 128:(c + 1) * 128, t0:t0 + NT], o[:, 0:NT], reads=[o])
            else:
                P.copy("dve", o[:, 0:NT], pq[:, 0:NT], [pq], [o])
                P.dma("sp", kT[(c - 4) * 128:(c - 3) * 128, t0:t0 + NT], o[:, 0:NT], reads=[o])
        for j in range(nb):
            pv = psr.next()
            for kc in range(8):
                P.mm(pv[:, 0:512], xt[:, kc, j * 128:(j + 1) * 128], We[:, kc, 2048:2560],
                     kc == 0, kc == 7, [(We, ("v", 0, kc)), (We, ("v", 1, kc)), xt], [pv])
            o = ob[oi % 4]
            oi += 1
            if j % 2 == 0:
                P.act(o[:, 0:512], pv[:, 0:512], AF.Copy, [pv], [o])
            else:
                P.copy("dve", o[:, 0:512], pv[:, 0:512], [pv], [o])
            P.dma("sp", vtm[(b0 + j) * 128:(b0 + j + 1) * 128, :], o[:, 0:512], reads=[o])


def phase_conv(P, C, hcT, S, w_dwT, cprm, mixT):
    P.barrier()
    P.arena_reset(C["arena_base"])
    KW = 31
    PAD = KW - 1
    hc = [P.alloc([PAD + S], BF16, "hc%d" % g) for g in range(4)]
    Dg = P.alloc([4, KW, 128], BF16, "Dg")
    wT = P.alloc([4, KW], F32, "wT")
    prm = P.alloc([4, 3], F32, "prm")
    cv = [P.alloc([512], F32, "cv%d" % g) for g in range(4)]
    sq = [P.alloc([512], F32, "sq%d" % g) for g in range(4)]
    mean_sb = P.alloc([512], F32, "mean")
    rstd = P.alloc([512], F32, "rstd")
    tmp = [P.alloc([512], F32, "tmp%d" % i) for i in range(2)]
    ob = [P.alloc([512], BF16, "ob%d" % i) for i in range(2)]
    P.dma("sp", wT[:, :, :], w_dwT[:, :, :], writes=[wT])
    P.dma("sp", prm[:, :, :], cprm[:, :, :], writes=[(prm, 0), (prm, 1), (prm, 2)])
    for g in range(4):
        P.op("pool", lambda e, g=g: e.memset(hc[g][:, 0:PAD], 0.0), writes=[(hc[g], "pad")])
        P.dma("sp", hc[g][:, PAD:PAD + S], hcT[g * 128:(g + 1) * 128, :], writes=[hc[g]])
        for k in range(KW):
            P.ts("dve", Dg[:, g, k, :], C["ident"][:, :], wT[:, g, k:k + 1], None, ALU.mult, None,
                 [C["ident"], wT], [(Dg, g)])
    psr = PsRot(P, [0, 1, 2, 3])
    psM = P.ps_tiles[4]
    psQ = P.ps_tiles[5]
    oi = 0
    for t0 in range(0, S, 512):
        N = min(512, S - t0)
        for g in range(4):
            pc = psr.next()
            for k in range(KW):
                P.mm(pc[:, 0:N], Dg[:, g, k, :], hc[g][:, t0 + k:t0 + k + N], k == 0, k == KW - 1,
                     [(Dg, g), hc[g], (hc[g], "pad")], [pc])
            P.act(cv[g][:, 0:N], pc[:, 0:N], AF.Identity, [pc, (prm, 0)], [cv[g]], bias=prm[:, g, 0:1])
            P.act(sq[g][:, 0:N], pc[:, 0:N], AF.Square, [pc, (prm, 0)], [sq[g]], bias=prm[:, g, 0:1])
        for g in range(4):
            P.mm(psM[:, 0:N], C["onesF"][:, :], cv[g][:, 0:N], g == 0, g == 3, [C["onesF"], cv[g]], [psM])
        for g in range(4):
            P.mm(psQ[:, 0:N], C["onesF"][:, :], sq[g][:, 0:N], g == 0, g == 3, [C["onesF"], sq[g]], [psQ])
        P.act(mean_sb[:, 0:N], psM[:, 0:N], AF.Copy, [psM], [mean_sb], scale=1.0 / 512)
        P.tt("dve", rstd[:, 0:N], mean_sb[:, 0:N], mean_sb[:, 0:N], ALU.mult, [mean_sb], [rstd])
        P.stt("dve", rstd[:, 0:N], psQ[:, 0:N], 1.0 / 512, rstd[:, 0:N], ALU.mult, ALU.subtract,
              [psQ, rstd], [rstd])
        P.ts("dve", rstd[:, 0:N], rstd[:, 0:N], LN_EPS, None, ALU.add, None, [rstd], [rstd])
        P.act(rstd[:, 0:N], rstd[:, 0:N], AF.Sqrt, [rstd], [rstd])
        P.op("dve", lambda e, N=N: e.reciprocal(rstd[:, 0:N], rstd[:, 0:N]), reads=[rstd], writes=[rstd])
        for g in range(4):
            tp = tmp[g % 2]
            eng = "dve" if g % 2 == 0 else "pool"
            P.tt(eng, tp[:, 0:N], cv[g][:, 0:N], mean_sb[:, 0:N], ALU.subtract, [cv[g], mean_sb], [tp])
            P.tt(eng, tp[:, 0:N], tp[:, 0:N], rstd[:, 0:N], ALU.mult, [tp, rstd], [tp])
            o = ob[oi % 2]
            oi += 1
            P.act(o[:, 0:N], tp[:, 0:N], AF.Silu, [tp, (prm, 1), (prm, 2)], [o],
                  scale=prm[:, g, 1:2], bias=prm[:, g, 2:3])
            P.dma("sp", mixT[g * 128:(g + 1) * 128, t0:t0 + N], o[:, 0:N], reads=[o])


def phase_outproj(P, C, mixT, x_res, x_out, nblocks, w_o, ln_g, ln_b):
    P.barrier()
    P.arena_reset(C["arena_base"])
    Wo = P.alloc([8, D_MODEL], BF16, "Wo")
    load_w_bf16(P, Wo, w_o, 8, D_MODEL, split=2)
    gbc = P.alloc([D_MODEL], F32, "gbc")
    bbc = P.alloc([D_MODEL], F32, "bbc")
    load_bcast(P, gbc, ln_g, D_MODEL)
    load_bcast(P, bbc, ln_b, D_MODEL)
    TB = 4
    mT = [P.alloc([8, TB * 128], BF16, "mT%d" % i) for i in range(2)]
    xres = [P.alloc([D_MODEL], F32, "xres%d" % i) for i in range(2)]
    rr = [P.alloc([D_MODEL], F32, "r%d" % i) for i in range(2)]
    stats = [P.alloc([2, 6], F32, "st%d" % i) for i in range(2)]
    mv = [P.alloc([4], F32, "mv%d" % i) for i in range(2)]
    psr = PsRot(P, [0, 1, 2, 3])
    cres = 1.0 / DN_ALPHA
    eps = LN_EPS / (DN_ALPHA * DN_ALPHA)
    for ti, (b0, nb) in enumerate(tiles_of(nblocks, TB)):
        NT = nb * 128
        mt = mT[ti % 2]
        P.dma("sp", mt[:, :, 0:NT], mixT[:, b0 * 128:b0 * 128 + NT].rearrange("(c p) t -> p c t", p=128),
              writes=[mt])
        for j in range(nb):
            g = b0 + j
            s = g % 2
            P.dma("sp", xres[s][:, :], x_res[g * 128:(g + 1) * 128, :], writes=[xres[s]])
            for half in range(2):
                py = psr.next()
                for fc in range(8):
                    P.mm(py[:, 0:512], mt[:, fc, j * 128:(j + 1) * 128],
                         Wo[:, fc, half * 512:(half + 1) * 512], fc == 0, fc == 7,
                         [mt, (Wo, fc)], [py])
                P.stt("dve", rr[s][:, half * 512:(half + 1) * 512], py[:, 0:512], cres,
                      xres[s][:, half * 512:(half + 1) * 512], ALU.mult, ALU.add,
                      [py, xres[s]], [rr[s]])
            layer_norm_block(P, rr[s], rr[s], gbc, bbc, stats[s], mv[s], eps)
            P.dma("sp", x_out[g * 128:(g + 1) * 128, :], rr[s][:, :], reads=[rr[s]])


ATT_WIN = 3


def phase_attn(P, C, qT, kT, vtm, S, mixT, row0):
    P.barrier()
    P.arena_reset(C["arena_base"])
    NBLK = S // 128
    kTs = P.alloc([4, S], BF16, "kTs")
    qTs = P.alloc([4, S], BF16, "qTs")
    vs = P.alloc([NBLK, 256], BF16, "vs")
    NB2 = 2
    ex = [[P.alloc([512], F32, "ex%d_%d" % (b, d)) for d in range(ATT_WIN)] for b in range(NB2)]
    spb = [[P.alloc([512], BF16, "sp%d_%d" % (b, d)) for d in range(ATT_WIN)] for b in range(NB2)]
    att = [[P.alloc([512], BF16, "att%d_%d" % (b, d)) for d in range(ATT_WIN)] for b in range(NB2)]
    wt = [P.alloc([512], F32, "w%d" % i) for i in range(2)]
    ot = [P.alloc([512], BF16, "ot%d" % i) for i in range(2)]
    psZ = PsRot(P, [0, 1, 2])
    psL = PsRot(P, [3, 4, 5])
    psO = PsRot(P, [6, 7])
    wi = 0
    for c in range(4):
        P.dma("sp", kTs[:, c, :], kT[c * 128:(c + 1) * 128, :], writes=[(kTs, c)])
        P.dma("sp", qTs[:, c, :], qT[c * 128:(c + 1) * 128, :], writes=[(qTs, c)])
    for par in range(2):
        pb = par * 64
        vsrc = vtm[:, par * 256:(par + 1) * 256].rearrange("(b p) f -> p b f", p=128)
        for b0 in range(0, NBLK, 8):
            b1 = min(NBLK, b0 + 8)
            P.dma("sp", vs[:, b0:b1, :], vsrc[:, b0:b1, :], writes=[(vs, b0 // 8)])
        mixv = mixT[row0:row0 + 512, :].rearrange("(h two d) t -> two d h t", two=2, d=64)[par]
        for i in range(NBLK):
            b = i % NB2
            ndk = min(ATT_WIN, i + 1)
            for dk in range(ndk):
                kb = i - dk
                pz = psZ.next()
                for hh in range(4):
                    P.mm(pz[:, hh * 128:(hh + 1) * 128], kTs[pb:pb + 64, hh, kb * 128:(kb + 1) * 128],
                         qTs[pb:pb + 64, hh, i * 128:(i + 1) * 128], True, True,
                         [(kTs, hh), (qTs, hh)], [pz])
                e_t = ex[b][dk]
                P.act(e_t[:, :], pz[:, 0:512], AF.Exp, [pz], [e_t])
                if dk == 0:
                    P.tt("pool", e_t[:, :], e_t[:, :], C["dmask"][:, :], ALU.mult, [e_t, C["dmask"]], [e_t])
                P.act(spb[b][dk][:, :], e_t[:, :], AF.Ln, [e_t], [spb[b][dk]], bias=1.0)
            for dk in range(ndk):
                pl = psL.next()
                P.mm(pl[:, 0:512], C["trim"][:, :], spb[b][dk][:, :], True, dk == 0,
                     [C["trim"], spb[b][dk]], [pl])
                for d2 in range(dk):
                    P.mm(pl[:, 0:512], C["onesB"][:, :], spb[b][d2][:, :], False, d2 == dk - 1,
                         [C["onesB"], spb[b][d2]], [pl])
                w = wt[wi % 2]
                wi += 1
                P.act(w[:, :], pl[:, 0:512], AF.Exp, [pl], [w], scale=-1.0)
                P.tt("dve", att[b][dk][:, :], ex[b][dk][:, :], w[:, :], ALU.mult, [ex[b][dk], w], [att[b][dk]])
            po = psO.next()
            for hh in range(4):
                for dk in range(ndk):
                    kb = i - dk
                    P.mm(po[0:64, hh * 128:(hh + 1) * 128], vs[:, kb, hh * 64:(hh + 1) * 64],
                         att[b][dk][:, hh * 128:(hh + 1) * 128], dk == 0, dk == ndk - 1,
                         [(vs, kb // 8), att[b][dk]], [po])
            o = ot[i % 2]
            P.copy("dve", o[0:64, :], po[0:64, 0:512], [po], [o])
            P.dma("sp", mixv[:, :, i * 128:(i + 1) * 128],
                  o[0:64, :].rearrange("d (h t) -> d h t", h=4), reads=[o])


LDC = float(np.exp(-0.5))
GN_EPS = 64 * 1e-5
NPRM1 = 42


def phase_odd_mixer(P, C, x_in, nblocks, D, mixT):
    P.barrier()
    P.arena_reset(C["arena_base"])
    A = P.alloc
    Wi = A([8, 2304], BF16, "Wi")
    load_w_bf16(P, Wi, D["w_in"], 8, 2304, split=4)
    wa2 = A([512], BF16, "wa2")
    g2s = A([512], BF16, "g2s")
    wp = A([4, 128], BF16, "wp")
    P.dma("pool", wa2[:, :], D["wa2"][:, :], writes=[wa2])
    P.dma("pool", g2s[:, :], D["g2"][:, :], writes=[g2s])
    P.dma("pool", wp[:, :, :], D["wpool"][:, :, :], writes=[wp])
    prm = A([NPRM1], F32, "prm1")
    P.dma("sp", prm[:, :], D["prm1"][:, :], writes=[prm])
    mGa = A([4, 256], F32, "mGa")
    mN = A([4, 128], F32, "mN")
    triu = A([128], F32, "triu")
    bd = A([128], F32, "bd")
    hsel = A([2], BF16, "hsel")
    icnt = A([4, 128], F32, "icnt")
    P.dma("sp", mGa[:, :, :], D["mGa"].rearrange("p (h t) -> p h t", h=4), writes=[mGa])
    P.dma("sp", mN[:, :, :], D["mN"].rearrange("p (h t) -> p h t", h=4), writes=[mN])
    P.dma("sp", triu[:, :], D["triu"][:, :], writes=[triu])
    P.dma("sp", bd[:, :], D["bd"][:, :], writes=[bd])
    P.dma("pool", hsel[:, :], D["hsel"][:, :], writes=[hsel])
    P.dma("sp", icnt[:, :, :], D["icnt"].rearrange("p (h t) -> p h t", h=4), writes=[icnt])
    cwin = A([4, 128], F32, "cwin")
    for g in range(4):
        P.op("pool", lambda e, g=g: e.memset(cwin[:, g, :], 1.0 / (2 << g)), writes=[cwin])
    w0tm = A([512], F32, "w0tm")
    lnxg = A([512], F32, "lnxg")
    lnxb = A([512], F32, "lnxb")
    load_bcast(P, w0tm, D["w0row"], 512)
    load_bcast(P, lnxg, D["lnxg"], 512)
    load_bcast(P, lnxb, D["lnxb"], 512)
    Ssh = A([14, 128], F32, "Ssh")
    ones3 = Ssh
    P.op("pool", lambda e: e.memset(ones3[:, :, :], 1.0), writes=[ones3])
    mu_bc = A([14, 128], F32, "mu_bc")
    P.tt("pool", mu_bc[:, :, :], ones3[:, :, :], prm[:, 0:14].unsqueeze(2).to_broadcast([128, 14, 128]),
         ALU.mult, [ones3, prm], [mu_bc])

    def bc4(col, name):
        t = A([4, 128], F32, name)
        P.tt("pool", t[:, :, :], ones3[:, 0:4, :],
             prm[:, col:col + 4].unsqueeze(2).to_broadcast([128, 4, 128]), ALU.mult, [ones3, prm], [t])
        return t

    w0f = bc4(14, "w0f")
    a0f = bc4(18, "a0f")
    kkb = bc4(22, "kkb")
    kab = bc4(26, "kab")
    rkb = bc4(30, "rkb")
    oma = A([4, 128], F32, "oma")
    P.ts("pool", oma[:, :, :], kab[:, :, :], -1.0, 1.0, ALU.mult, ALU.add, [kab], [oma])
    pbs = A([4], F32, "pbs")
    P.tt("pool", pbs[:, :], prm[:, 34:38], prm[:, 38:42], ALU.mult, [prm], [pbs])
    identB = C["ident"]
    identB4 = A([4, 128], BF16, "identB4")
    for h in range(4):
        P.copy("pool", identB4[:, h, :], identB[:, :], [identB], [identB4])
    identF = A([128], F32, "identF")
    P.copy("pool", identF[:, :], identB[:, :], [identB], [identF])

    pbuf = A([14, 129], F32, "pbuf")
    P.op("pool", lambda e: e.memset(pbuf[:, :, 0:1], 0.0), writes=[(pbuf, "h")])
    PADU = 16
    ubuf = A([4, PADU + 128], F32, "ubuf")
    P.op("pool", lambda e: e.memset(ubuf[:, :, 0:PADU], 0.0), writes=[(ubuf, "h")])
    Ssf = A([4, 64], F32, "Ssf")
    Sb = A([4, 64], BF16, "Sb")
    P.op("pool", lambda e: e.memset(Ssf[:, :, :], 0.0), writes=[Ssf])
    P.op("pool", lambda e: e.memset(Sb[:, :, :], 0.0), writes=[Sb])

    xin = [A([D_MODEL], F32, "xin%d" % i) for i in range(2)]
    xbf = [A([D_MODEL], BF16, "xbf%d" % i) for i in range(2)]
    xT = [A([8, 128], BF16, "xT%d" % i) for i in range(2)]
    lor = A([128], BF16, "lor")
    sgb = A([128], BF16, "sgb")
    sgf = A([4, 128], F32, "sgf")
    icl = A([4, 128], F32, "icl")
    sgt = A([512], F32, "sgt")
    gate_sb = A([512], F32, "gate_sb")
    clsb = A([4, 128], F32, "clsb")
    cle = A([4, 128], F32, "cle")
    E1 = A([4, 128], F32, "E1")
    E2 = A([4, 128], F32, "E2")
    E3 = A([4, 128], F32, "E3")
    E4 = A([4, 128], F32, "E4")
    nbv = A([4], F32, "nbv")
    gC = A([4], F32, "gC")
    kk = A([4, 128], F32, "kk")
    sq = A([4, 128], F32, "sq")
    rn = A([4, 128], F32, "rn")
    t1 = A([4, 128], F32, "t1")
    kmod = A([4, 128], F32, "kmod")
    bvec = A([4, 128], F32, "bvec")
    ARf = A([4, 256], BF16, "ARf")
    ARz = [A([4, 256], BF16, "ARz%d" % p) for p in range(2)]
    Kt = A([4, 128], BF16, "Kt")
    Bt = A([4, 128], BF16, "Bt")
    Kh = A([4, 128], BF16, "Kh")
    Bh = A([4, 128], BF16, "Bh")
    rkr = A([4, 128], BF16, "rkr")
    Vtf = A([512], F32, "Vtf")
    Vtb = A([512], BF16, "Vtb")
    Kht = A([512], BF16, "Kht")
    Bht = A([512], BF16, "Bht")
    bon = A([8], F32, "bon")
    GkM = [A([4, 256], BF16, "GkM%d" % p) for p in range(2)]
    GbM = [A([4, 256], BF16, "GbM%d" % p) for p in range(2)]
    Xk = [[A([4, 128], BF16, "X%d_%d" % (p, i)) for i in range(2)] for p in range(2)]
    Nk = [[A([4, 128], BF16, "N%d_%d" % (p, i)) for i in range(2)] for p in range(2)]
    Pk = [[A([4, 128], BF16, "P%d_%d" % (p, i)) for i in range(2)] for p in range(2)]
    Qk = [[A([4, 128], BF16, "Q%d_%d" % (p, i)) for i in range(2)] for p in range(2)]
    RHSb = A([512], BF16, "RHSb")
    Ub = A([512], BF16, "Ub")
    ysb = A([512], F32, "ysb")
    ysq = A([512], F32, "ysq")
    st = A([4, 8], F32, "st")
    yob = A([512], BF16, "yob")
    yT = A([4, 128], BF16, "yT")
    pooled = A([4, 128], BF16, "pooled")
    ptmp = [A([4, PADU + 128], F32, "ptmp%d" % i) for i in range(2)]
    pout = A([4, 128], BF16, "pout")
    pm = A([2], F32, "pm")
    P.op("pool", lambda e: e.memset(pm[:, :], 0.0), writes=[pm])
    P.op("pool", lambda e: e.memset(pm[0:64, 0:1], 1.0), writes=[pm])
    P.op("pool", lambda e: e.memset(pm[64:128, 1:2], 1.0), writes=[pm])

    psr = PsRot(P, [0, 1, 2, 3, 4, 5])
    psT = P.ps_tiles[6]
    psS = P.ps_tiles[7]
    ctr = [0]

    def v3(ps, n=4, w=128):
        return ps[:, 0:n * w].rearrange("p (c t) -> p c t", c=n)

    for blk in range(nblocks):
        t0 = blk * 128
        xt = xT[blk % 2]
        if C.get('odd_stop', 99) <= -1:
            continue
        load_xT_tile(P, C, x_in, blk, 1, xin, xbf, xt, psT, ctr)
        if C.get('odd_stop', 99) <= 0:
            continue
        for grp in range(5):
            c0 = grp * 4
            ncg = min(4, 18 - c0)
            pp = psr.next()
            for ci in range(ncg):
                c = c0 + ci
                for kc in range(8):
                    P.mm(pp[:, ci * 128:(ci + 1) * 128], Wi[:, kc, c * 128:(c + 1) * 128], xt[:, kc, :],
                         kc == 0, kc == 7, [(Wi, kc), xt], [pp])
            if grp < 3:
                P.copy("dve", pbuf[:, c0:c0 + 4, 1:129], v3(pp), [pp], [pbuf])
            elif grp == 3:
                P.copy("dve", pbuf[:, 12:14, 1:129], v3(pp, 2), [pp], [pbuf])
                P.copy("dve", ubuf[:, 0:2, PADU:PADU + 128], pp[:, 256:512].rearrange("p (c t) -> p c t", c=2),
                       [pp], [ubuf])
            else:
                P.copy("dve", ubuf[:, 2:4, PADU:PADU + 128], v3(pp, 2), [pp], [ubuf])
        if C.get('odd_stop', 99) <= 0.5:
            continue
        P.tt("dve", Ssh[:, :, :], pbuf[:, :, 0:128], pbuf[:, :, 1:129], ALU.subtract,
             [pbuf, (pbuf, "h")], [Ssh])
        P.tt("pool", Ssh[:, :, :], Ssh[:, :, :], mu_bc[:, :, :], ALU.mult, [Ssh, mu_bc], [Ssh])
        P.tt("dve", Ssh[:, :, :], Ssh[:, :, :], pbuf[:, :, 1:129], ALU.add, [Ssh, pbuf], [Ssh])
        P.copy("pool", pbuf[:, :, 0:1], pbuf[:, :, 128:129], [pbuf], [(pbuf, "h")])
        if C.get('odd_stop', 99) <= 1:
            continue
        r_ = Ssh[:, 0:4, :]
        k_ = Ssh[:, 4:8, :]
        v_ = Ssh[:, 8:12, :]
        P.act(lor[0:64, :], Ssh[0:64, 12, :], AF.Tanh, [Ssh], [(lor, 0)])
        P.copy("dve", lor[64:128, :], Ssh[64:128, 12, :], [Ssh], [(lor, 1)])
        P.act(sgb[:, :], Ssh[:, 13, :], AF.Sigmoid, [Ssh], [sgb])
        pdw = psr.next()
        for c in range(4):
            P.mm(pdw[:, c * 128:(c + 1) * 128], wa2[0:64, c * 128:(c + 1) * 128], lor[0:64, :], True, True,
                 [wa2, (lor, 0)], [pdw])
        pda = psr.next()
        for c in range(4):
            P.mm(pda[:, c * 128:(c + 1) * 128], wa2[64:128, c * 128:(c + 1) * 128], lor[64:128, :], True, True,
                 [wa2, (lor, 1)], [pda])
        pdt = psr.next()
        P.mm(pdt[:, 0:512], lor[0:64, :], wa2[0:64, :], True, True, [wa2, (lor, 0)], [pdt])
        pgt = psr.next()
        P.mm(pgt[:, 0:512], sgb[:, :], g2s[:, :], True, True, [sgb, g2s], [pgt])
        P.tt("dve", sgf[:, :, :], v3(pdw), w0f[:, :, :], ALU.add, [pdw, w0f], [sgf])
        P.act(sgf[:, :, :], sgf[:, :, :], AF.Sigmoid, [sgf], [sgf])
        P.tt("dve", icl[:, :, :], v3(pda), a0f[:, :, :], ALU.add, [pda, a0f], [icl])
        P.act(icl[:, :, :], icl[:, :, :], AF.Sigmoid, [icl], [icl])
        P.tt("dve", sgt[:, :], pdt[:, 0:512], w0tm[:, :], ALU.add, [pdt, w0tm], [sgt])
        P.act(sgt[:, :], sgt[:, :], AF.Sigmoid, [sgt], [sgt])
        P.copy("act", gate_sb[:, :], pgt[:, 0:512], [pgt], [gate_sb])
        if C.get('odd_stop', 99) <= 2:
            continue
        pcl = psr.next()
        for c in range(4):
            P.mm(pcl[:, c * 128:(c + 1) * 128], sgt[:, c * 128:(c + 1) * 128], triu[:, :], True, True,
                 [sgt, triu], [pcl])
        P.copy("dve", clsb[:, :, :], v3(pcl), [pcl], [clsb])
        P.tt("pool", cle[:, :, :], clsb[:, :, :], sgf[:, :, :], ALU.subtract, [clsb, sgf], [cle])
        P.ts("dve", nbv[:, :], clsb[:, :, 127], -LDC, None, ALU.mult, None, [clsb], [nbv])
        P.tt("pool", kk[:, :, :], k_, kkb[:, :, :], ALU.mult, [Ssh, kkb], [kk])
        P.tt("pool", sq[:, :, :], kk[:, :, :], kk[:, :, :], ALU.mult, [kk], [sq])
        pss = psr.next()
        P.mm(pss[:, 0:512], bd[:, :], sq[:, :, :].rearrange("p c t -> p (c t)"), True, True, [bd, sq], [pss])
        P.ts("dve", rn[:, :, :], v3(pss), 1e-24, None, ALU.max, None, [pss], [rn])
        P.act(rn[:, :, :], rn[:, :, :], AF.Sqrt, [rn], [rn])
        P.op("dve", lambda e: e.reciprocal(rn[:, :, :], rn[:, :, :]), reads=[rn], writes=[rn])
        P.tt("dve", kk[:, :, :], kk[:, :, :], rn[:, :, :], ALU.mult, [kk, rn], [kk])
        P.tt("pool", t1[:, :, :], icl[:, :, :], kab[:, :, :], ALU.mult, [icl, kab], [t1])
        P.tt("pool", t1[:, :, :], t1[:, :, :], oma[:, :, :], ALU.add, [t1, oma], [t1])
        P.tt("dve", kmod[:, :, :], k_, t1[:, :, :], ALU.mult, [Ssh, t1], [kmod])
        P.tt("pool", bvec[:, :, :], kk[:, :, :], icl[:, :, :], ALU.mult, [kk, icl], [bvec])
        P.act(E1[:, :, :], clsb[:, :, :], AF.Exp, [clsb], [E1], scale=-LDC)
        P.act(E2[:, :, :], clsb[:, :, :], AF.Exp, [clsb], [E2], scale=LDC)
        P.act(E3[:, :, :], cle[:, :, :], AF.Exp, [cle], [E3], scale=-LDC)
        for c in range(4):
            P.act(E4[:, c, :], clsb[:, c, :], AF.Exp, [clsb, nbv], [E4], scale=LDC, bias=nbv[:, c:c + 1])
        P.act(gC[:, :], nbv[:, :], AF.Exp, [nbv], [gC])
        P.stt("dve", ARf[:, :, 0:128], kk[:, :, :], -1.0, E3[:, :, :], ALU.mult, ALU.mult, [kk, E3], [(ARf, 0)])
        P.tt("pool", ARf[:, :, 128:256], r_, E1[:, :, :], ALU.mult, [Ssh, E1], [(ARf, 1)])
        for p in range(2):
            P.ts("dve" if p == 0 else "pool", ARz[p][:, :, :], ARf[:, :, :], pm[:, p:p + 1], None, ALU.mult, None,
                 [(ARf, 0), (ARf, 1), pm], [ARz[p]])
        P.tt("dve", Kt[:, :, :], kmod[:, :, :], E2[:, :, :], ALU.mult, [kmod, E2], [Kt])
        P.tt("pool", Bt[:, :, :], bvec[:, :, :], E2[:, :, :], ALU.mult, [bvec, E2], [Bt])
        P.tt("dve", Kh[:, :, :], kmod[:, :, :], E4[:, :, :], ALU.mult, [kmod, E4], [Kh])
        P.tt("pool", Bh[:, :, :], bvec[:, :, :], E4[:, :, :], ALU.mult, [bvec, E4], [Bh])
        P.tt("pool", t1[:, :, :], r_, rkb[:, :, :], ALU.mult, [Ssh, rkb], [t1])
        P.tt("dve", rkr[:, :, :], t1[:, :, :], kmod[:, :, :], ALU.mult, [t1, kmod], [rkr])
        if C.get('odd_stop', 99) <= 3:
            continue
        pvt = psr.next()
        for c in range(4):
            P.tr(pvt[:, c * 128:(c + 1) * 128], Ssh[:, 8 + c, :], identF[:, :], [Ssh, identF], [pvt])
        P.copy("act", Vtf[:, :], pvt[:, 0:512], [pvt], [Vtf])
        P.copy("dve", Vtb[:, :], pvt[:, 0:512], [pvt], [Vtb])
        psv = psT.ap.bitcast(BF16)
        for c in range(4):
            P.tr(psv[:, c * 128:(c + 1) * 128], Kh[:, c, :], identB[:, :], [Kh, identB], [psT])
        for c in range(4):
            P.tr(psv[:, 512 + c * 128:512 + (c + 1) * 128], Bh[:, c, :], identB[:, :], [Bh, identB], [psT])
        P.copy("act", Kht[:, :], psv[:, 0:512], [psT], [Kht])
        P.copy("dve", Bht[:, :], psv[:, 512:1024], [psT], [Bht])
        pbn = psr.next()
        for c in range(4):
            P.mm(pbn[:, c * 2:(c + 1) * 2], rkr[:, c, :], hsel[:, :], True, True, [rkr, hsel], [pbn])
        P.copy("act", bon[:, :], pbn[:, 0:8], [pbn], [bon])
        if C.get('odd_stop', 99) <= 4:
            continue
        TT = [None, None]
        for par in range(2):
            az = ARz[par]
            pgk = [psr.next(), psr.next()]
            for hh in range(4):
                P.mm(pgk[hh // 2][:, (hh % 2) * 256:(hh % 2 + 1) * 256], Kt[:, hh, :], az[:, hh, :], True, True,
                     [Kt, az], [pgk[hh // 2]])
            for i2 in range(2):
                P.tt("dve", GkM[par][:, 2 * i2:2 * i2 + 2, :], v3(pgk[i2], 2, 256), mGa[:, 2 * i2:2 * i2 + 2, :],
                     ALU.mult, [pgk[i2], mGa], [GkM[par]])
            pgb = [psr.next(), psr.next()]
            for hh in range(4):
                P.mm(pgb[hh // 2][:, (hh % 2) * 256:(hh % 2 + 1) * 256], Bt[:, hh, :], az[:, hh, :], True, True,
                     [Bt, az], [pgb[hh // 2]])
            for i2 in range(2):
                P.tt("dve", GbM[par][:, 2 * i2:2 * i2 + 2, :], v3(pgb[i2], 2, 256), mGa[:, 2 * i2:2 * i2 + 2, :],
                     ALU.mult, [pgb[i2], mGa], [GbM[par]])
            pn0 = psr.next()
            for hh in range(4):
                P.mm(pn0[:, hh * 128:(hh + 1) * 128], az[:, hh, 0:128], Bt[:, hh, :], True, True, [az, Bt], [pn0])
            X, N_, Pm, Q = Xk[par], Nk[par], Pk[par], Qk[par]
            P.tt("dve", N_[0][:, :, :], v3(pn0), mN[:, :, :], ALU.mult, [pn0, mN], [N_[0]])
            P.copy("pool", X[0][:, :, :], GbM[par][:, :, 0:128], [GbM[par]], [X[0]])
            P.tt("pool", Pm[0][:, :, :], X[0][:, :, :], identB4[:, :, :], ALU.add, [X[0], identB4], [Pm[0]])
            P.tt("pool", Q[0][:, :, :], N_[0][:, :, :], identB4[:, :, :], ALU.add, [N_[0], identB4], [Q[0]])
            NLEV = 6
            for lv in range(NLEV):
                a, b = lv % 2, (lv + 1) % 2
                last = lv == NLEV - 1
                px = psr.next()
                for hh in range(4):
                    P.mm(px[:, hh * 128:(hh + 1) * 128], N_[a][:, hh, :], X[a][:, hh, :], True, True,
                         [N_[a], X[a]], [px])
                if not last:
                    pn = psr.next()
                    for hh in range(4):
                        P.mm(pn[:, hh * 128:(hh + 1) * 128], X[a][:, hh, :], N_[a][:, hh, :], True, True,
                             [N_[a], X[a]], [pn])
                P.copy("act", X[b][:, :, :], v3(px), [px], [X[b]])
                if not last:
                    P.copy("dve", N_[b][:, :, :], v3(pn), [pn], [N_[b]])
                pp_ = psr.next()
                for hh in range(4):
                    P.mm(pp_[:, hh * 128:(hh + 1) * 128], Q[a][:, hh, :], X[b][:, hh, :], True, True,
                         [Q[a], X[b]], [pp_])
                if not last:
                    pq_ = psr.next()
                    for hh in range(4):
                        P.mm(pq_[:, hh * 128:(hh + 1) * 128], Pm[a][:, hh, :], N_[b][:, hh, :], True, True,
                             [Pm[a], N_[b]], [pq_])
                P.tt("dve", Pm[b][:, :, :], v3(pp_), Pm[a][:, :, :], ALU.add, [pp_, Pm[a]], [Pm[b]])
                if not last:
                    P.tt("dve", Q[b][:, :, :], v3(pq_), Q[a][:, :, :], ALU.add, [pq_, Q[a]], [Q[b]])
            TT[par] = Pm[NLEV % 2]
        if C.get('odd_stop', 99) <= 5:
            continue
        prh = psr.next()
        for hh in range(4):
            for par in range(2):
                col = hh * 128 + par * 64
                P.mm(prh[:, col:col + 64], ARz[par][:, hh, 0:128], Sb[:, hh, :], True, False,
                     [ARz[par], Sb], [prh])
                P.mm(prh[:, col:col + 64], GkM[par][:, hh, 0:128], Vtb[:, col:col + 64], False, True,
                     [GkM[par], Vtb], [prh])
        P.copy("act", RHSb[:, :], prh[:, 0:512], [prh], [RHSb])
        pu = psr.next()
        for hh in range(4):
            for par in range(2):
                col = hh * 128 + par * 64
                P.mm(pu[:, col:col + 64], TT[par][:, hh, :], RHSb[:, col:col + 64], True, True,
                     [TT[par], RHSb], [pu])
        P.copy("dve", Ub[:, :], pu[:, 0:512], [pu], [Ub])
        py = psr.next()
        for hh in range(4):
            for par in range(2):
                col = hh * 128 + par * 64
                P.mm(py[:, col:col + 64], ARz[par][:, hh, 128:256], Sb[:, hh, :], True, False,
                     [ARz[par], Sb], [py])
                P.mm(py[:, col:col + 64], GkM[par][:, hh, 128:256], Vtb[:, col:col + 64], False, False,
                     [GkM[par], Vtb], [py])
                P.mm(py[:, col:col + 64], GbM[par][:, hh, 128:256], Ub[:, col:col + 64], False, True,
                     [GbM[par], Ub], [py])
        for hh in range(4):
            P.mm(psS[:, hh * 128:(hh + 1) * 128], Kht[:, hh * 128:(hh + 1) * 128], Vtb[:, hh * 128:(hh + 1) * 128],
                 True, False, [Kht, Vtb], [psS])
            P.mm(psS[:, hh * 128:(hh + 1) * 128], Bht[:, hh * 128:(hh + 1) * 128], Ub[:, hh * 128:(hh + 1) * 128],
                 False, True, [Bht, Ub], [psS])
        P.tt("pool", Ssf[:, :, :], Ssf[:, :, :], gC[:, :].unsqueeze(2).to_broadcast([128, 4, 64]), ALU.mult,
             [Ssf, gC], [Ssf])
        P.tt("dve", Ssf[0:64, :, :], Ssf[0:64, :, :], v3(psS)[0:64, :, 0:64], ALU.add, [Ssf, psS], [Ssf])
        P.tt("dve", Ssf[64:128, :, :], Ssf[64:128, :, :], v3(psS)[64:128, :, 64:128], ALU.add, [Ssf, psS], [Ssf])
        P.copy("pool", Sb[:, :, :], Ssf[:, :, :], [Ssf], [Sb])
        if C.get('odd_stop', 99) <= 6:
            continue
        P.copy("act", ysb[:, :], py[:, 0:512], [py], [ysb])
        P.act(ysq[:, :], py[:, 0:512], AF.Square, [py], [ysq])
        y3 = ysb[:, :].rearrange("p (h i) -> p h i", h=8)
        P.op("dve", lambda e, y3=y3: e.tensor_reduce(st[:, 0, :], y3, AX.X, ALU.add), reads=[ysb], writes=[(st, 0)])
        P.op("dve", lambda e: e.tensor_reduce(st[:, 1, :], ysq[:, :].rearrange("p (h i) -> p h i", h=8), AX.X, ALU.add),
             reads=[ysq], writes=[(st, 1)])
        P.ts("dve", st[:, 0, :], st[:, 0, :], 1.0 / 64, None, ALU.mult, None, [(st, 0)], [(st, 0)])
        P.tt("dve", st[:, 3, :], st[:, 0, :], st[:, 0, :], ALU.mult, [(st, 0)], [(st, 3)])
        P.stt("dve", st[:, 2, :], st[:, 1, :], 1.0 / 64, st[:, 3, :], ALU.mult, ALU.subtract,
              [(st, 1), (st, 3)], [(st, 2)])
        P.ts("dve", st[:, 2, :], st[:, 2, :], GN_EPS, None, ALU.add, None, [(st, 2)], [(st, 2)])
        P.act(st[:, 2, :], st[:, 2, :], AF.Sqrt, [(st, 2)], [(st, 2)])
        P.op("dve", lambda e: e.reciprocal(st[:, 2, :], st[:, 2, :]), reads=[(st, 2)], writes=[(st, 2)])
        P.tt("dve", y3, y3, st[:, 0, :].unsqueeze(2).to_broadcast([128, 8, 64]), ALU.subtract, [ysb, (st, 0)], [ysb])
        P.tt("dve", y3, y3, st[:, 2, :].unsqueeze(2).to_broadcast([128, 8, 64]), ALU.mult, [ysb, (st, 2)], [ysb])
        P.tt("pool", ysb[:, :], ysb[:, :], lnxg[:, :], ALU.mult, [ysb, lnxg], [ysb])
        P.tt("pool", ysb[:, :], ysb[:, :], lnxb[:, :], ALU.add, [ysb, lnxb], [ysb])
        P.tt("pool", ysq[:, :].rearrange("p (h i) -> p h i", h=8), Vtf[:, :].rearrange("p (h i) -> p h i", h=8),
             bon[:, :].unsqueeze(2).to_broadcast([128, 8, 64]), ALU.mult, [Vtf, bon, ysq], [ysq])
        P.tt("pool", ysb[:, :], ysb[:, :], ysq[:, :], ALU.add, [ysb, ysq], [ysb])
        P.tt("dve", yob[:, :], ysb[:, :], gate_sb[:, :], ALU.mult, [ysb, gate_sb], [yob])
        for c in range(4):
            P.tr(psv[:, c * 128:(c + 1) * 128], yob[:, c * 128:(c + 1) * 128], identB[:, :], [yob, identB], [psT])
        P.copy("act", yT[:, :, :], psv[:, 0:512].rearrange("p (c t) -> p c t", c=4), [psT], [yT])
        P.dma("sp", mixT[0:512, t0:t0 + 128].rearrange("(c p) t -> p c t", p=128), yT[:, :, :], reads=[yT])
        if C.get('odd_stop', 99) <= 7:
            continue
        W = PADU + 128
        a_, b_ = ptmp
        P.tt("pool", a_[:, :, 1:W], ubuf[:, :, 1:W], ubuf[:, :, 0:W - 1], ALU.add, [ubuf, (ubuf, "h")], [a_])
        P.tt("pool", b_[:, 1:4, 3:W], a_[:, 1:4, 3:W], a_[:, 1:4, 1:W - 2], ALU.add, [a_], [b_])
        P.tt("pool", a_[:, 2:4, 7:W], b_[:, 2:4, 7:W], b_[:, 2:4, 3:W - 4], ALU.add, [b_, a_], [(a_, 1)])
        P.tt("pool", b_[:, 3:4, 15:W], a_[:, 3:4, 15:W], a_[:, 3:4, 7:W - 8], ALU.add, [a_, (a_, 1), b_], [(b_, 1)])
        srcs = [a_, b_, a_, b_]
        for g in range(4):
            win = 2 << g
            src = srcs[g]
            deps = [a_, b_, (a_, 1), (b_, 1), ubuf]
            ic = icnt if blk == 0 else cwin
            P.tt("dve", src[:, g, PADU:W], src[:, g, PADU:W], ic[:, g, :], ALU.mult, deps + [ic], [(src, "f%d" % g)])
            P.tt("dve", pooled[:, g, :], src[:, g, PADU:W], ubuf[:, g, PADU:W], ALU.subtract,
                 deps + [(src, "f%d" % g)], [(pooled, g)])
        P.copy("pool", ubuf[:, :, 0:PADU], ubuf[:, :, 128:128 + PADU], [ubuf, (pooled, 0), (pooled, 1), (pooled, 2), (pooled, 3)],
               [(ubuf, "h")])
        ppl = psr.next()
        for g in range(4):
            P.mm(ppl[:, g * 128:(g + 1) * 128], wp[:, g, :], pooled[:, g, :], True, True, [wp, (pooled, g)], [ppl])
        for g in range(4):
            P.act(pout[:, g, :], ppl[:, g * 128:(g + 1) * 128], AF.Identity, [ppl, prm, pbs], [pout],
                  scale=prm[:, 38 + g:39 + g], bias=pbs[:, g:g + 1])
        P.dma("sp", mixT[512:1024, t0:t0 + 128].rearrange("(c p) t -> p c t", p=128), pout[:, :, :], reads=[pout])


def host_consts_odd():
    k = np.arange(128)
    strict = (k[:, None] < k[None, :]).astype(np.float32)
    incl = (k[:, None] <= k[None, :]).astype(np.float32)
    mGa = np.tile(np.concatenate([strict, incl], axis=1), (1, 4))
    mN = np.tile((k[None, :] < k[:, None]).astype(np.float32), (1, 4))
    bd = (k[:, None] // 64 == k[None, :] // 64).astype(np.float32)
    hsel = np.stack([(k < 64), (k >= 64)], axis=1).astype(np.float32)
    t = np.arange(128)
    icnt = np.concatenate([np.tile(1.0 / np.minimum(t + 1, 2 << g)[None, :], (128, 1)) for g in range(4)],
                          axis=1).astype(np.float32)
    return {"mGa": np.ascontiguousarray(mGa), "mN": np.ascontiguousarray(mN), "triu": incl, "bd": bd,
            "hsel": hsel, "icnt": np.ascontiguousarray(icnt)}


def host_odd_params(inp, i):
    def fm(v):
        return np.asarray(v, np.float32).reshape(4, 128).T
    prm1 = np.concatenate([
        np.asarray(inp["o_mu"][i], np.float32).reshape(14, 128).T,
        fm(inp["o_w0"][i]), fm(inp["o_a0"][i]), fm(inp["o_k_k"][i].reshape(512)),
        fm(inp["o_k_a"][i].reshape(512)), fm(inp["o_r_k"][i].reshape(512)),
        fm(inp["o_b_pool"][i].reshape(512)), fm(inp["o_pool_scale"][i])], axis=1)
    return {
        "w_in": np.ascontiguousarray(inp["o_w_in"][i], np.float32),
        "wa2": np.ascontiguousarray(np.concatenate([inp["o_w2"][i], inp["o_a2"][i]], axis=0), np.float32),
        "g2": np.ascontiguousarray(inp["o_g2"][i], np.float32),
        "wpool": np.ascontiguousarray(np.transpose(inp["o_w_pool"][i], (1, 0, 2)), np.float32),
        "prm1": np.ascontiguousarray(prm1, np.float32),
        "w0row": np.ascontiguousarray(inp["o_w0"][i], np.float32),
        "lnxg": np.ascontiguousarray(inp["o_lnx_g"][i].reshape(512), np.float32),
        "lnxb": np.ascontiguousarray(inp["o_lnx_b"][i].reshape(512), np.float32),
    }


def host_even_params(inp, i):
    return {
        "e_w_in": np.ascontiguousarray(inp["e_w_in"][i], np.float32),
        "e_w_out": np.ascontiguousarray(inp["e_w_out"][i], np.float32),
        "e_dwT": np.ascontiguousarray(np.asarray(inp["e_w_dw"][i], np.float32).T.reshape(4, 128, 31).transpose(1, 0, 2)),
        "e_prm": np.ascontiguousarray(np.stack([inp["e_b_dw"][i], inp["e_conv_g"][i], inp["e_conv_b"][i]], -1)
                                      .astype(np.float32).reshape(4, 128, 3).transpose(1, 0, 2)),
    }


def build_program(S, shapes):
    NBLK = S // 128
    nc = bass.Bass("TRN2", target_bir_lowering=False)
    I = {k: nc.dram_tensor(k, list(shp), F32, kind="ExternalInput").ap() for k, shp in shapes.items()}
    out = nc.dram_tensor("out", [S, D_MODEL], F32, kind="ExternalOutput").ap()

    def scratch(name, shape, dt):
        return nc.dram_tensor(name, shape, dt, kind="Internal").ap()

    xa = scratch("s_xa", [S, D_MODEL], F32)
    xb = scratch("s_xb", [S, D_MODEL], F32)
    hcT = scratch("s_hcT", [512, S], BF16)
    qT = scratch("s_qT", [512, S], BF16)
    kT = scratch("s_kT", [512, S], BF16)
    vtm = scratch("s_vtm", [S, 512], BF16)
    mixT = scratch("s_mixT", [1024, S], BF16)
    es = ExitStack()
    P = Prog(nc, es)
    C = setup_consts(P, {k[2:]: v for k, v in I.items() if k.startswith("c_")})
    g, b = I["ln_g"], I["ln_b"]
    fi, fo = I["ffn_in"], I["ffn_out"]
    phase_ffn(P, C, I["x"], 0, xa, 0, NBLK, fi[0, 0], fo[0, 0], g[0, 0], b[0, 0])
    phase_inproj_even(P, C, xa, NBLK, I["e_w_in"], hcT, qT, kT, vtm)
    phase_conv(P, C, hcT, S, I["e_dwT"], I["e_prm"], mixT)
    phase_attn(P, C, qT, kT, vtm, S, mixT, 512)
    phase_outproj(P, C, mixT, xa, xb, NBLK, I["e_w_out"], g[0, 1], b[0, 1])
    phase_ffn(P, C, xb, 0, xa, 0, NBLK, fi[0, 1], fo[0, 1], g[0, 2], b[0, 2])
    phase_ffn(P, C, xa, 0, xb, 0, NBLK, fi[1, 0], fo[1, 0], g[1, 0], b[1, 0])
    D = {k[2:]: v for k, v in I.items() if k.startswith("o_") or k.startswith("k_")}
    phase_odd_mixer(P, C, xb, NBLK, D, mixT)
    phase_outproj(P, C, mixT, xb, xa, NBLK, I["o_w_out"], g[1, 1], b[1, 1])
    phase_ffn(P, C, xa, 0, out, 0, NBLK, fi[1, 1], fo[1, 1], g[1, 2], b[1, 2])
    P.finish()
    es.close()
    return nc


ACTIVE_CORES = (0, 1, 4, 5)


def kernel(**inputs):
    inp = {k: np.asarray(v) for k, v in inputs.items()}
    x = np.asarray(inp["x"], np.float32)
    B, S, _ = x.shape
    shared = {
        "ffn_in": np.ascontiguousarray(inp["ffn_in"], np.float32),
        "ffn_out": np.ascontiguousarray(inp["ffn_out"], np.float32),
        "ln_g": np.ascontiguousarray(inp["ln_g"], np.float32),
        "ln_b": np.ascontiguousarray(inp["ln_b"], np.float32),
        "o_w_out": np.ascontiguousarray(inp["o_w_out"][0], np.float32),
    }
    shared.update(host_even_params(inp, 0))
    for k, v in host_odd_params(inp, 0).items():
        shared["o_" + k] = v
    for k, v in host_consts_odd().items():
        shared["k_" + k] = v
    for k, v in host_consts().items():
        shared["c_" + k] = v
    n_cores = 8
    assert B <= len(ACTIVE_CORES)
    in_maps = []
    zero_x = np.zeros((S, D_MODEL), np.float32)
    for c in range(n_cores):
        m = dict(shared)
        if c in ACTIVE_CORES and ACTIVE_CORES.index(c) < B:
            m["x"] = np.ascontiguousarray(x[ACTIVE_CORES.index(c)])
        else:
            m["x"] = zero_x
        in_maps.append(m)
    shapes = {k: v.shape for k, v in in_maps[0].items()}
    nc = build_program(S, shapes)
    res = run_bass_kernel_spmd(nc, in_maps, core_ids=list(range(n_cores)))
    out = np.stack([np.asarray(res.results[ACTIVE_CORES[bi]]["out"], np.float32) for bi in range(B)], axis=0)
    return out
```

```python
import numpy as np
from contextlib import ExitStack
import concourse.bass as bass
import concourse.mybir as mybir
from concourse.bass_utils import run_bass_kernel_spmd

F32 = mybir.dt.float32
BF16 = mybir.dt.bfloat16
AF = mybir.ActivationFunctionType
ALU = mybir.AluOpType
AX = mybir.AxisListType

D_MODEL = 1024
D_FF = 2816
DEPTH = 2
DN_ALPHA = (2.0 * DEPTH) ** 0.25
LN_EPS = 1e-5
ARENA_BYTES = 207 * 1024
SQ_DVE = "pool"


class Op:
    __slots__ = ("eng", "fn", "idx", "deps", "dma", "dslot", "dval", "signal",
                 "sig", "waits")

    def __init__(self, eng, fn, idx, dma):
        self.eng = eng
        self.fn = fn
        self.idx = idx
        self.dma = dma
        self.deps = set()
        self.dslot = -1
        self.dval = 0
        self.signal = False
        self.sig = 0
        self.waits = []


class Tile:
    def __init__(self, ap, name=""):
        self.ap = ap
        self.name = name

    def __getitem__(self, k):
        return self.ap[k]


class Prog:
    CENG = ("pe", "act", "dve", "pool")

    def __init__(self, nc, es, n_dma_sems=48):
        self.nc = nc
        self.eng = {"pe": nc.tensor, "act": nc.scalar, "dve": nc.vector,
                    "pool": nc.gpsimd, "sp": nc.sync}
        self.sem = {e: es.enter_context(nc.semaphore("sem_" + e)) for e in self.CENG}
        self.dsem = [es.enter_context(nc.semaphore("dsem%d" % i)) for i in range(n_dma_sems)]
        self.dsem_val = [0] * n_dma_sems
        self.dsem_last = [None] * n_dma_sems
        self.dnext = 0
        self.ops = []
        self.last_w = {}
        self.readers = {}
        self.count = {e: 0 for e in self.eng}
        self.last_op = {e: None for e in self.eng}
        self.arena = es.enter_context(nc.sbuf_tensor("arena", [128, ARENA_BYTES // 4], F32))
        self.cur = 0
        self.psum = [es.enter_context(nc.psum_tensor("ps%d" % i, [128, 512], F32)) for i in range(8)]
        self.ps_tiles = [Tile(p, "ps%d" % i) for i, p in enumerate(self.psum)]

    def alloc(self, free_shape, dtype, name=""):
        n = 1
        for s in free_shape:
            n *= s
        esz = 4 if dtype == F32 else 2
        nbytes = (n * esz + 63) // 64 * 64
        off = self.cur
        self.cur += nbytes
        assert self.cur <= ARENA_BYTES, "arena overflow %s: %d" % (name, self.cur)
        ap = self.arena[:, off // 4:(off + nbytes) // 4]
        if dtype != F32:
            ap = ap.bitcast(dtype)
        ap = ap[:, 0:n]
        if len(free_shape) == 2:
            ap = ap.rearrange("p (a b) -> p a b", a=free_shape[0])
        elif len(free_shape) == 3:
            ap = ap.rearrange("p (a b c) -> p a b c", a=free_shape[0], b=free_shape[1])
        return Tile(ap, name)

    def arena_reset(self, to=0):
        self.cur = to

    def op(self, eng, fn, reads=(), writes=(), dma=False):
        o = Op(eng, fn, self.count[eng], dma)
        self.count[eng] += 1
        deps = o.deps
        for t in reads:
            w = self.last_w.get(t)
            if w is not None:
                deps.add(w)
        for t in writes:
            w = self.last_w.get(t)
            if w is not None:
                deps.add(w)
            rd = self.readers.get(t)
            if rd:
                deps.update(rd.values())
        if dma:
            slot = self.dnext
            self.dnext = (self.dnext + 1) % len(self.dsem)
            prev = self.dsem_last[slot]
            if prev is not None:
                deps.add(prev)
            self.dsem_val[slot] += 16
            o.dslot = slot
            o.dval = self.dsem_val[slot]
            self.dsem_last[slot] = o
        for t in reads:
            key = ("d", id(o)) if dma else eng
            self.readers.setdefault(t, {})[key] = o
        for t in writes:
            self.last_w[t] = o
            self.readers[t] = {}
        self.ops.append(o)
        self.last_op[eng] = o
        return o

    def dma(self, eng, out, in_, reads=(), writes=()):
        return self.op(eng, lambda e: e.dma_start(out=out, in_=in_), reads, writes, dma=True)

    def mm(self, out, lhsT, rhs, start, stop, reads, writes):
        return self.op("pe", lambda e: e.matmul(out, lhsT, rhs, start=start, stop=stop), reads, writes)

    def tr(self, out, in_, ident, reads, writes):
        return self.op("pe", lambda e: e.transpose(out, in_, ident), reads, writes)

    def act(self, out, in_, func, reads, writes, bias=None, scale=None, eng="act", accum_out=None):
        kw = {}
        if bias is not None:
            kw["bias"] = bias
        if scale is not None:
            kw["scale"] = scale
        if accum_out is not None:
            kw["accum_out"] = accum_out
        return self.op(eng, lambda e: e.activation(out=out, in_=in_, func=func, **kw), reads, writes)

    def tt(self, eng, out, in0, in1, op, reads, writes):
        return self.op(eng, lambda e: e.tensor_tensor(out, in0, in1, op), reads, writes)

    def stt(self, eng, out, in0, scalar, in1, op0, op1, reads, writes):
        return self.op(eng, lambda e: e.scalar_tensor_tensor(out, in0, scalar, in1, op0, op1), reads, writes)

    def ts(self, eng, out, in0, s1, s2, op0, op1, reads, writes):
        if s2 is None:
            return self.op(eng, lambda e: e.tensor_scalar(out, in0, s1, None, op0), reads, writes)
        return self.op(eng, lambda e: e.tensor_scalar(out, in0, s1, s2, op0, op1), reads, writes)

    def copy(self, eng, out, in_, reads, writes):
        if eng == "act":
            return self.op(eng, lambda e: e.activation(out=out, in_=in_, func=AF.Copy), reads, writes)
        return self.op(eng, lambda e: e.tensor_copy(out, in_), reads, writes)

    def barrier(self):
        deps = set(o for o in self.last_op.values() if o is not None)
        deps.update(o for o in self.dsem_last if o is not None)
        for e in self.eng:
            o = Op(e, None, self.count[e], False)
            self.count[e] += 1
            o.deps = set(deps)
            self.ops.append(o)

    def finish(self):
        deps = set(o for o in self.last_op.values() if o is not None)
        deps.update(o for o in self.dsem_last if o is not None)
        o = Op("sp", None, self.count["sp"], False)
        o.deps = deps
        self.ops.append(o)
        self.emit()

    def emit(self):
        waited = {e: {} for e in self.eng}
        for o in self.ops:
            need = {}
            for d in o.deps:
                if d is o:
                    continue
                if d.dma:
                    key, val = ("d", d.dslot), d.dval
                else:
                    if d.eng == "pe" and o.eng == "pe":
                        continue
                    key, val = d.eng, d.idx
                if waited[o.eng].get(key, -1) >= val:
                    continue
                if key not in need or need[key][0] < val:
                    need[key] = (val, d)
            for key, (val, d) in need.items():
                waited[o.eng][key] = val
                if not d.dma:
                    d.signal = True
                o.waits.append(d)
        signum = {e: 0 for e in self.eng}
        n_inst = 0
        for o in self.ops:
            e = self.eng[o.eng]
            if o.signal and not o.dma:
                signum[o.eng] += 1
                o.sig = signum[o.eng]
            for d in o.waits:
                if d.dma:
                    e.wait_ge(self.dsem[d.dslot], d.dval)
                else:
                    e.wait_ge(self.sem[d.eng], d.sig)
                n_inst += 1
            if o.fn is None:
                continue
            inst = o.fn(e)
            n_inst += 1
            if o.dma:
                inst.then_inc(self.dsem[o.dslot], 16)
            elif o.signal:
                inst.then_inc(self.sem[o.eng], 1)
        self.n_inst = n_inst
        self.n_sig = dict(signum)
        self.ops = []


def load_w_bf16(P, dst, src_ap, kchunks, ncols, split=1):
    v = src_ap.rearrange("(k p) n -> p k n", p=128)
    step = max(1, kchunks // split)
    for k0 in range(0, kchunks, step):
        k1 = min(kchunks, k0 + step)
        P.dma("pool", dst[:, k0:k1, :], v[:, k0:k1, :], writes=[(dst, k) for k in range(k0, k1)])


def load_bcast(P, dst, vec_ap, n):
    P.dma("sp", dst[:, 0:n], vec_ap.partition_broadcast(128), writes=[dst])


def transpose_block(P, C, x_f32, xbf, xT, tcol, ps_t, nchunks=8, cast_eng="pool"):
    P.copy(cast_eng, xbf[:, 0:nchunks * 128], x_f32[:, 0:nchunks * 128], [x_f32], [xbf])
    psv = ps_t.ap.bitcast(BF16)
    for c in range(nchunks):
        P.tr(psv[:, c * 128:(c + 1) * 128], xbf[:, c * 128:(c + 1) * 128], C["ident"][:, :],
             [xbf, C["ident"]], [ps_t])
    P.copy("dve", xT[:, 0:nchunks, tcol:tcol + 128],
           psv[:, 0:nchunks * 128].rearrange("p (c t) -> p c t", c=nchunks), [ps_t], [xT])


def layer_norm_block(P, r, out, gbc, bbc, stats, mv, eps, D=1024):
    nch = D // 512
    for h in range(nch):
        P.op("dve", lambda e, h=h: e.bn_stats(stats[:, h, :], r[:, h * 512:(h + 1) * 512]),
             reads=[r], writes=[stats])
    P.op("dve", lambda e: e.bn_aggr(mv[:, 0:2], stats[:, 0:nch, :]), reads=[stats], writes=[mv])
    P.ts("dve", mv[:, 2:3], mv[:, 1:2], eps, None, ALU.add, None, [mv], [mv])
    P.act(mv[:, 2:3], mv[:, 2:3], AF.Sqrt, [mv], [mv])
    P.op("dve", lambda e: e.reciprocal(mv[:, 2:3], mv[:, 2:3]), reads=[mv], writes=[mv])
    P.stt("dve", mv[:, 3:4], mv[:, 0:1], -1.0, mv[:, 2:3], ALU.mult, ALU.mult, [mv], [mv])
    P.act(r[:, 0:D], r[:, 0:D], AF.Identity, [r, mv], [r], bias=mv[:, 3:4], scale=mv[:, 2:3])
    P.tt("pool", r[:, 0:D], r[:, 0:D], gbc[:, 0:D], ALU.mult, [r, gbc], [r])
    P.tt("pool", out[:, 0:D], r[:, 0:D], bbc[:, 0:D], ALU.add, [r, bbc], [out])


def tiles_of(nblocks, per):
    out = []
    b = 0
    while b < nblocks:
        n = min(per, nblocks - b)
        out.append((b, n))
        b += n
    return out


def phase_ffn(P, C, x_in, in_blk0, x_out, out_blk0, nblocks, w_in, w_out, ln_g, ln_b):
    P.barrier()
    P.arena_reset(C["arena_base"])
    NF = D_FF // 128
    Win = P.alloc([8, 2 * D_FF], BF16, "Win")
    Wout = P.alloc([NF, D_MODEL], BF16, "Wout")
    gbc = P.alloc([D_MODEL], F32, "gbc")
    bbc = P.alloc([D_MODEL], F32, "bbc")
    TB = C.get('TB', 4)
    xin1 = P.alloc([D_MODEL], F32, "xin")
    xbf1 = P.alloc([D_MODEL], BF16, "xbf")
    xres1 = P.alloc([D_MODEL], F32, "xres")
    xin = [xin1, xin1]
    xbf = [xbf1, xbf1]
    xres = [xres1, xres1]
    xT = [P.alloc([8, TB * 128], BF16, "xT%d" % i) for i in range(2)]
    gT = P.alloc([NF, TB * 128], BF16, "gT")
    sg = [P.alloc([TB * 128], F32, "sg%d" % i) for i in range(2)]
    rr = [P.alloc([D_MODEL], F32, "r%d" % i) for i in range(3)]
    stats = [P.alloc([2, 6], F32, "st%d" % i) for i in range(3)]
    mv = [P.alloc([4], F32, "mv%d" % i) for i in range(3)]

    load_w_bf16(P, Win, w_in, 8, 2 * D_FF, split=8)
    load_w_bf16(P, Wout, w_out, NF, D_MODEL, split=2)
    load_bcast(P, gbc, ln_g, D_MODEL)
    load_bcast(P, bbc, ln_b, D_MODEL)

    psG = [P.ps_tiles[0], P.ps_tiles[1]]
    psU = [P.ps_tiles[2], P.ps_tiles[3]]
    psY = [P.ps_tiles[4], P.ps_tiles[5]]
    psT = P.ps_tiles[6]
    cres = 0.5 / DN_ALPHA
    eps = LN_EPS / (DN_ALPHA * DN_ALPHA)

    tl = tiles_of(nblocks, TB)
    psYr = PsRot(P, [4, 5, 7])
    psv = psT.ap.bitcast(BF16)

    def pro_load(ti, j):
        b0, nb = tl[ti]
        g = b0 + j
        s = g % 2
        P.dma("sp", xin[s][:, :], x_in[(in_blk0 + g) * 128:(in_blk0 + g + 1) * 128, :], writes=[xin[s]])
        P.copy("pool", xbf[s][:, :], xin[s][:, :], [xin[s]], [xbf[s]])

    def pro_tr(ti, j):
        b0, nb = tl[ti]
        s = (b0 + j) % 2
        xt = xT[ti % 2]
        for c in range(8):
            P.tr(psv[:, c * 128:(c + 1) * 128], xbf[s][:, c * 128:(c + 1) * 128], C["ident"][:, :],
                 [xbf[s], C["ident"]], [psT])
        P.copy("dve", xt[:, 0:8, j * 128:(j + 1) * 128],
               psv[:, 0:1024].rearrange("p (c t) -> p c t", c=8), [psT], [xt])

    def epilogue_ln(g):
        s3 = g % 3
        layer_norm_block(P, rr[s3], rr[s3], gbc, bbc, stats[s3], mv[s3], eps)
        P.dma("pool", x_out[(out_blk0 + g) * 128:(out_blk0 + g + 1) * 128, :], rr[s3][:, :], reads=[rr[s3]])

    grp = 0
    for j in range(tl[0][1]):
        pro_load(0, j)
        pro_tr(0, j)
    pending_ln = None
    for ti, (b0, nb) in enumerate(tl):
        NT = nb * 128
        xt = xT[ti % 2]
        nxt = tl[ti + 1][1] if ti + 1 < len(tl) else 0
        for fp in range(NF):
            pg = psG[grp % 2]
            pu = psU[grp % 2]
            sgt = sg[grp % 2]
            grp += 1
            for kc in range(8):
                P.mm(pg[:, 0:NT], Win[:, kc, fp * 128:(fp + 1) * 128], xt[:, kc, 0:NT],
                     kc == 0, kc == 7, [(Win, kc), xt], [pg])
            for kc in range(8):
                P.mm(pu[:, 0:NT], Win[:, kc, D_FF + fp * 128:D_FF + (fp + 1) * 128], xt[:, kc, 0:NT],
                     kc == 0, kc == 7, [(Win, kc), xt], [pu])
            P.act(sgt[:, 0:NT], pg[:, 0:NT], AF.Silu, [pg], [sgt])
            P.tt("dve", gT[:, fp, 0:NT], sgt[:, 0:NT], pu[:, 0:NT], ALU.mult, [sgt, pu], [(gT, fp)])
            if fp >= 1 and (fp - 1) % 5 == 0 and (fp - 1) // 5 < nxt:
                pro_load(ti + 1, (fp - 1) // 5)
            if fp >= 4 and (fp - 4) % 5 == 0 and (fp - 4) // 5 < nxt:
                pro_tr(ti + 1, (fp - 4) // 5)
        for j in range(nb):
            g = b0 + j
            s = g % 2
            P.dma("sp", xres[s][:, :], x_in[(in_blk0 + g) * 128:(in_blk0 + g + 1) * 128, :],
                  writes=[xres[s]])
            for half in range(2):
                py = psYr.next()
                for fc in range(NF):
                    P.mm(py[:, 0:512], gT[:, fc, j * 128:(j + 1) * 128],
                         Wout[:, fc, half * 512:(half + 1) * 512], fc == 0, fc == NF - 1,
                         [(gT, fc), (Wout, fc)], [py])
                P.stt("dve", rr[g % 3][:, half * 512:(half + 1) * 512], py[:, 0:512], cres,
                      xres[s][:, half * 512:(half + 1) * 512], ALU.mult, ALU.add,
                      [py, xres[s]], [rr[g % 3]])
            if pending_ln is not None:
                epilogue_ln(pending_ln)
            pending_ln = g
    if pending_ln is not None:
        epilogue_ln(pending_ln)


def setup_consts(P, consts_dram):
    C = {}
    ident = P.alloc([128], BF16, "ident")
    P.dma("pool", ident[:, :], consts_dram["ident"][:, :], writes=[ident])
    C["ident"] = ident
    onesF = P.alloc([128], F32, "onesF")
    P.op("pool", lambda e: e.memset(onesF[:, :], 1.0), writes=[onesF])
    C["onesF"] = onesF
    onesB = P.alloc([128], BF16, "onesB")
    P.op("pool", lambda e: e.memset(onesB[:, :], 1.0), writes=[onesB])
    C["onesB"] = onesB
    trim = P.alloc([128], BF16, "trim")
    P.dma("pool", trim[:, :], consts_dram["trim"][:, :], writes=[trim])
    C["trim"] = trim
    C["dmask_dram"] = consts_dram["dmask"]
    C["arena_base"] = P.cur
    return C


def host_consts():
    k = np.arange(128)
    trim = (k[:, None] >= k[None, :]).astype(np.float32)
    dm = (k[:, None] < k[None, :]).astype(np.float32)
    return {"ident": np.eye(128, dtype=np.float32), "trim": trim,
            "dmask": np.ascontiguousarray(np.tile(dm, (1, 4)))}


class PsRot:
    def __init__(self, P, banks):
        self.t = [P.ps_tiles[b] for b in banks]
        self.i = 0

    def next(self):
        t = self.t[self.i % len(self.t)]
        self.i += 1
        return t


def load_xT_tile(P, C, x_dram, blk0, nb, xin, xbf, xt, psT, ctr):
    for j in range(nb):
        s = ctr[0] % 2
        ctr[0] += 1
        P.dma("sp", xin[s][:, :], x_dram[(blk0 + j) * 128:(blk0 + j + 1) * 128, :], writes=[xin[s]])
        transpose_block(P, C, xin[s], xbf[s], xt, j * 128, psT)


def phase_inproj_even(P, C, x_in, nblocks, w_in, hcT, qT, kT, vtm):
    P.barrier()
    P.arena_reset(C["arena_base"])
    NCOL = 2560
    We = P.alloc([8, NCOL], BF16, "We")
    wv = w_in.rearrange("(k p) n -> p k n", p=128)
    for k0 in range(0, 8, 2):
        P.dma("pool", We[:, k0:k0 + 2, 0:2048], wv[:, k0:k0 + 2, 0:2048],
              writes=[(We, k) for k in range(k0, k0 + 2)])
    wvv = wv[:, :, 2048:2560].rearrange("p k (h two d) -> p k two h d", two=2, d=64)
    for par in range(2):
        for k in range(8):
            P.dma("pool", We[:, k, 2048 + par * 256:2048 + (par + 1) * 256].rearrange(
                "p (h d) -> p h d", d=64), wvv[:, k, par], writes=[(We, ("v", par, k))])
    TB = 4
    xin = [P.alloc([D_MODEL], F32, "xin%d" % i) for i in range(2)]
    xbf = [P.alloc([D_MODEL], BF16, "xbf%d" % i) for i in range(2)]
    xT = [P.alloc([8, TB * 128], BF16, "xT%d" % i) for i in range(2)]
    sgm = [P.alloc([TB * 128], F32, "sgm%d" % i) for i in range(2)]
    ob = [P.alloc([TB * 128], BF16, "ob%d" % i) for i in range(4)]
    psr = PsRot(P, [0, 1, 2, 3, 4, 5])
    psT = P.ps_tiles[6]
    ctr = [0]
    oi = 0
    for ti, (b0, nb) in enumerate(tiles_of(nblocks, TB)):
        NT = nb * 128
        t0 = b0 * 128
        xt = xT[ti % 2]
        load_xT_tile(P, C, x_in, b0, nb, xin, xbf, xt, psT, ctr)
        for c in range(4):
            pa = psr.next()
            pg = psr.next()
            for kc in range(8):
                P.mm(pa[:, 0:NT], We[:, kc, c * 128:(c + 1) * 128], xt[:, kc, 0:NT],
                     kc == 0, kc == 7, [(We, kc), xt], [pa])
            for kc in range(8):
                P.mm(pg[:, 0:NT], We[:, kc, 512 + c * 128:512 + (c + 1) * 128], xt[:, kc, 0:NT],
                     kc == 0, kc == 7, [(We, kc), xt], [pg])
            sg = sgm[c % 2]
            P.act(sg[:, 0:NT], pg[:, 0:NT], AF.Sigmoid, [pg], [sg])
            o = ob[oi % 4]
            oi += 1
            P.tt("dve", o[:, 0:NT], sg[:, 0:NT], pa[:, 0:NT], ALU.mult, [sg, pa], [o])
            P.dma("sp", hcT[c * 128:(c + 1) * 128, t0:t0 + NT], o[:, 0:NT], reads=[o])
        for c in range(8):
            pq = psr.next()
            col = 1024 + c * 128
            for kc in range(8):
                P.mm(pq[:, 0:NT], We[:, kc, col:col + 128], xt[:, kc, 0:NT],
                     kc == 0, kc == 7, [(We, kc), xt], [pq])
            o = ob[oi % 4]
            oi += 1
            if c < 4:
                P.act(o[:, 0:NT], pq[:, 0:NT], AF.Copy, [pq], [o], scale=0.125)
                P.dma("sp", qT[c * 128:(c + 1) * 128, t0:t0 + NT], o[:, 0:NT], reads=[o])
            else:
                P.copy("dve", o[:, 0:NT], pq[:, 0:NT], [pq], [o])
                P.dma("sp", kT[(c - 4) * 128:(c - 3) * 128, t0:t0 + NT], o[:, 0:NT], reads=[o])
        for j in range(nb):
            pv = psr.next()
            for kc in range(8):
                P.mm(pv[:, 0:512], xt[:, kc, j * 128:(j + 1) * 128], We[:, kc, 2048:2560],
                     kc == 0, kc == 7, [(We, ("v", 0, kc)), (We, ("v", 1, kc)), xt], [pv])
            o = ob[oi % 4]
            oi += 1
            if j % 2 == 0:
                P.act(o[:, 0:512], pv[:, 0:512], AF.Copy, [pv], [o])
            else:
                P.copy("dve", o[:, 0:512], pv[:, 0:512], [pv], [o])
            P.dma("sp", vtm[(b0 + j) * 128:(b0 + j + 1) * 128, :], o[:, 0:512], reads=[o])


def phase_conv(P, C, hcT, S, w_dwT, cprm, mixT):
    P.barrier()
    P.arena_reset(C["arena_base"])
    KW = 31
    PAD = KW - 1
    hc = [P.alloc([PAD + S], BF16, "hc%d" % g) for g in range(4)]
    Dg = P.alloc([4, KW, 128], BF16, "Dg")
    wT = P.alloc([4, KW], F32, "wT")
    prm = P.alloc([4, 3], F32, "prm")
    cv = [P.alloc([512], F32, "cv%d" % g) for g in range(4)]
    sq = [P.alloc([512], F32, "sq%d" % g) for g in range(4)]
    mean_sb = P.alloc([512], F32, "mean")
    rstd = P.alloc([512], F32, "rstd")
    tmp = [P.alloc([512], F32, "tmp%d" % i) for i in range(2)]
    ob = [P.alloc([512], BF16, "ob%d" % i) for i in range(2)]
    P.dma("sp", wT[:, :, :], w_dwT[:, :, :], writes=[wT])
    P.dma("sp", prm[:, :, :], cprm[:, :, :], writes=[(prm, 0), (prm, 1), (prm, 2)])
    for g in range(4):
        P.op("pool", lambda e, g=g: e.memset(hc[g][:, 0:PAD], 0.0), writes=[(hc[g], "pad")])
        P.dma("sp", hc[g][:, PAD:PAD + S], hcT[g * 128:(g + 1) * 128, :], writes=[hc[g]])
        for k in range(KW):
            P.ts("dve", Dg[:, g, k, :], C["ident"][:, :], wT[:, g, k:k + 1], None, ALU.mult, None,
                 [C["ident"], wT], [(Dg, g)])
    psr = PsRot(P, [0, 1, 2, 3])
    psM = P.ps_tiles[4]
    psQ = P.ps_tiles[5]
    oi = 0
    for t0 in range(0, S, 512):
        N = min(512, S - t0)
        for g in range(4):
            pc = psr.next()
            for k in range(KW):
                P.mm(pc[:, 0:N], Dg[:, g, k, :], hc[g][:, t0 + k:t0 + k + N], k == 0, k == KW - 1,
                     [(Dg, g), hc[g], (hc[g], "pad")], [pc])
            P.act(cv[g][:, 0:N], pc[:, 0:N], AF.Identity, [pc, (prm, 0)], [cv[g]], bias=prm[:, g, 0:1])
            P.act(sq[g][:, 0:N], pc[:, 0:N], AF.Square, [pc, (prm, 0)], [sq[g]], bias=prm[:, g, 0:1])
        for g in range(4):
            P.mm(psM[:, 0:N], C["onesF"][:, :], cv[g][:, 0:N], g == 0, g == 3, [C["onesF"], cv[g]], [psM])
        for g in range(4):
            P.mm(psQ[:, 0:N], C["onesF"][:, :], sq[g][:, 0:N], g == 0, g == 3, [C["onesF"], sq[g]], [psQ])
        P.act(mean_sb[:, 0:N], psM[:, 0:N], AF.Copy, [psM], [mean_sb], scale=1.0 / 512)
        P.tt("dve", rstd[:, 0:N], mean_sb[:, 0:N], mean_sb[:, 0:N], ALU.mult, [mean_sb], [rstd])
        P.stt("dve", rstd[:, 0:N], psQ[:, 0:N], 1.0 / 512, rstd[:, 0:N], ALU.mult, ALU.subtract,
              [psQ, rstd], [rstd])
        P.ts("dve", rstd[:, 0:N], rstd[:, 0:N], LN_EPS, None, ALU.add, None, [rstd], [rstd])
        P.act(rstd[:, 0:N], rstd[:, 0:N], AF.Sqrt, [rstd], [rstd])
        P.op("dve", lambda e, N=N: e.reciprocal(rstd[:, 0:N], rstd[:, 0:N]), reads=[rstd], writes=[rstd])
        for g in range(4):
            tp = tmp[g % 2]
            eng = "dve" if g % 2 == 0 else "pool"
            P.tt(eng, tp[:, 0:N], cv[g][:, 0:N], mean_sb[:, 0:N], ALU.subtract, [cv[g], mean_sb], [tp])
            P.tt(eng, tp[:, 0:N], tp[:, 0:N], rstd[:, 0:N], ALU.mult, [tp, rstd], [tp])
            o = ob[oi % 2]
            oi += 1
            P.act(o[:, 0:N], tp[:, 0:N], AF.Silu, [tp, (prm, 1), (prm, 2)], [o],
                  scale=prm[:, g, 1:2], bias=prm[:, g, 2:3])
            P.dma("sp", mixT[g * 128:(g + 1) * 128, t0:t0 + N], o[:, 0:N], reads=[o])


def phase_outproj(P, C, mixT, x_res, x_out, nblocks, w_o, ln_g, ln_b):
    P.barrier()
    P.arena_reset(C["arena_base"])
    Wo = P.alloc([8, D_MODEL], BF16, "Wo")
    load_w_bf16(P, Wo, w_o, 8, D_MODEL, split=2)
    gbc = P.alloc([D_MODEL], F32, "gbc")
    bbc = P.alloc([D_MODEL], F32, "bbc")
    load_bcast(P, gbc, ln_g, D_MODEL)
    load_bcast(P, bbc, ln_b, D_MODEL)
    TB = 4
    mT = [P.alloc([8, TB * 128], BF16, "mT%d" % i) for i in range(2)]
    xres = [P.alloc([D_MODEL], F32, "xres%d" % i) for i in range(2)]
    rr = [P.alloc([D_MODEL], F32, "r%d" % i) for i in range(2)]
    stats = [P.alloc([2, 6], F32, "st%d" % i) for i in range(2)]
    mv = [P.alloc([4], F32, "mv%d" % i) for i in range(2)]
    psr = PsRot(P, [0, 1, 2, 3])
    cres = 1.0 / DN_ALPHA
    eps = LN_EPS / (DN_ALPHA * DN_ALPHA)
    for ti, (b0, nb) in enumerate(tiles_of(nblocks, TB)):
        NT = nb * 128
        mt = mT[ti % 2]
        P.dma("sp", mt[:, :, 0:NT], mixT[:, b0 * 128:b0 * 128 + NT].rearrange("(c p) t -> p c t", p=128),
              writes=[mt])
        for j in range(nb):
            g = b0 + j
            s = g % 2
            P.dma("sp", xres[s][:, :], x_res[g * 128:(g + 1) * 128, :], writes=[xres[s]])
            for half in range(2):
                py = psr.next()
                for fc in range(8):
                    P.mm(py[:, 0:512], mt[:, fc, j * 128:(j + 1) * 128],
                         Wo[:, fc, half * 512:(half + 1) * 512], fc == 0, fc == 7,
                         [mt, (Wo, fc)], [py])
                P.stt("dve", rr[s][:, half * 512:(half + 1) * 512], py[:, 0:512], cres,
                      xres[s][:, half * 512:(half + 1) * 512], ALU.mult, ALU.add,
                      [py, xres[s]], [rr[s]])
            layer_norm_block(P, rr[s], rr[s], gbc, bbc, stats[s], mv[s], eps)
            P.dma("pool", x_out[g * 128:(g + 1) * 128, :], rr[s][:, :], reads=[rr[s]])


ATT_WIN = 3


def phase_attn(P, C, qT, kT, vtm, S, mixT, row0):
    P.barrier()
    P.arena_reset(C["arena_base"])
    NBLK = S // 128
    dmask = P.alloc([512], F32, "dmask")
    P.dma("sp", dmask[:, :], C["dmask_dram"][:, :], writes=[dmask])
    C = dict(C)
    C["dmask"] = dmask
    kTs = P.alloc([4, S], BF16, "kTs")
    qTs = P.alloc([4, S], BF16, "qTs")
    vs = P.alloc([NBLK, 256], BF16, "vs")
    NB2 = 2
    ex = [[P.alloc([512], F32, "ex%d_%d" % (b, d)) for d in range(ATT_WIN)] for b in range(NB2)]
    spb = [[P.alloc([512], BF16, "sp%d_%d" % (b, d)) for d in range(ATT_WIN)] for b in range(NB2)]
    att = [[P.alloc([512], BF16, "att%d_%d" % (b, d)) for d in range(ATT_WIN)] for b in range(NB2)]
    wt = [P.alloc([512], F32, "w%d" % i) for i in range(2)]
    ot = [P.alloc([512], BF16, "ot%d" % i) for i in range(2)]
    psZ = PsRot(P, [0, 1, 2])
    psL = PsRot(P, [3, 4, 5])
    psO = PsRot(P, [6, 7])
    wi = 0
    for c in range(4):
        P.dma("sp", kTs[:, c, :], kT[c * 128:(c + 1) * 128, :], writes=[(kTs, c)])
        P.dma("sp", qTs[:, c, :], qT[c * 128:(c + 1) * 128, :], writes=[(qTs, c)])
    for par in range(2):
        pb = par * 64
        vsrc = vtm[:, par * 256:(par + 1) * 256].rearrange("(b p) f -> p b f", p=128)
        for b0 in range(0, NBLK, 8):
            b1 = min(NBLK, b0 + 8)
            P.dma("sp", vs[:, b0:b1, :], vsrc[:, b0:b1, :], writes=[(vs, b0 // 8)])
        mixv = mixT[row0:row0 + 512, :].rearrange("(h two d) t -> two d h t", two=2, d=64)[par]
        for i in range(NBLK):
            b = i % NB2
            ndk = min(ATT_WIN, i + 1)
            for dk in range(ndk):
                kb = i - dk
                pz = psZ.next()
                for hh in range(4):
                    P.mm(pz[:, hh * 128:(hh + 1) * 128], kTs[pb:pb + 64, hh, kb * 128:(kb + 1) * 128],
                         qTs[pb:pb + 64, hh, i * 128:(i + 1) * 128], True, True,
                         [(kTs, hh), (qTs, hh)], [pz])
                e_t = ex[b][dk]
                P.act(e_t[:, :], pz[:, 0:512], AF.Exp, [pz], [e_t])
                if dk == 0:
                    P.tt("pool", e_t[:, :], e_t[:, :], C["dmask"][:, :], ALU.mult, [e_t, C["dmask"]], [e_t])
                P.act(spb[b][dk][:, :], e_t[:, :], AF.Ln, [e_t], [spb[b][dk]], bias=1.0)
            for dk in range(ndk):
                pl = psL.next()
                P.mm(pl[:, 0:512], C["trim"][:, :], spb[b][dk][:, :], True, dk == 0,
                     [C["trim"], spb[b][dk]], [pl])
                for d2 in range(dk):
                    P.mm(pl[:, 0:512], C["onesB"][:, :], spb[b][d2][:, :], False, d2 == dk - 1,
                         [C["onesB"], spb[b][d2]], [pl])
                w = wt[wi % 2]
                wi += 1
                P.act(w[:, :], pl[:, 0:512], AF.Exp, [pl], [w], scale=-1.0)
                P.tt("dve", att[b][dk][:, :], ex[b][dk][:, :], w[:, :], ALU.mult, [ex[b][dk], w], [att[b][dk]])
            po = psO.next()
            for hh in range(4):
                for dk in range(ndk):
                    kb = i - dk
                    P.mm(po[0:64, hh * 128:(hh + 1) * 128], vs[:, kb, hh * 64:(hh + 1) * 64],
                         att[b][dk][:, hh * 128:(hh + 1) * 128], dk == 0, dk == ndk - 1,
                         [(vs, kb // 8), att[b][dk]], [po])
            o = ot[i % 2]
            P.copy("dve", o[0:64, :], po[0:64, 0:512], [po], [o])
            P.dma("sp", mixv[:, :, i * 128:(i + 1) * 128],
                  o[0:64, :].rearrange("d (h t) -> d h t", h=4), reads=[o])


LDC = float(np.exp(-0.5))
GN_EPS = 64 * 1e-5
NPRM1 = 42


def phase_odd_mixer(P, C, x_in, nblocks, D, mixT):
    P.barrier()
    P.arena_reset(C["arena_base"])
    A = P.alloc
    Wi = A([8, 2304], BF16, "Wi")
    load_w_bf16(P, Wi, D["w_in"], 8, 2304, split=4)
    wa2 = A([512], BF16, "wa2")
    g2s = A([512], BF16, "g2s")
    wp = A([4, 128], BF16, "wp")
    P.dma("pool", wa2[:, :], D["wa2"][:, :], writes=[wa2])
    P.dma("pool", g2s[:, :], D["g2"][:, :], writes=[g2s])
    P.dma("pool", wp[:, :, :], D["wpool"][:, :, :], writes=[wp])
    prm = A([NPRM1], F32, "prm1")
    P.dma("sp", prm[:, :], D["prm1"][:, :], writes=[prm])
    mGa = A([4, 256], F32, "mGa")
    mN = A([4, 128], F32, "mN")
    triu = A([128], F32, "triu")
    bd = A([128], F32, "bd")
    hsel = A([2], BF16, "hsel")
    icnt = A([4, 128], F32, "icnt")
    P.dma("sp", mGa[:, :, :], D["mGa"].rearrange("p (h t) -> p h t", h=4), writes=[mGa])
    P.dma("sp", mN[:, :, :], D["mN"].rearrange("p (h t) -> p h t", h=4), writes=[mN])
    P.dma("sp", triu[:, :], D["triu"][:, :], writes=[triu])
    P.dma("sp", bd[:, :], D["bd"][:, :], writes=[bd])
    P.dma("pool", hsel[:, :], D["hsel"][:, :], writes=[hsel])
    P.dma("sp", icnt[:, :, :], D["icnt"].rearrange("p (h t) -> p h t", h=4), writes=[icnt])
    cwin = A([4, 128], F32, "cwin")
    for g in range(4):
        P.op("pool", lambda e, g=g: e.memset(cwin[:, g, :], 1.0 / (2 << g)), writes=[cwin])
    w0tm = A([512], F32, "w0tm")
    lnxg = A([512], F32, "lnxg")
    lnxb = A([512], F32, "lnxb")
    load_bcast(P, w0tm, D["w0row"], 512)
    load_bcast(P, lnxg, D["lnxg"], 512)
    load_bcast(P, lnxb, D["lnxb"], 512)
    Ssh = A([14, 128], F32, "Ssh")
    ones3 = Ssh
    P.op("pool", lambda e: e.memset(ones3[:, :, :], 1.0), writes=[ones3])
    mu_bc = A([14, 128], F32, "mu_bc")
    P.tt("pool", mu_bc[:, :, :], ones3[:, :, :], prm[:, 0:14].unsqueeze(2).to_broadcast([128, 14, 128]),
         ALU.mult, [ones3, prm], [mu_bc])

    def bc4(col, name):
        t = A([4, 128], F32, name)
        P.tt("pool", t[:, :, :], ones3[:, 0:4, :],
             prm[:, col:col + 4].unsqueeze(2).to_broadcast([128, 4, 128]), ALU.mult, [ones3, prm], [t])
        return t

    w0f = bc4(14, "w0f")
    a0f = bc4(18, "a0f")
    kkb = bc4(22, "kkb")
    kab = bc4(26, "kab")
    rkb = bc4(30, "rkb")
    oma = A([4, 128], F32, "oma")
    P.ts("pool", oma[:, :, :], kab[:, :, :], -1.0, 1.0, ALU.mult, ALU.add, [kab], [oma])
    pbs = A([4], F32, "pbs")
    P.tt("pool", pbs[:, :], prm[:, 34:38], prm[:, 38:42], ALU.mult, [prm], [pbs])
    identB = C["ident"]
    identB4 = A([4, 128], BF16, "identB4")
    for h in range(4):
        P.copy("pool", identB4[:, h, :], identB[:, :], [identB], [identB4])
    identF = A([128], F32, "identF")
    P.copy("pool", identF[:, :], identB[:, :], [identB], [identF])

    pbuf = A([14, 129], F32, "pbuf")
    P.op("pool", lambda e: e.memset(pbuf[:, :, 0:1], 0.0), writes=[(pbuf, "h")])
    PADU = 16
    ubuf = A([4, PADU + 128], F32, "ubuf")
    P.op("pool", lambda e: e.memset(ubuf[:, :, 0:PADU], 0.0), writes=[(ubuf, "h")])
    Ssf = A([4, 64], F32, "Ssf")
    Sb = A([4, 64], BF16, "Sb")
    P.op("pool", lambda e: e.memset(Ssf[:, :, :], 0.0), writes=[Ssf])
    P.op("pool", lambda e: e.memset(Sb[:, :, :], 0.0), writes=[Sb])

    xin = [A([D_MODEL], F32, "xin%d" % i) for i in range(2)]
    xbf = [A([D_MODEL], BF16, "xbf%d" % i) for i in range(2)]
    xT = [A([8, 128], BF16, "xT%d" % i) for i in range(2)]
    lor = A([128], BF16, "lor")
    sgb = A([128], BF16, "sgb")
    sgf = A([4, 128], F32, "sgf")
    icl = A([4, 128], F32, "icl")
    sgt = A([512], F32, "sgt")
    gate_sb = A([512], F32, "gate_sb")
    clsb = A([4, 128], F32, "clsb")
    cle = A([4, 128], F32, "cle")
    E1 = A([4, 128], F32, "E1")
    E2 = A([4, 128], F32, "E2")
    E3 = A([4, 128], F32, "E3")
    E4 = A([4, 128], F32, "E4")
    nbv = A([4], F32, "nbv")
    gC = A([4], F32, "gC")
    kk = A([4, 128], F32, "kk")
    sq = A([4, 128], F32, "sq")
    rn = A([4, 128], F32, "rn")
    t1 = A([4, 128], F32, "t1")
    kmod = A([4, 128], F32, "kmod")
    bvec = A([4, 128], F32, "bvec")
    ARf = A([4, 256], BF16, "ARf")
    ARz = [A([4, 256], BF16, "ARz%d" % p) for p in range(2)]
    Kt = A([4, 128], BF16, "Kt")
    Bt = A([4, 128], BF16, "Bt")
    Kh = A([4, 128], BF16, "Kh")
    Bh = A([4, 128], BF16, "Bh")
    rkr = A([4, 128], BF16, "rkr")
    Vtf = A([512], F32, "Vtf")
    Vtb = A([512], BF16, "Vtb")
    Kht = A([512], BF16, "Kht")
    Bht = A([512], BF16, "Bht")
    bon = A([8], F32, "bon")
    GkM = [A([4, 256], BF16, "GkM%d" % p) for p in range(2)]
    GbM = [A([4, 256], BF16, "GbM%d" % p) for p in range(2)]
    Xk = [[A([4, 128], BF16, "X%d_%d" % (p, i)) for i in range(2)] for p in range(2)]
    Nk = [[A([4, 128], BF16, "N%d_%d" % (p, i)) for i in range(2)] for p in range(2)]
    Pk = [[A([4, 128], BF16, "P%d_%d" % (p, i)) for i in range(2)] for p in range(2)]
    Qk = [[A([4, 128], BF16, "Q%d_%d" % (p, i)) for i in range(2)] for p in range(2)]
    RHSb = A([512], BF16, "RHSb")
    Ub = A([512], BF16, "Ub")
    ysb = A([512], F32, "ysb")
    ysq = A([512], F32, "ysq")
    st = A([4, 8], F32, "st")
    yob = A([512], BF16, "yob")
    yT = A([4, 128], BF16, "yT")
    pooled = A([4, 128], BF16, "pooled")
    ptmp = [A([4, PADU + 128], F32, "ptmp%d" % i) for i in range(2)]
    pout = A([4, 128], BF16, "pout")
    pm = A([2], F32, "pm")
    P.op("pool", lambda e: e.memset(pm[:, :], 0.0), writes=[pm])
    P.op("pool", lambda e: e.memset(pm[0:64, 0:1], 1.0), writes=[pm])
    P.op("pool", lambda e: e.memset(pm[64:128, 1:2], 1.0), writes=[pm])

    psr = PsRot(P, [0, 1, 2, 3, 4, 5])
    psrA = PsRot(P, [0, 1, 2])
    psrB = PsRot(P, [3, 4, 5])
    psT = P.ps_tiles[6]
    psS = P.ps_tiles[7]
    ctr = [0]

    def v3(ps, n=4, w=128):
        return ps[:, 0:n * w].rearrange("p (c t) -> p c t", c=n)

    for blk in range(nblocks):
        t0 = blk * 128
        xt = xT[blk % 2]
        if C.get('odd_stop', 99) <= -1:
            continue
        load_xT_tile(P, C, x_in, blk, 1, xin, xbf, xt, psT, ctr)
        if C.get('odd_stop', 99) <= 0:
            continue
        for grp in range(5):
            c0 = grp * 4
            ncg = min(4, 18 - c0)
            pp = psr.next()
            for ci in range(ncg):
                c = c0 + ci
                for kc in range(8):
                    P.mm(pp[:, ci * 128:(ci + 1) * 128], Wi[:, kc, c * 128:(c + 1) * 128], xt[:, kc, :],
                         kc == 0, kc == 7, [(Wi, kc), xt], [pp])
            if grp < 3:
                P.copy("dve", pbuf[:, c0:c0 + 4, 1:129], v3(pp), [pp], [pbuf])
            elif grp == 3:
                P.copy("dve", pbuf[:, 12:14, 1:129], v3(pp, 2), [pp], [pbuf])
                P.copy("dve", ubuf[:, 0:2, PADU:PADU + 128], pp[:, 256:512].rearrange("p (c t) -> p c t", c=2),
                       [pp], [ubuf])
            else:
                P.copy("dve", ubuf[:, 2:4, PADU:PADU + 128], v3(pp, 2), [pp], [ubuf])
        if C.get('odd_stop', 99) <= 0.5:
            continue
        P.tt("dve", Ssh[:, :, :], pbuf[:, :, 0:128], pbuf[:, :, 1:129], ALU.subtract,
             [pbuf, (pbuf, "h")], [Ssh])
        P.tt("pool", Ssh[:, :, :], Ssh[:, :, :], mu_bc[:, :, :], ALU.mult, [Ssh, mu_bc], [Ssh])
        P.tt("dve", Ssh[:, :, :], Ssh[:, :, :], pbuf[:, :, 1:129], ALU.add, [Ssh, pbuf], [Ssh])
        P.copy("pool", pbuf[:, :, 0:1], pbuf[:, :, 128:129], [pbuf], [(pbuf, "h")])
        if C.get('odd_stop', 99) <= 1:
            continue
        r_ = Ssh[:, 0:4, :]
        k_ = Ssh[:, 4:8, :]
        v_ = Ssh[:, 8:12, :]
        P.act(lor[0:64, :], Ssh[0:64, 12, :], AF.Tanh, [Ssh], [(lor, 0)])
        P.copy("dve", lor[64:128, :], Ssh[64:128, 12, :], [Ssh], [(lor, 1)])
        P.act(sgb[:, :], Ssh[:, 13, :], AF.Sigmoid, [Ssh], [sgb])
        pdw = psr.next()
        for c in range(4):
            P.mm(pdw[:, c * 128:(c + 1) * 128], wa2[0:64, c * 128:(c + 1) * 128], lor[0:64, :], True, True,
                 [wa2, (lor, 0)], [pdw])
        pda = psr.next()
        for c in range(4):
            P.mm(pda[:, c * 128:(c + 1) * 128], wa2[64:128, c * 128:(c + 1) * 128], lor[64:128, :], True, True,
                 [wa2, (lor, 1)], [pda])
        pdt = psr.next()
        P.mm(pdt[:, 0:512], lor[0:64, :], wa2[0:64, :], True, True, [wa2, (lor, 0)], [pdt])
        pgt = psr.next()
        P.mm(pgt[:, 0:512], sgb[:, :], g2s[:, :], True, True, [sgb, g2s], [pgt])
        P.tt("dve", sgf[:, :, :], v3(pdw), w0f[:, :, :], ALU.add, [pdw, w0f], [sgf])
        P.act(sgf[:, :, :], sgf[:, :, :], AF.Sigmoid, [sgf], [sgf])
        P.tt("dve", icl[:, :, :], v3(pda), a0f[:, :, :], ALU.add, [pda, a0f], [icl])
        P.act(icl[:, :, :], icl[:, :, :], AF.Sigmoid, [icl], [icl])
        P.tt("dve", sgt[:, :], pdt[:, 0:512], w0tm[:, :], ALU.add, [pdt, w0tm], [sgt])
        P.act(sgt[:, :], sgt[:, :], AF.Sigmoid, [sgt], [sgt])
        P.copy("act", gate_sb[:, :], pgt[:, 0:512], [pgt], [gate_sb])
        if C.get('odd_stop', 99) <= 2:
            continue
        pcl = psr.next()
        for c in range(4):
            P.mm(pcl[:, c * 128:(c + 1) * 128], sgt[:, c * 128:(c + 1) * 128], triu[:, :], True, True,
                 [sgt, triu], [pcl])
        P.copy("dve", clsb[:, :, :], v3(pcl), [pcl], [clsb])
        P.tt("pool", cle[:, :, :], clsb[:, :, :], sgf[:, :, :], ALU.subtract, [clsb, sgf], [cle])
        P.ts("dve", nbv[:, :], clsb[:, :, 127], -LDC, None, ALU.mult, None, [clsb], [nbv])
        P.tt("pool", kk[:, :, :], k_, kkb[:, :, :], ALU.mult, [Ssh, kkb], [kk])
        P.tt("pool", sq[:, :, :], kk[:, :, :], kk[:, :, :], ALU.mult, [kk], [sq])
        pss = psr.next()
        P.mm(pss[:, 0:512], bd[:, :], sq[:, :, :].rearrange("p c t -> p (c t)"), True, True, [bd, sq], [pss])
        P.ts("dve", rn[:, :, :], v3(pss), 1e-24, None, ALU.max, None, [pss], [rn])
        P.act(rn[:, :, :], rn[:, :, :], AF.Sqrt, [rn], [rn])
        P.op("dve", lambda e: e.reciprocal(rn[:, :, :], rn[:, :, :]), reads=[rn], writes=[rn])
        P.tt("dve", kk[:, :, :], kk[:, :, :], rn[:, :, :], ALU.mult, [kk, rn], [kk])
        P.tt("pool", t1[:, :, :], icl[:, :, :], kab[:, :, :], ALU.mult, [icl, kab], [t1])
        P.tt("pool", t1[:, :, :], t1[:, :, :], oma[:, :, :], ALU.add, [t1, oma], [t1])
        P.tt("dve", kmod[:, :, :], k_, t1[:, :, :], ALU.mult, [Ssh, t1], [kmod])
        P.tt("pool", bvec[:, :, :], kk[:, :, :], icl[:, :, :], ALU.mult, [kk, icl], [bvec])
        P.act(E1[:, :, :], clsb[:, :, :], AF.Exp, [clsb], [E1], scale=-LDC)
        P.act(E2[:, :, :], clsb[:, :, :], AF.Exp, [clsb], [E2], scale=LDC)
        P.act(E3[:, :, :], cle[:, :, :], AF.Exp, [cle], [E3], scale=-LDC)
        for c in range(4):
            P.act(E4[:, c, :], clsb[:, c, :], AF.Exp, [clsb, nbv], [E4], scale=LDC, bias=nbv[:, c:c + 1])
        P.act(gC[:, :], nbv[:, :], AF.Exp, [nbv], [gC])
        P.stt("dve", ARf[:, :, 0:128], kk[:, :, :], -1.0, E3[:, :, :], ALU.mult, ALU.mult, [kk, E3], [(ARf, 0)])
        P.tt("pool", ARf[:, :, 128:256], r_, E1[:, :, :], ALU.mult, [Ssh, E1], [(ARf, 1)])
        for p in range(2):
            P.ts("dve" if p == 0 else "pool", ARz[p][:, :, :], ARf[:, :, :], pm[:, p:p + 1], None, ALU.mult, None,
                 [(ARf, 0), (ARf, 1), pm], [ARz[p]])
        P.tt("dve", Kt[:, :, :], kmod[:, :, :], E2[:, :, :], ALU.mult, [kmod, E2], [Kt])
        P.tt("pool", Bt[:, :, :], bvec[:, :, :], E2[:, :, :], ALU.mult, [bvec, E2], [Bt])
        P.tt("dve", Kh[:, :, :], kmod[:, :, :], E4[:, :, :], ALU.mult, [kmod, E4], [Kh])
        P.tt("pool", Bh[:, :, :], bvec[:, :, :], E4[:, :, :], ALU.mult, [bvec, E4], [Bh])
        P.tt("pool", t1[:, :, :], r_, rkb[:, :, :], ALU.mult, [Ssh, rkb], [t1])
        P.tt("dve", rkr[:, :, :], t1[:, :, :], kmod[:, :, :], ALU.mult, [t1, kmod], [rkr])
        if C.get('odd_stop', 99) <= 3:
            continue
        pvt = psr.next()
        for c in range(4):
            P.tr(pvt[:, c * 128:(c + 1) * 128], Ssh[:, 8 + c, :], identF[:, :], [Ssh, identF], [pvt])
        P.copy("act", Vtf[:, :], pvt[:, 0:512], [pvt], [Vtf])
        P.copy("dve", Vtb[:, :], pvt[:, 0:512], [pvt], [Vtb])
        psv = psT.ap.bitcast(BF16)
        for c in range(4):
            P.tr(psv[:, c * 128:(c + 1) * 128], Kh[:, c, :], identB[:, :], [Kh, identB], [psT])
        for c in range(4):
            P.tr(psv[:, 512 + c * 128:512 + (c + 1) * 128], Bh[:, c, :], identB[:, :], [Bh, identB], [psT])
        P.copy("act", Kht[:, :], psv[:, 0:512], [psT], [Kht])
        P.copy("dve", Bht[:, :], psv[:, 512:1024], [psT], [Bht])
        pbn = psr.next()
        for c in range(4):
            P.mm(pbn[:, c * 2:(c + 1) * 2], rkr[:, c, :], hsel[:, :], True, True, [rkr, hsel], [pbn])
        P.copy("act", bon[:, :], pbn[:, 0:8], [pbn], [bon])
        if C.get('odd_stop', 99) <= 4:
            continue
        TT = [None, None]

        def intra(par, pr):
            az = ARz[par]
            pgk = [pr.next(), pr.next()]
            for hh in range(4):
                P.mm(pgk[hh // 2][:, (hh % 2) * 256:(hh % 2 + 1) * 256], Kt[:, hh, :], az[:, hh, :], True, True,
                     [Kt, az], [pgk[hh // 2]])
            yield
            for i2 in range(2):
                P.tt("dve", GkM[par][:, 2 * i2:2 * i2 + 2, :], v3(pgk[i2], 2, 256), mGa[:, 2 * i2:2 * i2 + 2, :],
                     ALU.mult, [pgk[i2], mGa], [GkM[par]])
            pgb = [pr.next(), pr.next()]
            for hh in range(4):
                P.mm(pgb[hh // 2][:, (hh % 2) * 256:(hh % 2 + 1) * 256], Bt[:, hh, :], az[:, hh, :], True, True,
                     [Bt, az], [pgb[hh // 2]])
            yield
            for i2 in range(2):
                P.tt("dve", GbM[par][:, 2 * i2:2 * i2 + 2, :], v3(pgb[i2], 2, 256), mGa[:, 2 * i2:2 * i2 + 2, :],
                     ALU.mult, [pgb[i2], mGa], [GbM[par]])
            pn0 = pr.next()
            for hh in range(4):
                P.mm(pn0[:, hh * 128:(hh + 1) * 128], az[:, hh, 0:128], Bt[:, hh, :], True, True, [az, Bt], [pn0])
            yield
            X, N_, Pm, Q = Xk[par], Nk[par], Pk[par], Qk[par]
            P.tt("dve", N_[0][:, :, :], v3(pn0), mN[:, :, :], ALU.mult, [pn0, mN], [N_[0]])
            P.copy("pool", X[0][:, :, :], GbM[par][:, :, 0:128], [GbM[par]], [X[0]])
            P.tt("pool", Pm[0][:, :, :], X[0][:, :, :], identB4[:, :, :], ALU.add, [X[0], identB4], [Pm[0]])
            P.tt("pool", Q[0][:, :, :], N_[0][:, :, :], identB4[:, :, :], ALU.add, [N_[0], identB4], [Q[0]])
            yield
            NLEV = 6
            for lv in range(NLEV):
                a, b = lv % 2, (lv + 1) % 2
                last = lv == NLEV - 1
                px = pr.next()
                for hh in range(4):
                    P.mm(px[:, hh * 128:(hh + 1) * 128], N_[a][:, hh, :], X[a][:, hh, :], True, True,
                         [N_[a], X[a]], [px])
                if not last:
                    pn = pr.next()
                    for hh in range(4):
                        P.mm(pn[:, hh * 128:(hh + 1) * 128], X[a][:, hh, :], N_[a][:, hh, :], True, True,
                             [N_[a], X[a]], [pn])
                yield
                P.copy("act", X[b][:, :, :], v3(px), [px], [X[b]])
                if not last:
                    P.copy("dve", N_[b][:, :, :], v3(pn), [pn], [N_[b]])
                yield
                pp_ = pr.next()
                for hh in range(4):
                    P.mm(pp_[:, hh * 128:(hh + 1) * 128], Q[a][:, hh, :], X[b][:, hh, :], True, True,
                         [Q[a], X[b]], [pp_])
                if not last:
                    pq_ = pr.next()
                    for hh in range(4):
                        P.mm(pq_[:, hh * 128:(hh + 1) * 128], Pm[a][:, hh, :], N_[b][:, hh, :], True, True,
                             [Pm[a], N_[b]], [pq_])
                yield
                P.tt("dve", Pm[b][:, :, :], v3(pp_), Pm[a][:, :, :], ALU.add, [pp_, Pm[a]], [Pm[b]])
                if not last:
                    P.tt("dve", Q[b][:, :, :], v3(pq_), Q[a][:, :, :], ALU.add, [pq_, Q[a]], [Q[b]])
                yield
            TT[par] = Pm[NLEV % 2]

        gens = [intra(0, psrA), intra(1, psrB)]
        alive = [True, True]
        while any(alive):
            for gi in range(2):
                if alive[gi]:
                    try:
                        next(gens[gi])
                    except StopIteration:
                        alive[gi] = False
        if C.get('odd_stop', 99) <= 5:
            continue
        prh = psr.next()
        for hh in range(4):
            for par in range(2):
                col = hh * 128 + par * 64
                P.mm(prh[:, col:col + 64], ARz[par][:, hh, 0:128], Sb[:, hh, :], True, False,
                     [ARz[par], Sb], [prh])
                P.mm(prh[:, col:col + 64], GkM[par][:, hh, 0:128], Vtb[:, col:col + 64], False, True,
                     [GkM[par], Vtb], [prh])
        P.copy("act", RHSb[:, :], prh[:, 0:512], [prh], [RHSb])
        pu = psr.next()
        for hh in range(4):
            for par in range(2):
                col = hh * 128 + par * 64
                P.mm(pu[:, col:col + 64], TT[par][:, hh, :], RHSb[:, col:col + 64], True, True,
                     [TT[par], RHSb], [pu])
        P.copy("dve", Ub[:, :], pu[:, 0:512], [pu], [Ub])
        py = psr.next()
        for hh in range(4):
            for par in range(2):
                col = hh * 128 + par * 64
                P.mm(py[:, col:col + 64], ARz[par][:, hh, 128:256], Sb[:, hh, :], True, False,
                     [ARz[par], Sb], [py])
                P.mm(py[:, col:col + 64], GkM[par][:, hh, 128:256], Vtb[:, col:col + 64], False, False,
                     [GkM[par], Vtb], [py])
                P.mm(py[:, col:col + 64], GbM[par][:, hh, 128:256], Ub[:, col:col + 64], False, True,
                     [GbM[par], Ub], [py])
        for hh in range(4):
            P.mm(psS[:, hh * 128:(hh + 1) * 128], Kht[:, hh * 128:(hh + 1) * 128], Vtb[:, hh * 128:(hh + 1) * 128],
                 True, False, [Kht, Vtb], [psS])
            P.mm(psS[:, hh * 128:(hh + 1) * 128], Bht[:, hh * 128:(hh + 1) * 128], Ub[:, hh * 128:(hh + 1) * 128],
                 False, True, [Bht, Ub], [psS])
        P.tt("pool", Ssf[:, :, :], Ssf[:, :, :], gC[:, :].unsqueeze(2).to_broadcast([128, 4, 64]), ALU.mult,
             [Ssf, gC], [Ssf])
        P.tt("dve", Ssf[0:64, :, :], Ssf[0:64, :, :], v3(psS)[0:64, :, 0:64], ALU.add, [Ssf, psS], [Ssf])
        P.tt("dve", Ssf[64:128, :, :], Ssf[64:128, :, :], v3(psS)[64:128, :, 64:128], ALU.add, [Ssf, psS], [Ssf])
        P.copy("pool", Sb[:, :, :], Ssf[:, :, :], [Ssf], [Sb])
        if C.get('odd_stop', 99) <= 6:
            continue
        P.copy("act", ysb[:, :], py[:, 0:512], [py], [ysb])
        P.act(ysq[:, :], py[:, 0:512], AF.Square, [py], [ysq])
        y3 = ysb[:, :].rearrange("p (h i) -> p h i", h=8)
        P.op("dve", lambda e, y3=y3: e.tensor_reduce(st[:, 0, :], y3, AX.X, ALU.add), reads=[ysb], writes=[(st, 0)])
        P.op("dve", lambda e: e.tensor_reduce(st[:, 1, :], ysq[:, :].rearrange("p (h i) -> p h i", h=8), AX.X, ALU.add),
             reads=[ysq], writes=[(st, 1)])
        P.ts("dve", st[:, 0, :], st[:, 0, :], 1.0 / 64, None, ALU.mult, None, [(st, 0)], [(st, 0)])
        P.tt("dve", st[:, 3, :], st[:, 0, :], st[:, 0, :], ALU.mult, [(st, 0)], [(st, 3)])
        P.stt("dve", st[:, 2, :], st[:, 1, :], 1.0 / 64, st[:, 3, :], ALU.mult, ALU.subtract,
              [(st, 1), (st, 3)], [(st, 2)])
        P.ts("dve", st[:, 2, :], st[:, 2, :], GN_EPS, None, ALU.add, None, [(st, 2)], [(st, 2)])
        P.act(st[:, 2, :], st[:, 2, :], AF.Sqrt, [(st, 2)], [(st, 2)])
        P.op("dve", lambda e: e.reciprocal(st[:, 2, :], st[:, 2, :]), reads=[(st, 2)], writes=[(st, 2)])
        P.tt("dve", y3, y3, st[:, 0, :].unsqueeze(2).to_broadcast([128, 8, 64]), ALU.subtract, [ysb, (st, 0)], [ysb])
        P.tt("dve", y3, y3, st[:, 2, :].unsqueeze(2).to_broadcast([128, 8, 64]), ALU.mult, [ysb, (st, 2)], [ysb])
        P.tt("pool", ysb[:, :], ysb[:, :], lnxg[:, :], ALU.mult, [ysb, lnxg], [ysb])
        P.tt("pool", ysb[:, :], ysb[:, :], lnxb[:, :], ALU.add, [ysb, lnxb], [ysb])
        P.tt("pool", ysq[:, :].rearrange("p (h i) -> p h i", h=8), Vtf[:, :].rearrange("p (h i) -> p h i", h=8),
             bon[:, :].unsqueeze(2).to_broadcast([128, 8, 64]), ALU.mult, [Vtf, bon, ysq], [ysq])
        P.tt("pool", ysb[:, :], ysb[:, :], ysq[:, :], ALU.add, [ysb, ysq], [ysb])
        P.tt("dve", yob[:, :], ysb[:, :], gate_sb[:, :], ALU.mult, [ysb, gate_sb], [yob])
        for c in range(4):
            P.tr(psv[:, c * 128:(c + 1) * 128], yob[:, c * 128:(c + 1) * 128], identB[:, :], [yob, identB], [psT])
        P.copy("act", yT[:, :, :], psv[:, 0:512].rearrange("p (c t) -> p c t", c=4), [psT], [yT])
        P.dma("sp", mixT[0:512, t0:t0 + 128].rearrange("(c p) t -> p c t", p=128), yT[:, :, :], reads=[yT])
        if C.get('odd_stop', 99) <= 7:
            continue
        W = PADU + 128
        a_, b_ = ptmp
        P.tt("pool", a_[:, :, 1:W], ubuf[:, :, 1:W], ubuf[:, :, 0:W - 1], ALU.add, [ubuf, (ubuf, "h")], [a_])
        P.tt("pool", b_[:, 1:4, 3:W], a_[:, 1:4, 3:W], a_[:, 1:4, 1:W - 2], ALU.add, [a_], [b_])
        P.tt("pool", a_[:, 2:4, 7:W], b_[:, 2:4, 7:W], b_[:, 2:4, 3:W - 4], ALU.add, [b_, a_], [(a_, 1)])
        P.tt("pool", b_[:, 3:4, 15:W], a_[:, 3:4, 15:W], a_[:, 3:4, 7:W - 8], ALU.add, [a_, (a_, 1), b_], [(b_, 1)])
        srcs = [a_, b_, a_, b_]
        for g in range(4):
            win = 2 << g
            src = srcs[g]
            deps = [a_, b_, (a_, 1), (b_, 1), ubuf]
            ic = icnt if blk == 0 else cwin
            P.tt("dve", src[:, g, PADU:W], src[:, g, PADU:W], ic[:, g, :], ALU.mult, deps + [ic], [(src, "f%d" % g)])
            P.tt("dve", pooled[:, g, :], src[:, g, PADU:W], ubuf[:, g, PADU:W], ALU.subtract,
                 deps + [(src, "f%d" % g)], [(pooled, g)])
        P.copy("pool", ubuf[:, :, 0:PADU], ubuf[:, :, 128:128 + PADU], [ubuf, (pooled, 0), (pooled, 1), (pooled, 2), (pooled, 3)],
               [(ubuf, "h")])
        ppl = psr.next()
        for g in range(4):
            P.mm(ppl[:, g * 128:(g + 1) * 128], wp[:, g, :], pooled[:, g, :], True, True, [wp, (pooled, g)], [ppl])
        for g in range(4):
            P.act(pout[:, g, :], ppl[:, g * 128:(g + 1) * 128], AF.Identity, [ppl, prm, pbs], [pout],
                  scale=prm[:, 38 + g:39 + g], bias=pbs[:, g:g + 1])
        P.dma("sp", mixT[512:1024, t0:t0 + 128].rearrange("(c p) t -> p c t", p=128), pout[:, :, :], reads=[pout])


def host_consts_odd():
    k = np.arange(128)
    strict = (k[:, None] < k[None, :]).astype(np.float32)
    incl = (k[:, None] <= k[None, :]).astype(np.float32)
    mGa = np.tile(np.concatenate([strict, incl], axis=1), (1, 4))
    mN = np.tile((k[None, :] < k[:, None]).astype(np.float32), (1, 4))
    bd = (k[:, None] // 64 == k[None, :] // 64).astype(np.float32)
    hsel = np.stack([(k < 64), (k >= 64)], axis=1).astype(np.float32)
    t = np.arange(128)
    icnt = np.concatenate([np.tile(1.0 / np.minimum(t + 1, 2 << g)[None, :], (128, 1)) for g in range(4)],
                          axis=1).astype(np.float32)
    return {"mGa": np.ascontiguousarray(mGa), "mN": np.ascontiguousarray(mN), "triu": incl, "bd": bd,
            "hsel": hsel, "icnt": np.ascontiguousarray(icnt)}


def host_odd_params(inp, i):
    def fm(v):
        return np.asarray(v, np.float32).reshape(4, 128).T
    prm1 = np.concatenate([
        np.asarray(inp["o_mu"][i], np.float32).reshape(14, 128).T,
        fm(inp["o_w0"][i]), fm(inp["o_a0"][i]), fm(inp["o_k_k"][i].reshape(512)),
        fm(inp["o_k_a"][i].reshape(512)), fm(inp["o_r_k"][i].reshape(512)),
        fm(inp["o_b_pool"][i].reshape(512)), fm(inp["o_pool_scale"][i])], axis=1)
    return {
        "w_in": np.ascontiguousarray(inp["o_w_in"][i], np.float32),
        "wa2": np.ascontiguousarray(np.concatenate([inp["o_w2"][i], inp["o_a2"][i]], axis=0), np.float32),
        "g2": np.ascontiguousarray(inp["o_g2"][i], np.float32),
        "wpool": np.ascontiguousarray(np.transpose(inp["o_w_pool"][i], (1, 0, 2)), np.float32),
        "prm1": np.ascontiguousarray(prm1, np.float32),
        "w0row": np.ascontiguousarray(inp["o_w0"][i], np.float32),
        "lnxg": np.ascontiguousarray(inp["o_lnx_g"][i].reshape(512), np.float32),
        "lnxb": np.ascontiguousarray(inp["o_lnx_b"][i].reshape(512), np.float32),
    }


def host_even_params(inp, i):
    return {
        "e_w_in": np.ascontiguousarray(inp["e_w_in"][i], np.float32),
        "e_w_out": np.ascontiguousarray(inp["e_w_out"][i], np.float32),
        "e_dwT": np.ascontiguousarray(np.asarray(inp["e_w_dw"][i], np.float32).T.reshape(4, 128, 31).transpose(1, 0, 2)),
        "e_prm": np.ascontiguousarray(np.stack([inp["e_b_dw"][i], inp["e_conv_g"][i], inp["e_conv_b"][i]], -1)
                                      .astype(np.float32).reshape(4, 128, 3).transpose(1, 0, 2)),
    }


def build_program(S, shapes):
    NBLK = S // 128
    nc = bass.Bass("TRN2", target_bir_lowering=False)
    I = {k: nc.dram_tensor(k, list(shp), F32, kind="ExternalInput").ap() for k, shp in shapes.items()}
    out = nc.dram_tensor("out", [S, D_MODEL], F32, kind="ExternalOutput").ap()

    def scratch(name, shape, dt):
        return nc.dram_tensor(name, shape, dt, kind="Internal").ap()

    xa = scratch("s_xa", [S, D_MODEL], F32)
    xb = scratch("s_xb", [S, D_MODEL], F32)
    hcT = scratch("s_hcT", [512, S], BF16)
    qT = scratch("s_qT", [512, S], BF16)
    kT = scratch("s_kT", [512, S], BF16)
    vtm = scratch("s_vtm", [S, 512], BF16)
    mixT = scratch("s_mixT", [1024, S], BF16)
    es = ExitStack()
    P = Prog(nc, es)
    C = setup_consts(P, {k[2:]: v for k, v in I.items() if k.startswith("c_")})
    g, b = I["ln_g"], I["ln_b"]
    fi, fo = I["ffn_in"], I["ffn_out"]
    phase_ffn(P, C, I["x"], 0, xa, 0, NBLK, fi[0, 0], fo[0, 0], g[0, 0], b[0, 0])
    phase_inproj_even(P, C, xa, NBLK, I["e_w_in"], hcT, qT, kT, vtm)
    phase_conv(P, C, hcT, S, I["e_dwT"], I["e_prm"], mixT)
    phase_attn(P, C, qT, kT, vtm, S, mixT, 512)
    phase_outproj(P, C, mixT, xa, xb, NBLK, I["e_w_out"], g[0, 1], b[0, 1])
    phase_ffn(P, C, xb, 0, xa, 0, NBLK, fi[0, 1], fo[0, 1], g[0, 2], b[0, 2])
    phase_ffn(P, C, xa, 0, xb, 0, NBLK, fi[1, 0], fo[1, 0], g[1, 0], b[1, 0])
    D = {k[2:]: v for k, v in I.items() if k.startswith("o_") or k.startswith("k_")}
    phase_odd_mixer(P, C, xb, NBLK, D, mixT)
    phase_outproj(P, C, mixT, xb, xa, NBLK, I["o_w_out"], g[1, 1], b[1, 1])
    phase_ffn(P, C, xa, 0, out, 0, NBLK, fi[1, 1], fo[1, 1], g[1, 2], b[1, 2])
    P.finish()
    es.close()
    return nc


ACTIVE_CORES = (0, 1, 4, 5)


def kernel(**inputs):
    inp = {k: np.asarray(v) for k, v in inputs.items()}
    x = np.asarray(inp["x"], np.float32)
    B, S, _ = x.shape
    shared = {
        "ffn_in": np.ascontiguousarray(inp["ffn_in"], np.float32),
        "ffn_out": np.ascontiguousarray(inp["ffn_out"], np.float32),
        "ln_g": np.ascontiguousarray(inp["ln_g"], np.float32),
        "ln_b": np.ascontiguousarray(inp["ln_b"], np.float32),
        "o_w_out": np.ascontiguousarray(inp["o_w_out"][0], np.float32),
    }
    shared.update(host_even_params(inp, 0))
    for k, v in host_odd_params(inp, 0).items():
        shared["o_" + k] = v
    for k, v in host_consts_odd().items():
        shared["k_" + k] = v
    for k, v in host_consts().items():
        shared["c_" + k] = v
    n_cores = 8
    assert B <= len(ACTIVE_CORES)
    in_maps = []
    zero_x = np.zeros((S, D_MODEL), np.float32)
    for c in range(n_cores):
        m = dict(shared)
        if c in ACTIVE_CORES and ACTIVE_CORES.index(c) < B:
            m["x"] = np.ascontiguousarray(x[ACTIVE_CORES.index(c)])
        else:
            m["x"] = zero_x
        in_maps.append(m)
    shapes = {k: v.shape for k, v in in_maps[0].items()}
    nc = build_program(S, shapes)
    res = run_bass_kernel_spmd(nc, in_maps, core_ids=list(range(n_cores)))
    out = np.stack([np.asarray(res.results[ACTIVE_CORES[bi]]["out"], np.float32) for bi in range(B)], axis=0)
    return out
```

```python
import numpy as np
from contextlib import ExitStack
import concourse.bass as bass
import concourse.mybir as mybir
from concourse.bass_utils import run_bass_kernel_spmd

F32 = mybir.dt.float32
BF16 = mybir.dt.bfloat16
AF = mybir.ActivationFunctionType
ALU = mybir.AluOpType
AX = mybir.AxisListType

D_MODEL = 1024
D_FF = 2816
DEPTH = 2
DN_ALPHA = (2.0 * DEPTH) ** 0.25
LN_EPS = 1e-5
ARENA_BYTES = 207 * 1024
SQ_DVE = "pool"


class Op:
    __slots__ = ("eng", "fn", "idx", "deps", "dma", "dslot", "dval", "signal",
                 "sig", "waits")

    def __init__(self, eng, fn, idx, dma):
        self.eng = eng
        self.fn = fn
        self.idx = idx
        self.dma = dma
        self.deps = set()
        self.dslot = -1
        self.dval = 0
        self.signal = False
        self.sig = 0
        self.waits = []


class Tile:
    def __init__(self, ap, name=""):
        self.ap = ap
        self.name = name

    def __getitem__(self, k):
        return self.ap[k]


class Prog:
    CENG = ("pe", "act", "dve", "pool")

    def __init__(self, nc, es, n_dma_sems=48):
        self.nc = nc
        self.eng = {"pe": nc.tensor, "act": nc.scalar, "dve": nc.vector,
                    "pool": nc.gpsimd, "sp": nc.sync}
        self.sem = {e: es.enter_context(nc.semaphore("sem_" + e)) for e in self.CENG}
        self.dsem = [es.enter_context(nc.semaphore("dsem%d" % i)) for i in range(n_dma_sems)]
        self.dsem_val = [0] * n_dma_sems
        self.dsem_last = [None] * n_dma_sems
        self.dnext = 0
        self.ops = []
        self.last_w = {}
        self.readers = {}
        self.count = {e: 0 for e in self.eng}
        self.last_op = {e: None for e in self.eng}
        self.arena = es.enter_context(nc.sbuf_tensor("arena", [128, ARENA_BYTES // 4], F32))
        self.cur = 0
        self.psum = [es.enter_context(nc.psum_tensor("ps%d" % i, [128, 512], F32)) for i in range(8)]
        self.ps_tiles = [Tile(p, "ps%d" % i) for i, p in enumerate(self.psum)]

    def alloc(self, free_shape, dtype, name=""):
        n = 1
        for s in free_shape:
            n *= s
        esz = 4 if dtype == F32 else 2
        nbytes = (n * esz + 63) // 64 * 64
        off = self.cur
        self.cur += nbytes
        assert self.cur <= ARENA_BYTES, "arena overflow %s: %d" % (name, self.cur)
        ap = self.arena[:, off // 4:(off + nbytes) // 4]
        if dtype != F32:
            ap = ap.bitcast(dtype)
        ap = ap[:, 0:n]
        if len(free_shape) == 2:
            ap = ap.rearrange("p (a b) -> p a b", a=free_shape[0])
        elif len(free_shape) == 3:
            ap = ap.rearrange("p (a b c) -> p a b c", a=free_shape[0], b=free_shape[1])
        return Tile(ap, name)

    def arena_reset(self, to=0):
        self.cur = to

    def op(self, eng, fn, reads=(), writes=(), dma=False):
        o = Op(eng, fn, self.count[eng], dma)
        self.count[eng] += 1
        deps = o.deps
        for t in reads:
            w = self.last_w.get(t)
            if w is not None:
                deps.add(w)
        for t in writes:
            w = self.last_w.get(t)
            if w is not None:
                deps.add(w)
            rd = self.readers.get(t)
            if rd:
                deps.update(rd.values())
        if dma:
            slot = self.dnext
            self.dnext = (self.dnext + 1) % len(self.dsem)
            prev = self.dsem_last[slot]
            if prev is not None:
                deps.add(prev)
            self.dsem_val[slot] += 16
            o.dslot = slot
            o.dval = self.dsem_val[slot]
            self.dsem_last[slot] = o
        for t in reads:
            key = ("d", id(o)) if dma else eng
            self.readers.setdefault(t, {})[key] = o
        for t in writes:
            self.last_w[t] = o
            self.readers[t] = {}
        self.ops.append(o)
        self.last_op[eng] = o
        return o

    def dma(self, eng, out, in_, reads=(), writes=()):
        return self.op(eng, lambda e: e.dma_start(out=out, in_=in_), reads, writes, dma=True)

    def mm(self, out, lhsT, rhs, start, stop, reads, writes):
        return self.op("pe", lambda e: e.matmul(out, lhsT, rhs, start=start, stop=stop), reads, writes)

    def tr(self, out, in_, ident, reads, writes):
        return self.op("pe", lambda e: e.transpose(out, in_, ident), reads, writes)

    def act(self, out, in_, func, reads, writes, bias=None, scale=None, eng="act", accum_out=None):
        kw = {}
        if bias is not None:
            kw["bias"] = bias
        if scale is not None:
            kw["scale"] = scale
        if accum_out is not None:
            kw["accum_out"] = accum_out
        return self.op(eng, lambda e: e.activation(out=out, in_=in_, func=func, **kw), reads, writes)

    def tt(self, eng, out, in0, in1, op, reads, writes):
        return self.op(eng, lambda e: e.tensor_tensor(out, in0, in1, op), reads, writes)

    def stt(self, eng, out, in0, scalar, in1, op0, op1, reads, writes):
        return self.op(eng, lambda e: e.scalar_tensor_tensor(out, in0, scalar, in1, op0, op1), reads, writes)

    def ts(self, eng, out, in0, s1, s2, op0, op1, reads, writes):
        if s2 is None:
            return self.op(eng, lambda e: e.tensor_scalar(out, in0, s1, None, op0), reads, writes)
        return self.op(eng, lambda e: e.tensor_scalar(out, in0, s1, s2, op0, op1), reads, writes)

    def copy(self, eng, out, in_, reads, writes):
        if eng == "act":
            return self.op(eng, lambda e: e.activation(out=out, in_=in_, func=AF.Copy), reads, writes)
        return self.op(eng, lambda e: e.tensor_copy(out, in_), reads, writes)

    def barrier(self):
        deps = set(o for o in self.last_op.values() if o is not None)
        deps.update(o for o in self.dsem_last if o is not None)
        for e in self.eng:
            o = Op(e, None, self.count[e], False)
            self.count[e] += 1
            o.deps = set(deps)
            self.ops.append(o)

    def finish(self):
        deps = set(o for o in self.last_op.values() if o is not None)
        deps.update(o for o in self.dsem_last if o is not None)
        o = Op("sp", None, self.count["sp"], False)
        o.deps = deps
        self.ops.append(o)
        self.emit()

    def emit(self):
        waited = {e: {} for e in self.eng}
        for o in self.ops:
            need = {}
            for d in o.deps:
                if d is o:
                    continue
                if d.dma:
                    key, val = ("d", d.dslot), d.dval
                else:
                    if d.eng == "pe" and o.eng == "pe":
                        continue
                    key, val = d.eng, d.idx
                if waited[o.eng].get(key, -1) >= val:
                    continue
                if key not in need or need[key][0] < val:
                    need[key] = (val, d)
            for key, (val, d) in need.items():
                waited[o.eng][key] = val
                if not d.dma:
                    d.signal = True
                o.waits.append(d)
        signum = {e: 0 for e in self.eng}
        n_inst = 0
        for o in self.ops:
            e = self.eng[o.eng]
            if o.signal and not o.dma:
                signum[o.eng] += 1
                o.sig = signum[o.eng]
            for d in o.waits:
                if d.dma:
                    e.wait_ge(self.dsem[d.dslot], d.dval)
                else:
                    e.wait_ge(self.sem[d.eng], d.sig)
                n_inst += 1
            if o.fn is None:
                continue
            inst = o.fn(e)
            n_inst += 1
            if o.dma:
                inst.then_inc(self.dsem[o.dslot], 16)
            elif o.signal:
                inst.then_inc(self.sem[o.eng], 1)
        self.n_inst = n_inst
        self.n_sig = dict(signum)
        self.ops = []


def load_w_bf16(P, dst, src_ap, kchunks, ncols, split=1):
    v = src_ap.rearrange("(k p) n -> p k n", p=128)
    step = max(1, kchunks // split)
    for k0 in range(0, kchunks, step):
        k1 = min(kchunks, k0 + step)
        P.dma("pool", dst[:, k0:k1, :], v[:, k0:k1, :], writes=[(dst, k) for k in range(k0, k1)])


def load_bcast(P, dst, vec_ap, n):
    P.dma("sp", dst[:, 0:n], vec_ap.partition_broadcast(128), writes=[dst])


def transpose_block(P, C, x_f32, xbf, xT, tcol, ps_t, nchunks=8, cast_eng="pool"):
    P.copy(cast_eng, xbf[:, 0:nchunks * 128], x_f32[:, 0:nchunks * 128], [x_f32], [xbf])
    psv = ps_t.ap.bitcast(BF16)
    for c in range(nchunks):
        P.tr(psv[:, c * 128:(c + 1) * 128], xbf[:, c * 128:(c + 1) * 128], C["ident"][:, :],
             [xbf, C["ident"]], [ps_t])
    P.copy("dve", xT[:, 0:nchunks, tcol:tcol + 128],
           psv[:, 0:nchunks * 128].rearrange("p (c t) -> p c t", c=nchunks), [ps_t], [xT])


def layer_norm_block(P, r, out, gbc, bbc, stats, mv, eps, D=1024):
    nch = D // 512
    for h in range(nch):
        P.op("dve", lambda e, h=h: e.bn_stats(stats[:, h, :], r[:, h * 512:(h + 1) * 512]),
             reads=[r], writes=[stats])
    P.op("dve", lambda e: e.bn_aggr(mv[:, 0:2], stats[:, 0:nch, :]), reads=[stats], writes=[mv])
    P.ts("dve", mv[:, 2:3], mv[:, 1:2], eps, None, ALU.add, None, [mv], [mv])
    P.act(mv[:, 2:3], mv[:, 2:3], AF.Sqrt, [mv], [mv])
    P.op("dve", lambda e: e.reciprocal(mv[:, 2:3], mv[:, 2:3]), reads=[mv], writes=[mv])
    P.stt("dve", mv[:, 3:4], mv[:, 0:1], -1.0, mv[:, 2:3], ALU.mult, ALU.mult, [mv], [mv])
    P.act(r[:, 0:D], r[:, 0:D], AF.Identity, [r, mv], [r], bias=mv[:, 3:4], scale=mv[:, 2:3])
    P.tt("pool", r[:, 0:D], r[:, 0:D], gbc[:, 0:D], ALU.mult, [r, gbc], [r])
    P.tt("pool", out[:, 0:D], r[:, 0:D], bbc[:, 0:D], ALU.add, [r, bbc], [out])


def tiles_of(nblocks, per):
    out = []
    b = 0
    while b < nblocks:
        n = min(per, nblocks - b)
        out.append((b, n))
        b += n
    return out


def phase_ffn(P, C, x_in, in_blk0, x_out, out_blk0, nblocks, w_in, w_out, ln_g, ln_b):
    P.barrier()
    P.arena_reset(C["arena_base"])
    NF = D_FF // 128
    Win = P.alloc([8, 2 * D_FF], BF16, "Win")
    Wout = P.alloc([NF, D_MODEL], BF16, "Wout")
    gbc = P.alloc([D_MODEL], F32, "gbc")
    bbc = P.alloc([D_MODEL], F32, "bbc")
    TB = C.get('TB', 4)
    xin1 = P.alloc([D_MODEL], F32, "xin")
    xbf1 = P.alloc([D_MODEL], BF16, "xbf")
    xres1 = P.alloc([D_MODEL], F32, "xres")
    xin = [xin1, xin1]
    xbf = [xbf1, xbf1]
    xres = [xres1, xres1]
    xT = [P.alloc([8, TB * 128], BF16, "xT%d" % i) for i in range(2)]
    gT = P.alloc([NF, TB * 128], BF16, "gT")
    sg = [P.alloc([TB * 128], F32, "sg%d" % i) for i in range(2)]
    rr = [P.alloc([D_MODEL], F32, "r%d" % i) for i in range(3)]
    stats = [P.alloc([2, 6], F32, "st%d" % i) for i in range(3)]
    mv = [P.alloc([4], F32, "mv%d" % i) for i in range(3)]

    load_w_bf16(P, Win, w_in, 8, 2 * D_FF, split=8)
    load_w_bf16(P, Wout, w_out, NF, D_MODEL, split=2)
    load_bcast(P, gbc, ln_g, D_MODEL)
    load_bcast(P, bbc, ln_b, D_MODEL)

    psG = [P.ps_tiles[0], P.ps_tiles[1]]
    psU = [P.ps_tiles[2], P.ps_tiles[3]]
    psY = [P.ps_tiles[4], P.ps_tiles[5]]
    psT = P.ps_tiles[6]
    cres = 0.5 / DN_ALPHA
    eps = LN_EPS / (DN_ALPHA * DN_ALPHA)

    tl = tiles_of(nblocks, TB)
    psYr = PsRot(P, [4, 5, 7])
    psv = psT.ap.bitcast(BF16)

    def pro_load(ti, j):
        b0, nb = tl[ti]
        g = b0 + j
        s = g % 2
        P.dma("sp", xin[s][:, :], x_in[(in_blk0 + g) * 128:(in_blk0 + g + 1) * 128, :], writes=[xin[s]])
        P.copy("pool", xbf[s][:, :], xin[s][:, :], [xin[s]], [xbf[s]])

    def pro_tr(ti, j):
        b0, nb = tl[ti]
        s = (b0 + j) % 2
        xt = xT[ti % 2]
        for c in range(8):
            P.tr(psv[:, c * 128:(c + 1) * 128], xbf[s][:, c * 128:(c + 1) * 128], C["ident"][:, :],
                 [xbf[s], C["ident"]], [psT])
        P.copy("dve", xt[:, 0:8, j * 128:(j + 1) * 128],
               psv[:, 0:1024].rearrange("p (c t) -> p c t", c=8), [psT], [xt])

    def epilogue_ln(g):
        s3 = g % 3
        layer_norm_block(P, rr[s3], rr[s3], gbc, bbc, stats[s3], mv[s3], eps)
        P.dma("pool", x_out[(out_blk0 + g) * 128:(out_blk0 + g + 1) * 128, :], rr[s3][:, :], reads=[rr[s3]])

    grp = 0
    for j in range(tl[0][1]):
        pro_load(0, j)
        pro_tr(0, j)
    pending_ln = None
    for ti, (b0, nb) in enumerate(tl):
        NT = nb * 128
        xt = xT[ti % 2]
        nxt = tl[ti + 1][1] if ti + 1 < len(tl) else 0
        for fp in range(NF):
            pg = psG[grp % 2]
            pu = psU[grp % 2]
            sgt = sg[grp % 2]
            grp += 1
            for kc in range(8):
                P.mm(pg[:, 0:NT], Win[:, kc, fp * 128:(fp + 1) * 128], xt[:, kc, 0:NT],
                     kc == 0, kc == 7, [(Win, kc), xt], [pg])
            for kc in range(8):
                P.mm(pu[:, 0:NT], Win[:, kc, D_FF + fp * 128:D_FF + (fp + 1) * 128], xt[:, kc, 0:NT],
                     kc == 0, kc == 7, [(Win, kc), xt], [pu])
            P.act(sgt[:, 0:NT], pg[:, 0:NT], AF.Silu, [pg], [sgt])
            P.tt("dve", gT[:, fp, 0:NT], sgt[:, 0:NT], pu[:, 0:NT], ALU.mult, [sgt, pu], [(gT, fp)])
            if fp >= 1 and (fp - 1) % 5 == 0 and (fp - 1) // 5 < nxt:
                pro_load(ti + 1, (fp - 1) // 5)
            if fp >= 4 and (fp - 4) % 5 == 0 and (fp - 4) // 5 < nxt:
                pro_tr(ti + 1, (fp - 4) // 5)
        for j in range(nb):
            g = b0 + j
            s = g % 2
            P.dma("sp", xres[s][:, :], x_in[(in_blk0 + g) * 128:(in_blk0 + g + 1) * 128, :],
                  writes=[xres[s]])
            for half in range(2):
                py = psYr.next()
                for fc in range(NF):
                    P.mm(py[:, 0:512], gT[:, fc, j * 128:(j + 1) * 128],
                         Wout[:, fc, half * 512:(half + 1) * 512], fc == 0, fc == NF - 1,
                         [(gT, fc), (Wout, fc)], [py])
                P.stt("dve", rr[g % 3][:, half * 512:(half + 1) * 512], py[:, 0:512], cres,
                      xres[s][:, half * 512:(half + 1) * 512], ALU.mult, ALU.add,
                      [py, xres[s]], [rr[g % 3]])
            if pending_ln is not None:
                epilogue_ln(pending_ln)
            pending_ln = g
    if pending_ln is not None:
        epilogue_ln(pending_ln)


def setup_consts(P, consts_dram):
    C = {}
    ident = P.alloc([128], BF16, "ident")
    P.dma("pool", ident[:, :], consts_dram["ident"][:, :], writes=[ident])
    C["ident"] = ident
    onesF = P.alloc([128], F32, "onesF")
    P.op("pool", lambda e: e.memset(onesF[:, :], 1.0), writes=[onesF])
    C["onesF"] = onesF
    onesB = P.alloc([128], BF16, "onesB")
    P.op("pool", lambda e: e.memset(onesB[:, :], 1.0), writes=[onesB])
    C["onesB"] = onesB
    trim = P.alloc([128], BF16, "trim")
    P.dma("pool", trim[:, :], consts_dram["trim"][:, :], writes=[trim])
    C["trim"] = trim
    C["dmask_dram"] = consts_dram["dmask"]
    C["arena_base"] = P.cur
    return C


def host_consts():
    k = np.arange(128)
    trim = (k[:, None] >= k[None, :]).astype(np.float32)
    dm = (k[:, None] < k[None, :]).astype(np.float32)
    return {"ident": np.eye(128, dtype=np.float32), "trim": trim,
            "dmask": np.ascontiguousarray(np.tile(dm, (1, 4)))}


class PsRot:
    def __init__(self, P, banks):
        self.t = [P.ps_tiles[b] for b in banks]
        self.i = 0

    def next(self):
        t = self.t[self.i % len(self.t)]
        self.i += 1
        return t


def load_xT_tile(P, C, x_dram, blk0, nb, xin, xbf, xt, psT, ctr):
    for j in range(nb):
        s = ctr[0] % 2
        ctr[0] += 1
        P.dma("sp", xin[s][:, :], x_dram[(blk0 + j) * 128:(blk0 + j + 1) * 128, :], writes=[xin[s]])
        transpose_block(P, C, xin[s], xbf[s], xt, j * 128, psT)


def phase_inproj_even(P, C, x_in, nblocks, w_in, hcT, qT, kT, vtm):
    P.barrier()
    P.arena_reset(C["arena_base"])
    NCOL = 2560
    We = P.alloc([8, NCOL], BF16, "We")
    wv = w_in.rearrange("(k p) n -> p k n", p=128)
    for k0 in range(0, 8, 2):
        P.dma("pool", We[:, k0:k0 + 2, 0:2048], wv[:, k0:k0 + 2, 0:2048],
              writes=[(We, k) for k in range(k0, k0 + 2)])
    wvv = wv[:, :, 2048:2560].rearrange("p k (h two d) -> p k two h d", two=2, d=64)
    for par in range(2):
        for k in range(8):
            P.dma("pool", We[:, k, 2048 + par * 256:2048 + (par + 1) * 256].rearrange(
                "p (h d) -> p h d", d=64), wvv[:, k, par], writes=[(We, ("v", par, k))])
    TB = 4
    xin = [P.alloc([D_MODEL], F32, "xin%d" % i) for i in range(2)]
    xbf = [P.alloc([D_MODEL], BF16, "xbf%d" % i) for i in range(2)]
    xT = [P.alloc([8, TB * 128], BF16, "xT%d" % i) for i in range(2)]
    sgm = [P.alloc([TB * 128], F32, "sgm%d" % i) for i in range(2)]
    ob = [P.alloc([TB * 128], BF16, "ob%d" % i) for i in range(4)]
    psr = PsRot(P, [0, 1, 2, 3, 4, 5])
    psT = P.ps_tiles[6]
    ctr = [0]
    oi = 0
    tl_ = tiles_of(nblocks, TB)
    load_xT_tile(P, C, x_in, tl_[0][0], tl_[0][1], xin, xbf, xT[0], psT, ctr)
    for ti, (b0, nb) in enumerate(tl_):
        NT = nb * 128
        t0 = b0 * 128
        xt = xT[ti % 2]
        for c in range(4):
            pa = psr.next()
            pg = psr.next()
            for kc in range(8):
                P.mm(pa[:, 0:NT], We[:, kc, c * 128:(c + 1) * 128], xt[:, kc, 0:NT],
                     kc == 0, kc == 7, [(We, kc), xt], [pa])
            for kc in range(8):
                P.mm(pg[:, 0:NT], We[:, kc, 512 + c * 128:512 + (c + 1) * 128], xt[:, kc, 0:NT],
                     kc == 0, kc == 7, [(We, kc), xt], [pg])
            sg = sgm[c % 2]
            P.act(sg[:, 0:NT], pg[:, 0:NT], AF.Sigmoid, [pg], [sg])
            o = ob[oi % 4]
            oi += 1
            P.tt("dve", o[:, 0:NT], sg[:, 0:NT], pa[:, 0:NT], ALU.mult, [sg, pa], [o])
            P.dma("sp", hcT[c * 128:(c + 1) * 128, t0:t0 + NT], o[:, 0:NT], reads=[o])
        if ti + 1 < len(tl_):
            load_xT_tile(P, C, x_in, tl_[ti + 1][0], tl_[ti + 1][1], xin, xbf, xT[(ti + 1) % 2], psT, ctr)
        for c in range(8):
            pq = psr.next()
            col = 1024 + c * 128
            for kc in range(8):
                P.mm(pq[:, 0:NT], We[:, kc, col:col + 128], xt[:, kc, 0:NT],
                     kc == 0, kc == 7, [(We, kc), xt], [pq])
            o = ob[oi % 4]
            oi += 1
            if c < 4:
                P.act(o[:, 0:NT], pq[:, 0:NT], AF.Copy, [pq], [o], scale=0.125)
                P.dma("sp", qT[c * 128:(c + 1) * 128, t0:t0 + NT], o[:, 0:NT], reads=[o])
            else:
                P.copy("dve", o[:, 0:NT], pq[:, 0:NT], [pq], [o])
                P.dma("sp", kT[(c - 4) * 128:(c - 3) * 128, t0:t0 + NT], o[:, 0:NT], reads=[o])
        for j in range(nb):
            pv = psr.next()
            for kc in range(8):
                P.mm(pv[:, 0:512], xt[:, kc, j * 128:(j + 1) * 128], We[:, kc, 2048:2560],
                     kc == 0, kc == 7, [(We, ("v", 0, kc)), (We, ("v", 1, kc)), xt], [pv])
            o = ob[oi % 4]
            oi += 1
            if j % 2 == 0:
                P.act(o[:, 0:512], pv[:, 0:512], AF.Copy, [pv], [o])
            else:
                P.copy("dve", o[:, 0:512], pv[:, 0:512], [pv], [o])
            P.dma("sp", vtm[(b0 + j) * 128:(b0 + j + 1) * 128, :], o[:, 0:512], reads=[o])


def phase_conv(P, C, hcT, S, w_dwT, cprm, mixT):
    P.barrier()
    P.arena_reset(C["arena_base"])
    KW = 31
    PAD = KW - 1
    hc = [P.alloc([PAD + S], BF16, "hc%d" % g) for g in range(4)]
    Dg = P.alloc([4, KW, 128], BF16, "Dg")
    wT = P.alloc([4, KW], F32, "wT")
    prm = P.alloc([4, 3], F32, "prm")
    cv = [P.alloc([512], F32, "cv%d" % g) for g in range(4)]
    sq = [P.alloc([512], F32, "sq%d" % g) for g in range(4)]
    mean_sb = P.alloc([512], F32, "mean")
    rstd = P.alloc([512], F32, "rstd")
    tmp = [P.alloc([512], F32, "tmp%d" % i) for i in range(2)]
    ob = [P.alloc([512], BF16, "ob%d" % i) for i in range(2)]
    P.dma("sp", wT[:, :, :], w_dwT[:, :, :], writes=[wT])
    P.dma("sp", prm[:, :, :], cprm[:, :, :], writes=[(prm, 0), (prm, 1), (prm, 2)])
    for g in range(4):
        P.op("pool", lambda e, g=g: e.memset(hc[g][:, 0:PAD], 0.0), writes=[(hc[g], "pad")])
        P.dma("sp", hc[g][:, PAD:PAD + S], hcT[g * 128:(g + 1) * 128, :], writes=[hc[g]])
        for k in range(KW):
            P.ts("dve", Dg[:, g, k, :], C["ident"][:, :], wT[:, g, k:k + 1], None, ALU.mult, None,
                 [C["ident"], wT], [(Dg, g)])
    psr = PsRot(P, [0, 1, 2, 3])
    psM = P.ps_tiles[4]
    psQ = P.ps_tiles[5]
    oi = 0
    for t0 in range(0, S, 512):
        N = min(512, S - t0)
        for g in range(4):
            pc = psr.next()
            for k in range(KW):
                P.mm(pc[:, 0:N], Dg[:, g, k, :], hc[g][:, t0 + k:t0 + k + N], k == 0, k == KW - 1,
                     [(Dg, g), hc[g], (hc[g], "pad")], [pc])
            P.act(cv[g][:, 0:N], pc[:, 0:N], AF.Identity, [pc, (prm, 0)], [cv[g]], bias=prm[:, g, 0:1])
            P.act(sq[g][:, 0:N], pc[:, 0:N], AF.Square, [pc, (prm, 0)], [sq[g]], bias=prm[:, g, 0:1])
        for g in range(4):
            P.mm(psM[:, 0:N], C["onesF"][:, :], cv[g][:, 0:N], g == 0, g == 3, [C["onesF"], cv[g]], [psM])
        for g in range(4):
            P.mm(psQ[:, 0:N], C["onesF"][:, :], sq[g][:, 0:N], g == 0, g == 3, [C["onesF"], sq[g]], [psQ])
        P.act(mean_sb[:, 0:N], psM[:, 0:N], AF.Copy, [psM], [mean_sb], scale=1.0 / 512)
        P.tt("dve", rstd[:, 0:N], mean_sb[:, 0:N], mean_sb[:, 0:N], ALU.mult, [mean_sb], [rstd])
        P.stt("dve", rstd[:, 0:N], psQ[:, 0:N], 1.0 / 512, rstd[:, 0:N], ALU.mult, ALU.subtract,
              [psQ, rstd], [rstd])
        P.ts("dve", rstd[:, 0:N], rstd[:, 0:N], LN_EPS, None, ALU.add, None, [rstd], [rstd])
        P.act(rstd[:, 0:N], rstd[:, 0:N], AF.Sqrt, [rstd], [rstd])
        P.op("dve", lambda e, N=N: e.reciprocal(rstd[:, 0:N], rstd[:, 0:N]), reads=[rstd], writes=[rstd])
        for g in range(4):
            tp = tmp[g % 2]
            eng = "dve" if g % 2 == 0 else "pool"
            P.tt(eng, tp[:, 0:N], cv[g][:, 0:N], mean_sb[:, 0:N], ALU.subtract, [cv[g], mean_sb], [tp])
            P.tt(eng, tp[:, 0:N], tp[:, 0:N], rstd[:, 0:N], ALU.mult, [tp, rstd], [tp])
            o = ob[oi % 2]
            oi += 1
            P.act(o[:, 0:N], tp[:, 0:N], AF.Silu, [tp, (prm, 1), (prm, 2)], [o],
                  scale=prm[:, g, 1:2], bias=prm[:, g, 2:3])
            P.dma("sp", mixT[g * 128:(g + 1) * 128, t0:t0 + N], o[:, 0:N], reads=[o])


def phase_outproj(P, C, mixT, x_res, x_out, nblocks, w_o, ln_g, ln_b):
    P.barrier()
    P.arena_reset(C["arena_base"])
    Wo = P.alloc([8, D_MODEL], BF16, "Wo")
    load_w_bf16(P, Wo, w_o, 8, D_MODEL, split=2)
    gbc = P.alloc([D_MODEL], F32, "gbc")
    bbc = P.alloc([D_MODEL], F32, "bbc")
    load_bcast(P, gbc, ln_g, D_MODEL)
    load_bcast(P, bbc, ln_b, D_MODEL)
    TB = 4
    mT = [P.alloc([8, TB * 128], BF16, "mT%d" % i) for i in range(2)]
    xres = [P.alloc([D_MODEL], F32, "xres%d" % i) for i in range(2)]
    rr = [P.alloc([D_MODEL], F32, "r%d" % i) for i in range(3)]
    stats = [P.alloc([2, 6], F32, "st%d" % i) for i in range(3)]
    mv = [P.alloc([4], F32, "mv%d" % i) for i in range(3)]
    psr = PsRot(P, [0, 1, 2, 3])
    cres = 1.0 / DN_ALPHA
    eps = LN_EPS / (DN_ALPHA * DN_ALPHA)
    def ln_out(g):
        s3 = g % 3
        layer_norm_block(P, rr[s3], rr[s3], gbc, bbc, stats[s3], mv[s3], eps)
        P.dma("pool", x_out[g * 128:(g + 1) * 128, :], rr[s3][:, :], reads=[rr[s3]])

    pend = None
    for ti, (b0, nb) in enumerate(tiles_of(nblocks, TB)):
        NT = nb * 128
        mt = mT[ti % 2]
        P.dma("sp", mt[:, :, 0:NT], mixT[:, b0 * 128:b0 * 128 + NT].rearrange("(c p) t -> p c t", p=128),
              writes=[mt])
        for j in range(nb):
            g = b0 + j
            s = g % 2
            P.dma("sp", xres[s][:, :], x_res[g * 128:(g + 1) * 128, :], writes=[xres[s]])
            for half in range(2):
                py = psr.next()
                for fc in range(8):
                    P.mm(py[:, 0:512], mt[:, fc, j * 128:(j + 1) * 128],
                         Wo[:, fc, half * 512:(half + 1) * 512], fc == 0, fc == 7,
                         [mt, (Wo, fc)], [py])
                P.stt("dve", rr[g % 3][:, half * 512:(half + 1) * 512], py[:, 0:512], cres,
                      xres[s][:, half * 512:(half + 1) * 512], ALU.mult, ALU.add,
                      [py, xres[s]], [rr[g % 3]])
            if pend is not None:
                ln_out(pend)
            pend = g
    if pend is not None:
        ln_out(pend)


ATT_WIN = 3


def phase_attn(P, C, qT, kT, vtm, S, mixT, row0):
    P.barrier()
    P.arena_reset(C["arena_base"])
    NBLK = S // 128
    dmask = P.alloc([512], F32, "dmask")
    P.dma("sp", dmask[:, :], C["dmask_dram"][:, :], writes=[dmask])
    C = dict(C)
    C["dmask"] = dmask
    kTs = P.alloc([4, S], BF16, "kTs")
    qTs = P.alloc([4, S], BF16, "qTs")
    vs = P.alloc([NBLK, 256], BF16, "vs")
    NB2 = 2
    ex = [[P.alloc([512], F32, "ex%d_%d" % (b, d)) for d in range(ATT_WIN)] for b in range(NB2)]
    spb = [[P.alloc([512], BF16, "sp%d_%d" % (b, d)) for d in range(ATT_WIN)] for b in range(NB2)]
    att = [[P.alloc([512], BF16, "att%d_%d" % (b, d)) for d in range(ATT_WIN)] for b in range(NB2)]
    wt = [P.alloc([512], F32, "w%d" % i) for i in range(2)]
    ot = [P.alloc([512], BF16, "ot%d" % i) for i in range(2)]
    psZ = PsRot(P, [0, 1, 2])
    psL = PsRot(P, [3, 4, 5])
    psO = PsRot(P, [6, 7])
    wi = 0
    for c in range(4):
        P.dma("sp", kTs[:, c, :], kT[c * 128:(c + 1) * 128, :], writes=[(kTs, c)])
        P.dma("sp", qTs[:, c, :], qT[c * 128:(c + 1) * 128, :], writes=[(qTs, c)])
    for par in range(2):
        pb = par * 64
        vsrc = vtm[:, par * 256:(par + 1) * 256].rearrange("(b p) f -> p b f", p=128)
        for b0 in range(0, NBLK, 8):
            b1 = min(NBLK, b0 + 8)
            P.dma("sp", vs[:, b0:b1, :], vsrc[:, b0:b1, :], writes=[(vs, b0 // 8)])
        mixv = mixT[row0:row0 + 512, :].rearrange("(h two d) t -> two d h t", two=2, d=64)[par]
        def qblock(i):
            b = i % NB2
            ndk = min(ATT_WIN, i + 1)
            for dk in range(ndk):
                kb = i - dk
                pz = psZ.next()
                for hh in range(4):
                    P.mm(pz[:, hh * 128:(hh + 1) * 128], kTs[pb:pb + 64, hh, kb * 128:(kb + 1) * 128],
                         qTs[pb:pb + 64, hh, i * 128:(i + 1) * 128], True, True,
                         [(kTs, hh), (qTs, hh)], [pz])
                e_t = ex[b][dk]
                P.act(e_t[:, :], pz[:, 0:512], AF.Exp, [pz], [e_t])
                if dk == 0:
                    P.tt("pool", e_t[:, :], e_t[:, :], C["dmask"][:, :], ALU.mult, [e_t, C["dmask"]], [e_t])
                P.act(spb[b][dk][:, :], e_t[:, :], AF.Ln, [e_t], [spb[b][dk]], bias=1.0)
                yield
            for dk in range(ndk):
                pl = psL.next()
                P.mm(pl[:, 0:512], C["trim"][:, :], spb[b][dk][:, :], True, dk == 0,
                     [C["trim"], spb[b][dk]], [pl])
                for d2 in range(dk):
                    P.mm(pl[:, 0:512], C["onesB"][:, :], spb[b][d2][:, :], False, d2 == dk - 1,
                         [C["onesB"], spb[b][d2]], [pl])
                w = wt[b]
                P.act(w[:, :], pl[:, 0:512], AF.Exp, [pl], [w], scale=-1.0)
                P.tt("dve", att[b][dk][:, :], ex[b][dk][:, :], w[:, :], ALU.mult, [ex[b][dk], w], [att[b][dk]])
                yield
            po = psO.next()
            for hh in range(4):
                for dk in range(ndk):
                    kb = i - dk
                    P.mm(po[0:64, hh * 128:(hh + 1) * 128], vs[:, kb, hh * 64:(hh + 1) * 64],
                         att[b][dk][:, hh * 128:(hh + 1) * 128], dk == 0, dk == ndk - 1,
                         [(vs, kb // 8), att[b][dk]], [po])
            o = ot[i % 2]
            P.copy("dve", o[0:64, :], po[0:64, 0:512], [po], [o])
            P.dma("sp", mixv[:, :, i * 128:(i + 1) * 128],
                  o[0:64, :].rearrange("d (h t) -> d h t", h=4), reads=[o])

        for i in range(0, NBLK, 2):
            gens = [qblock(i)] + ([qblock(i + 1)] if i + 1 < NBLK else [])
            alive = [True] * len(gens)
            while any(alive):
                for gi in range(len(gens)):
                    if alive[gi]:
                        try:
                            next(gens[gi])
                        except StopIteration:
                            alive[gi] = False


LDC = float(np.exp(-0.5))
GN_EPS = 64 * 1e-5
NPRM1 = 42


def phase_odd_mixer(P, C, x_in, nblocks, D, mixT):
    P.barrier()
    P.arena_reset(C["arena_base"])
    A = P.alloc
    Wi = A([8, 2304], BF16, "Wi")
    load_w_bf16(P, Wi, D["w_in"], 8, 2304, split=4)
    wa2 = A([512], BF16, "wa2")
    g2s = A([512], BF16, "g2s")
    wp = A([4, 128], BF16, "wp")
    P.dma("pool", wa2[:, :], D["wa2"][:, :], writes=[wa2])
    P.dma("pool", g2s[:, :], D["g2"][:, :], writes=[g2s])
    P.dma("pool", wp[:, :, :], D["wpool"][:, :, :], writes=[wp])
    prm = A([NPRM1], F32, "prm1")
    P.dma("sp", prm[:, :], D["prm1"][:, :], writes=[prm])
    mGa = A([4, 256], F32, "mGa")
    mN = A([4, 128], F32, "mN")
    triu = A([128], F32, "triu")
    bd = A([128], F32, "bd")
    hsel = A([2], BF16, "hsel")
    icnt = A([4, 128], F32, "icnt")
    P.dma("sp", mGa[:, :, :], D["mGa"].rearrange("p (h t) -> p h t", h=4), writes=[mGa])
    P.dma("sp", mN[:, :, :], D["mN"].rearrange("p (h t) -> p h t", h=4), writes=[mN])
    P.dma("sp", triu[:, :], D["triu"][:, :], writes=[triu])
    P.dma("sp", bd[:, :], D["bd"][:, :], writes=[bd])
    P.dma("pool", hsel[:, :], D["hsel"][:, :], writes=[hsel])
    P.dma("sp", icnt[:, :, :], D["icnt"].rearrange("p (h t) -> p h t", h=4), writes=[icnt])
    cwin = A([4, 128], F32, "cwin")
    for g in range(4):
        P.op("pool", lambda e, g=g: e.memset(cwin[:, g, :], 1.0 / (2 << g)), writes=[cwin])
    w0tm = A([512], F32, "w0tm")
    lnxg = A([512], F32, "lnxg")
    lnxb = A([512], F32, "lnxb")
    load_bcast(P, w0tm, D["w0row"], 512)
    load_bcast(P, lnxg, D["lnxg"], 512)
    load_bcast(P, lnxb, D["lnxb"], 512)
    Ssh = A([14, 128], F32, "Ssh")
    ones3 = Ssh
    P.op("pool", lambda e: e.memset(ones3[:, :, :], 1.0), writes=[ones3])
    mu_bc = A([14, 128], F32, "mu_bc")
    P.tt("pool", mu_bc[:, :, :], ones3[:, :, :], prm[:, 0:14].unsqueeze(2).to_broadcast([128, 14, 128]),
         ALU.mult, [ones3, prm], [mu_bc])

    def bc4(col, name):
        t = A([4, 128], F32, name)
        P.tt("pool", t[:, :, :], ones3[:, 0:4, :],
             prm[:, col:col + 4].unsqueeze(2).to_broadcast([128, 4, 128]), ALU.mult, [ones3, prm], [t])
        return t

    w0f = bc4(14, "w0f")
    a0f = bc4(18, "a0f")
    kkb = bc4(22, "kkb")
    kab = bc4(26, "kab")
    rkb = bc4(30, "rkb")
    oma = A([4, 128], F32, "oma")
    P.ts("pool", oma[:, :, :], kab[:, :, :], -1.0, 1.0, ALU.mult, ALU.add, [kab], [oma])
    pbs = A([4], F32, "pbs")
    P.tt("pool", pbs[:, :], prm[:, 34:38], prm[:, 38:42], ALU.mult, [prm], [pbs])
    identB = C["ident"]
    identB4 = A([4, 128], BF16, "identB4")
    for h in range(4):
        P.copy("pool", identB4[:, h, :], identB[:, :], [identB], [identB4])
    identF = A([128], F32, "identF")
    P.copy("pool", identF[:, :], identB[:, :], [identB], [identF])

    pbuf = A([14, 129], F32, "pbuf")
    P.op("pool", lambda e: e.memset(pbuf[:, :, 0:1], 0.0), writes=[(pbuf, "h")])
    PADU = 16
    ubuf = A([4, PADU + 128], F32, "ubuf")
    P.op("pool", lambda e: e.memset(ubuf[:, :, 0:PADU], 0.0), writes=[(ubuf, "h")])
    Ssf = A([4, 64], F32, "Ssf")
    Sb = A([4, 64], BF16, "Sb")
    P.op("pool", lambda e: e.memset(Ssf[:, :, :], 0.0), writes=[Ssf])
    P.op("pool", lambda e: e.memset(Sb[:, :, :], 0.0), writes=[Sb])

    xin = [A([D_MODEL], F32, "xin%d" % i) for i in range(2)]
    xbf = [A([D_MODEL], BF16, "xbf%d" % i) for i in range(2)]
    xT = [A([8, 128], BF16, "xT%d" % i) for i in range(2)]
    lor = A([128], BF16, "lor")
    sgb = A([128], BF16, "sgb")
    sgf = A([4, 128], F32, "sgf")
    icl = A([4, 128], F32, "icl")
    sgt = A([512], F32, "sgt")
    gate_sb = A([512], F32, "gate_sb")
    clsb = A([4, 128], F32, "clsb")
    cle = A([4, 128], F32, "cle")
    E1 = A([4, 128], F32, "E1")
    E2 = A([4, 128], F32, "E2")
    E3 = A([4, 128], F32, "E3")
    E4 = A([4, 128], F32, "E4")
    nbv = A([4], F32, "nbv")
    gC = A([4], F32, "gC")
    kk = A([4, 128], F32, "kk")
    sq = A([4, 128], F32, "sq")
    rn = A([4, 128], F32, "rn")
    t1 = A([4, 128], F32, "t1")
    kmod = A([4, 128], F32, "kmod")
    bvec = A([4, 128], F32, "bvec")
    ARf = A([4, 256], BF16, "ARf")
    ARz = [A([4, 256], BF16, "ARz%d" % p) for p in range(2)]
    Kt = A([4, 128], BF16, "Kt")
    Bt = A([4, 128], BF16, "Bt")
    Kh = A([4, 128], BF16, "Kh")
    Bh = A([4, 128], BF16, "Bh")
    rkr = A([4, 128], BF16, "rkr")
    Vtf = A([512], F32, "Vtf")
    Vtb = A([512], BF16, "Vtb")
    Kht = A([512], BF16, "Kht")
    Bht = A([512], BF16, "Bht")
    bon = A([8], F32, "bon")
    GkM = [A([4, 256], BF16, "GkM%d" % p) for p in range(2)]
    GbM = [A([4, 256], BF16, "GbM%d" % p) for p in range(2)]
    Xk = [[A([4, 128], BF16, "X%d_%d" % (p, i)) for i in range(2)] for p in range(2)]
    Nk = [[A([4, 128], BF16, "N%d_%d" % (p, i)) for i in range(2)] for p in range(2)]
    Pk = [[A([4, 128], BF16, "P%d_%d" % (p, i)) for i in range(2)] for p in range(2)]
    Qk = [[A([4, 128], BF16, "Q%d_%d" % (p, i)) for i in range(2)] for p in range(2)]
    RHSb = A([512], BF16, "RHSb")
    Ub = A([512], BF16, "Ub")
    ysb = A([512], F32, "ysb")
    ysq = A([512], F32, "ysq")
    st = A([4, 8], F32, "st")
    yob = A([512], BF16, "yob")
    yT = A([4, 128], BF16, "yT")
    pooled = A([4, 128], BF16, "pooled")
    ptmp = [A([4, PADU + 128], F32, "ptmp%d" % i) for i in range(2)]
    pout = A([4, 128], BF16, "pout")
    pm = A([2], F32, "pm")
    P.op("pool", lambda e: e.memset(pm[:, :], 0.0), writes=[pm])
    P.op("pool", lambda e: e.memset(pm[0:64, 0:1], 1.0), writes=[pm])
    P.op("pool", lambda e: e.memset(pm[64:128, 1:2], 1.0), writes=[pm])

    psr = PsRot(P, [0, 1, 2, 3, 4, 5])
    psrA = PsRot(P, [0, 1, 2])
    psrB = PsRot(P, [3, 4, 5])
    psT = P.ps_tiles[6]
    psS = P.ps_tiles[7]
    ctr = [0]

    def v3(ps, n=4, w=128):
        return ps[:, 0:n * w].rearrange("p (c t) -> p c t", c=n)

    for blk in range(nblocks):
        t0 = blk * 128
        xt = xT[blk % 2]
        if C.get('odd_stop', 99) <= -1:
            continue
        load_xT_tile(P, C, x_in, blk, 1, xin, xbf, xt, psT, ctr)
        if C.get('odd_stop', 99) <= 0:
            continue
        for grp in range(5):
            c0 = grp * 4
            ncg = min(4, 18 - c0)
            pp = psr.next()
            for ci in range(ncg):
                c = c0 + ci
                for kc in range(8):
                    P.mm(pp[:, ci * 128:(ci + 1) * 128], Wi[:, kc, c * 128:(c + 1) * 128], xt[:, kc, :],
                         kc == 0, kc == 7, [(Wi, kc), xt], [pp])
            if grp < 3:
                P.copy("dve", pbuf[:, c0:c0 + 4, 1:129], v3(pp), [pp], [pbuf])
            elif grp == 3:
                P.copy("dve", pbuf[:, 12:14, 1:129], v3(pp, 2), [pp], [pbuf])
                P.copy("dve", ubuf[:, 0:2, PADU:PADU + 128], pp[:, 256:512].rearrange("p (c t) -> p c t", c=2),
                       [pp], [ubuf])
            else:
                P.copy("dve", ubuf[:, 2:4, PADU:PADU + 128], v3(pp, 2), [pp], [ubuf])
        if C.get('odd_stop', 99) <= 0.5:
            continue
        P.tt("dve", Ssh[:, :, :], pbuf[:, :, 0:128], pbuf[:, :, 1:129], ALU.subtract,
             [pbuf, (pbuf, "h")], [Ssh])
        P.tt("pool", Ssh[:, :, :], Ssh[:, :, :], mu_bc[:, :, :], ALU.mult, [Ssh, mu_bc], [Ssh])
        P.tt("dve", Ssh[:, :, :], Ssh[:, :, :], pbuf[:, :, 1:129], ALU.add, [Ssh, pbuf], [Ssh])
        P.copy("pool", pbuf[:, :, 0:1], pbuf[:, :, 128:129], [pbuf], [(pbuf, "h")])
        if C.get('odd_stop', 99) <= 1:
            continue
        r_ = Ssh[:, 0:4, :]
        k_ = Ssh[:, 4:8, :]
        v_ = Ssh[:, 8:12, :]
        P.act(lor[0:64, :], Ssh[0:64, 12, :], AF.Tanh, [Ssh], [(lor, 0)])
        P.copy("dve", lor[64:128, :], Ssh[64:128, 12, :], [Ssh], [(lor, 1)])
        P.act(sgb[:, :], Ssh[:, 13, :], AF.Sigmoid, [Ssh], [sgb])
        pdw = psr.next()
        for c in range(4):
            P.mm(pdw[:, c * 128:(c + 1) * 128], wa2[0:64, c * 128:(c + 1) * 128], lor[0:64, :], True, True,
                 [wa2, (lor, 0)], [pdw])
        pda = psr.next()
        for c in range(4):
            P.mm(pda[:, c * 128:(c + 1) * 128], wa2[64:128, c * 128:(c + 1) * 128], lor[64:128, :], True, True,
                 [wa2, (lor, 1)], [pda])
        pdt = psr.next()
        P.mm(pdt[:, 0:512], lor[0:64, :], wa2[0:64, :], True, True, [wa2, (lor, 0)], [pdt])
        pgt = psr.next()
        P.mm(pgt[:, 0:512], sgb[:, :], g2s[:, :], True, True, [sgb, g2s], [pgt])
        P.tt("dve", sgf[:, :, :], v3(pdw), w0f[:, :, :], ALU.add, [pdw, w0f], [sgf])
        P.act(sgf[:, :, :], sgf[:, :, :], AF.Sigmoid, [sgf], [sgf])
        P.tt("dve", icl[:, :, :], v3(pda), a0f[:, :, :], ALU.add, [pda, a0f], [icl])
        P.act(icl[:, :, :], icl[:, :, :], AF.Sigmoid, [icl], [icl])
        P.tt("dve", sgt[:, :], pdt[:, 0:512], w0tm[:, :], ALU.add, [pdt, w0tm], [sgt])
        P.act(sgt[:, :], sgt[:, :], AF.Sigmoid, [sgt], [sgt])
        P.copy("act", gate_sb[:, :], pgt[:, 0:512], [pgt], [gate_sb])
        if C.get('odd_stop', 99) <= 2:
            continue
        pcl = psr.next()
        for c in range(4):
            P.mm(pcl[:, c * 128:(c + 1) * 128], sgt[:, c * 128:(c + 1) * 128], triu[:, :], True, True,
                 [sgt, triu], [pcl])
        P.copy("dve", clsb[:, :, :], v3(pcl), [pcl], [clsb])
        P.tt("pool", cle[:, :, :], clsb[:, :, :], sgf[:, :, :], ALU.subtract, [clsb, sgf], [cle])
        P.ts("dve", nbv[:, :], clsb[:, :, 127], -LDC, None, ALU.mult, None, [clsb], [nbv])
        P.tt("pool", kk[:, :, :], k_, kkb[:, :, :], ALU.mult, [Ssh, kkb], [kk])
        P.tt("pool", sq[:, :, :], kk[:, :, :], kk[:, :, :], ALU.mult, [kk], [sq])
        pss = psr.next()
        P.mm(pss[:, 0:512], bd[:, :], sq[:, :, :].rearrange("p c t -> p (c t)"), True, True, [bd, sq], [pss])
        P.ts("dve", rn[:, :, :], v3(pss), 1e-24, None, ALU.max, None, [pss], [rn])
        P.act(rn[:, :, :], rn[:, :, :], AF.Sqrt, [rn], [rn])
        P.op("dve", lambda e: e.reciprocal(rn[:, :, :], rn[:, :, :]), reads=[rn], writes=[rn])
        P.tt("dve", kk[:, :, :], kk[:, :, :], rn[:, :, :], ALU.mult, [kk, rn], [kk])
        P.tt("pool", t1[:, :, :], icl[:, :, :], kab[:, :, :], ALU.mult, [icl, kab], [t1])
        P.tt("pool", t1[:, :, :], t1[:, :, :], oma[:, :, :], ALU.add, [t1, oma], [t1])
        P.tt("dve", kmod[:, :, :], k_, t1[:, :, :], ALU.mult, [Ssh, t1], [kmod])
        P.tt("pool", bvec[:, :, :], kk[:, :, :], icl[:, :, :], ALU.mult, [kk, icl], [bvec])
        P.act(E1[:, :, :], clsb[:, :, :], AF.Exp, [clsb], [E1], scale=-LDC)
        P.act(E2[:, :, :], clsb[:, :, :], AF.Exp, [clsb], [E2], scale=LDC)
        P.act(E3[:, :, :], cle[:, :, :], AF.Exp, [cle], [E3], scale=-LDC)
        for c in range(4):
            P.act(E4[:, c, :], clsb[:, c, :], AF.Exp, [clsb, nbv], [E4], scale=LDC, bias=nbv[:, c:c + 1])
        P.act(gC[:, :], nbv[:, :], AF.Exp, [nbv], [gC])
        P.stt("dve", ARf[:, :, 0:128], kk[:, :, :], -1.0, E3[:, :, :], ALU.mult, ALU.mult, [kk, E3], [(ARf, 0)])
        P.tt("pool", ARf[:, :, 128:256], r_, E1[:, :, :], ALU.mult, [Ssh, E1], [(ARf, 1)])
        for p in range(2):
            P.ts("dve" if p == 0 else "pool", ARz[p][:, :, :], ARf[:, :, :], pm[:, p:p + 1], None, ALU.mult, None,
                 [(ARf, 0), (ARf, 1), pm], [ARz[p]])
        P.tt("dve", Kt[:, :, :], kmod[:, :, :], E2[:, :, :], ALU.mult, [kmod, E2], [Kt])
        P.tt("pool", Bt[:, :, :], bvec[:, :, :], E2[:, :, :], ALU.mult, [bvec, E2], [Bt])
        P.tt("dve", Kh[:, :, :], kmod[:, :, :], E4[:, :, :], ALU.mult, [kmod, E4], [Kh])
        P.tt("pool", Bh[:, :, :], bvec[:, :, :], E4[:, :, :], ALU.mult, [bvec, E4], [Bh])
        P.tt("pool", t1[:, :, :], r_, rkb[:, :, :], ALU.mult, [Ssh, rkb], [t1])
        P.tt("dve", rkr[:, :, :], t1[:, :, :], kmod[:, :, :], ALU.mult, [t1, kmod], [rkr])
        if C.get('odd_stop', 99) <= 3:
            continue
        pvt = psr.next()
        for c in range(4):
            P.tr(pvt[:, c * 128:(c + 1) * 128], Ssh[:, 8 + c, :], identF[:, :], [Ssh, identF], [pvt])
        P.copy("act", Vtf[:, :], pvt[:, 0:512], [pvt], [Vtf])
        P.copy("dve", Vtb[:, :], pvt[:, 0:512], [pvt], [Vtb])
        psv = psT.ap.bitcast(BF16)
        for c in range(4):
            P.tr(psv[:, c * 128:(c + 1) * 128], Kh[:, c, :], identB[:, :], [Kh, identB], [psT])
        for c in range(4):
            P.tr(psv[:, 512 + c * 128:512 + (c + 1) * 128], Bh[:, c, :], identB[:, :], [Bh, identB], [psT])
        P.copy("act", Kht[:, :], psv[:, 0:512], [psT], [Kht])
        P.copy("dve", Bht[:, :], psv[:, 512:1024], [psT], [Bht])
        pbn = psr.next()
        for c in range(4):
            P.mm(pbn[:, c * 2:(c + 1) * 2], rkr[:, c, :], hsel[:, :], True, True, [rkr, hsel], [pbn])
        P.copy("act", bon[:, :], pbn[:, 0:8], [pbn], [bon])
        if C.get('odd_stop', 99) <= 4:
            continue
        TT = [None, None]

        def intra(par, pr):
            az = ARz[par]
            pgk = [pr.next(), pr.next()]
            for hh in range(4):
                P.mm(pgk[hh // 2][:, (hh % 2) * 256:(hh % 2 + 1) * 256], Kt[:, hh, :], az[:, hh, :], True, True,
                     [Kt, az], [pgk[hh // 2]])
            yield
            for i2 in range(2):
                P.tt("dve", GkM[par][:, 2 * i2:2 * i2 + 2, :], v3(pgk[i2], 2, 256), mGa[:, 2 * i2:2 * i2 + 2, :],
                     ALU.mult, [pgk[i2], mGa], [GkM[par]])
            pgb = [pr.next(), pr.next()]
            for hh in range(4):
                P.mm(pgb[hh // 2][:, (hh % 2) * 256:(hh % 2 + 1) * 256], Bt[:, hh, :], az[:, hh, :], True, True,
                     [Bt, az], [pgb[hh // 2]])
            yield
            for i2 in range(2):
                P.tt("dve", GbM[par][:, 2 * i2:2 * i2 + 2, :], v3(pgb[i2], 2, 256), mGa[:, 2 * i2:2 * i2 + 2, :],
                     ALU.mult, [pgb[i2], mGa], [GbM[par]])
            pn0 = pr.next()
            for hh in range(4):
                P.mm(pn0[:, hh * 128:(hh + 1) * 128], az[:, hh, 0:128], Bt[:, hh, :], True, True, [az, Bt], [pn0])
            yield
            X, N_, Pm, Q = Xk[par], Nk[par], Pk[par], Qk[par]
            P.tt("dve", N_[0][:, :, :], v3(pn0), mN[:, :, :], ALU.mult, [pn0, mN], [N_[0]])
            P.copy("pool", X[0][:, :, :], GbM[par][:, :, 0:128], [GbM[par]], [X[0]])
            P.tt("pool", Pm[0][:, :, :], X[0][:, :, :], identB4[:, :, :], ALU.add, [X[0], identB4], [Pm[0]])
            P.tt("pool", Q[0][:, :, :], N_[0][:, :, :], identB4[:, :, :], ALU.add, [N_[0], identB4], [Q[0]])
            yield
            NLEV = 6
            for lv in range(NLEV):
                a, b = lv % 2, (lv + 1) % 2
                last = lv == NLEV - 1
                px = pr.next()
                for hh in range(4):
                    P.mm(px[:, hh * 128:(hh + 1) * 128], N_[a][:, hh, :], X[a][:, hh, :], True, True,
                         [N_[a], X[a]], [px])
                if not last:
                    pn = pr.next()
                    for hh in range(4):
                        P.mm(pn[:, hh * 128:(hh + 1) * 128], X[a][:, hh, :], N_[a][:, hh, :], True, True,
                             [N_[a], X[a]], [pn])
                yield
                P.copy("act", X[b][:, :, :], v3(px), [px], [X[b]])
                if not last:
                    P.copy("dve", N_[b][:, :, :], v3(pn), [pn], [N_[b]])
                yield
                pp_ = pr.next()
                for hh in range(4):
                    P.mm(pp_[:, hh * 128:(hh + 1) * 128], Q[a][:, hh, :], X[b][:, hh, :], True, True,
                         [Q[a], X[b]], [pp_])
                if not last:
                    pq_ = pr.next()
                    for hh in range(4):
                        P.mm(pq_[:, hh * 128:(hh + 1) * 128], Pm[a][:, hh, :], N_[b][:, hh, :], True, True,
                             [Pm[a], N_[b]], [pq_])
                yield
                P.tt("dve", Pm[b][:, :, :], v3(pp_), Pm[a][:, :, :], ALU.add, [pp_, Pm[a]], [Pm[b]])
                if not last:
                    P.tt("dve", Q[b][:, :, :], v3(pq_), Q[a][:, :, :], ALU.add, [pq_, Q[a]], [Q[b]])
                yield
            TT[par] = Pm[NLEV % 2]

        gens = [intra(0, psrA), intra(1, psrB)]
        alive = [True, True]
        while any(alive):
            for gi in range(2):
                if alive[gi]:
                    try:
                        next(gens[gi])
                    except StopIteration:
                        alive[gi] = False
        if C.get('odd_stop', 99) <= 5:
            continue
        prh = psr.next()
        for hh in range(4):
            for par in range(2):
                col = hh * 128 + par * 64
                P.mm(prh[:, col:col + 64], ARz[par][:, hh, 0:128], Sb[:, hh, :], True, False,
                     [ARz[par], Sb], [prh])
                P.mm(prh[:, col:col + 64], GkM[par][:, hh, 0:128], Vtb[:, col:col + 64], False, True,
                     [GkM[par], Vtb], [prh])
        P.copy("act", RHSb[:, :], prh[:, 0:512], [prh], [RHSb])
        pu = psr.next()
        for hh in range(4):
            for par in range(2):
                col = hh * 128 + par * 64
                P.mm(pu[:, col:col + 64], TT[par][:, hh, :], RHSb[:, col:col + 64], True, True,
                     [TT[par], RHSb], [pu])
        P.copy("dve", Ub[:, :], pu[:, 0:512], [pu], [Ub])
        py = psr.next()
        for hh in range(4):
            for par in range(2):
                col = hh * 128 + par * 64
                P.mm(py[:, col:col + 64], ARz[par][:, hh, 128:256], Sb[:, hh, :], True, False,
                     [ARz[par], Sb], [py])
                P.mm(py[:, col:col + 64], GkM[par][:, hh, 128:256], Vtb[:, col:col + 64], False, False,
                     [GkM[par], Vtb], [py])
                P.mm(py[:, col:col + 64], GbM[par][:, hh, 128:256], Ub[:, col:col + 64], False, True,
                     [GbM[par], Ub], [py])
        for hh in range(4):
            P.mm(psS[:, hh * 128:(hh + 1) * 128], Kht[:, hh * 128:(hh + 1) * 128], Vtb[:, hh * 128:(hh + 1) * 128],
                 True, False, [Kht, Vtb], [psS])
            P.mm(psS[:, hh * 128:(hh + 1) * 128], Bht[:, hh * 128:(hh + 1) * 128], Ub[:, hh * 128:(hh + 1) * 128],
                 False, True, [Bht, Ub], [psS])
        P.tt("pool", Ssf[:, :, :], Ssf[:, :, :], gC[:, :].unsqueeze(2).to_broadcast([128, 4, 64]), ALU.mult,
             [Ssf, gC], [Ssf])
        P.tt("dve", Ssf[0:64, :, :], Ssf[0:64, :, :], v3(psS)[0:64, :, 0:64], ALU.add, [Ssf, psS], [Ssf])
        P.tt("dve", Ssf[64:128, :, :], Ssf[64:128, :, :], v3(psS)[64:128, :, 64:128], ALU.add, [Ssf, psS], [Ssf])
        P.copy("pool", Sb[:, :, :], Ssf[:, :, :], [Ssf], [Sb])
        if C.get('odd_stop', 99) <= 6:
            continue
        P.copy("act", ysb[:, :], py[:, 0:512], [py], [ysb])
        P.act(ysq[:, :], py[:, 0:512], AF.Square, [py], [ysq])
        y3 = ysb[:, :].rearrange("p (h i) -> p h i", h=8)
        P.op("dve", lambda e, y3=y3: e.tensor_reduce(st[:, 0, :], y3, AX.X, ALU.add), reads=[ysb], writes=[(st, 0)])
        P.op("dve", lambda e: e.tensor_reduce(st[:, 1, :], ysq[:, :].rearrange("p (h i) -> p h i", h=8), AX.X, ALU.add),
             reads=[ysq], writes=[(st, 1)])
        P.ts("dve", st[:, 0, :], st[:, 0, :], 1.0 / 64, None, ALU.mult, None, [(st, 0)], [(st, 0)])
        P.tt("dve", st[:, 3, :], st[:, 0, :], st[:, 0, :], ALU.mult, [(st, 0)], [(st, 3)])
        P.stt("dve", st[:, 2, :], st[:, 1, :], 1.0 / 64, st[:, 3, :], ALU.mult, ALU.subtract,
              [(st, 1), (st, 3)], [(st, 2)])
        P.ts("dve", st[:, 2, :], st[:, 2, :], GN_EPS, None, ALU.add, None, [(st, 2)], [(st, 2)])
        P.act(st[:, 2, :], st[:, 2, :], AF.Sqrt, [(st, 2)], [(st, 2)])
        P.op("dve", lambda e: e.reciprocal(st[:, 2, :], st[:, 2, :]), reads=[(st, 2)], writes=[(st, 2)])
        P.tt("dve", y3, y3, st[:, 0, :].unsqueeze(2).to_broadcast([128, 8, 64]), ALU.subtract, [ysb, (st, 0)], [ysb])
        P.tt("dve", y3, y3, st[:, 2, :].unsqueeze(2).to_broadcast([128, 8, 64]), ALU.mult, [ysb, (st, 2)], [ysb])
        P.tt("pool", ysb[:, :], ysb[:, :], lnxg[:, :], ALU.mult, [ysb, lnxg], [ysb])
        P.tt("pool", ysb[:, :], ysb[:, :], lnxb[:, :], ALU.add, [ysb, lnxb], [ysb])
        P.tt("pool", ysq[:, :].rearrange("p (h i) -> p h i", h=8), Vtf[:, :].rearrange("p (h i) -> p h i", h=8),
             bon[:, :].unsqueeze(2).to_broadcast([128, 8, 64]), ALU.mult, [Vtf, bon, ysq], [ysq])
        P.tt("pool", ysb[:, :], ysb[:, :], ysq[:, :], ALU.add, [ysb, ysq], [ysb])
        P.tt("dve", yob[:, :], ysb[:, :], gate_sb[:, :], ALU.mult, [ysb, gate_sb], [yob])
        for c in range(4):
            P.tr(psv[:, c * 128:(c + 1) * 128], yob[:, c * 128:(c + 1) * 128], identB[:, :], [yob, identB], [psT])
        P.copy("act", yT[:, :, :], psv[:, 0:512].rearrange("p (c t) -> p c t", c=4), [psT], [yT])
        P.dma("sp", mixT[0:512, t0:t0 + 128].rearrange("(c p) t -> p c t", p=128), yT[:, :, :], reads=[yT])
        if C.get('odd_stop', 99) <= 7:
            continue
        W = PADU + 128
        a_, b_ = ptmp
        P.tt("pool", a_[:, :, 1:W], ubuf[:, :, 1:W], ubuf[:, :, 0:W - 1], ALU.add, [ubuf, (ubuf, "h")], [a_])
        P.tt("pool", b_[:, 1:4, 3:W], a_[:, 1:4, 3:W], a_[:, 1:4, 1:W - 2], ALU.add, [a_], [b_])
        P.tt("pool", a_[:, 2:4, 7:W], b_[:, 2:4, 7:W], b_[:, 2:4, 3:W - 4], ALU.add, [b_, a_], [(a_, 1)])
        P.tt("pool", b_[:, 3:4, 15:W], a_[:, 3:4, 15:W], a_[:, 3:4, 7:W - 8], ALU.add, [a_, (a_, 1), b_], [(b_, 1)])
        srcs = [a_, b_, a_, b_]
        for g in range(4):
            win = 2 << g
            src = srcs[g]
            deps = [a_, b_, (a_, 1), (b_, 1), ubuf]
            ic = icnt if blk == 0 else cwin
            P.tt("dve", src[:, g, PADU:W], src[:, g, PADU:W], ic[:, g, :], ALU.mult, deps + [ic], [(src, "f%d" % g)])
            P.tt("dve", pooled[:, g, :], src[:, g, PADU:W], ubuf[:, g, PADU:W], ALU.subtract,
                 deps + [(src, "f%d" % g)], [(pooled, g)])
        P.copy("pool", ubuf[:, :, 0:PADU], ubuf[:, :, 128:128 + PADU], [ubuf, (pooled, 0), (pooled, 1), (pooled, 2), (pooled, 3)],
               [(ubuf, "h")])
        ppl = psr.next()
        for g in range(4):
            P.mm(ppl[:, g * 128:(g + 1) * 128], wp[:, g, :], pooled[:, g, :], True, True, [wp, (pooled, g)], [ppl])
        for g in range(4):
            P.act(pout[:, g, :], ppl[:, g * 128:(g + 1) * 128], AF.Identity, [ppl, prm, pbs], [pout],
                  scale=prm[:, 38 + g:39 + g], bias=pbs[:, g:g + 1])
        P.dma("sp", mixT[512:1024, t0:t0 + 128].rearrange("(c p) t -> p c t", p=128), pout[:, :, :], reads=[pout])


def host_consts_odd():
    k = np.arange(128)
    strict = (k[:, None] < k[None, :]).astype(np.float32)
    incl = (k[:, None] <= k[None, :]).astype(np.float32)
    mGa = np.tile(np.concatenate([strict, incl], axis=1), (1, 4))
    mN = np.tile((k[None, :] < k[:, None]).astype(np.float32), (1, 4))
    bd = (k[:, None] // 64 == k[None, :] // 64).astype(np.float32)
    hsel = np.stack([(k < 64), (k >= 64)], axis=1).astype(np.float32)
    t = np.arange(128)
    icnt = np.concatenate([np.tile(1.0 / np.minimum(t + 1, 2 << g)[None, :], (128, 1)) for g in range(4)],
                          axis=1).astype(np.float32)
    return {"mGa": np.ascontiguousarray(mGa), "mN": np.ascontiguousarray(mN), "triu": incl, "bd": bd,
            "hsel": hsel, "icnt": np.ascontiguousarray(icnt)}


def host_odd_params(inp, i):
    def fm(v):
        return np.asarray(v, np.float32).reshape(4, 128).T
    prm1 = np.concatenate([
        np.asarray(inp["o_mu"][i], np.float32).reshape(14, 128).T,
        fm(inp["o_w0"][i]), fm(inp["o_a0"][i]), fm(inp["o_k_k"][i].reshape(512)),
        fm(inp["o_k_a"][i].reshape(512)), fm(inp["o_r_k"][i].reshape(512)),
        fm(inp["o_b_pool"][i].reshape(512)), fm(inp["o_pool_scale"][i])], axis=1)
    return {
        "w_in": np.ascontiguousarray(inp["o_w_in"][i], np.float32),
        "wa2": np.ascontiguousarray(np.concatenate([inp["o_w2"][i], inp["o_a2"][i]], axis=0), np.float32),
        "g2": np.ascontiguousarray(inp["o_g2"][i], np.float32),
        "wpool": np.ascontiguousarray(np.transpose(inp["o_w_pool"][i], (1, 0, 2)), np.float32),
        "prm1": np.ascontiguousarray(prm1, np.float32),
        "w0row": np.ascontiguousarray(inp["o_w0"][i], np.float32),
        "lnxg": np.ascontiguousarray(inp["o_lnx_g"][i].reshape(512), np.float32),
        "lnxb": np.ascontiguousarray(inp["o_lnx_b"][i].reshape(512), np.float32),
    }


def host_even_params(inp, i):
    return {
        "e_w_in": np.ascontiguousarray(inp["e_w_in"][i], np.float32),
        "e_w_out": np.ascontiguousarray(inp["e_w_out"][i], np.float32),
        "e_dwT": np.ascontiguousarray(np.asarray(inp["e_w_dw"][i], np.float32).T.reshape(4, 128, 31).transpose(1, 0, 2)),
        "e_prm": np.ascontiguousarray(np.stack([inp["e_b_dw"][i], inp["e_conv_g"][i], inp["e_conv_b"][i]], -1)
                                      .astype(np.float32).reshape(4, 128, 3).transpose(1, 0, 2)),
    }


def build_program(S, shapes):
    NBLK = S // 128
    nc = bass.Bass("TRN2", target_bir_lowering=False)
    I = {k: nc.dram_tensor(k, list(shp), F32, kind="ExternalInput").ap() for k, shp in shapes.items()}
    out = nc.dram_tensor("out", [S, D_MODEL], F32, kind="ExternalOutput").ap()

    def scratch(name, shape, dt):
        return nc.dram_tensor(name, shape, dt, kind="Internal").ap()

    xa = scratch("s_xa", [S, D_MODEL], F32)
    xb = scratch("s_xb", [S, D_MODEL], F32)
    hcT = scratch("s_hcT", [512, S], BF16)
    qT = scratch("s_qT", [512, S], BF16)
    kT = scratch("s_kT", [512, S], BF16)
    vtm = scratch("s_vtm", [S, 512], BF16)
    mixT = scratch("s_mixT", [1024, S], BF16)
    es = ExitStack()
    P = Prog(nc, es)
    C = setup_consts(P, {k[2:]: v for k, v in I.items() if k.startswith("c_")})
    g, b = I["ln_g"], I["ln_b"]
    fi, fo = I["ffn_in"], I["ffn_out"]
    phase_ffn(P, C, I["x"], 0, xa, 0, NBLK, fi[0, 0], fo[0, 0], g[0, 0], b[0, 0])
    phase_inproj_even(P, C, xa, NBLK, I["e_w_in"], hcT, qT, kT, vtm)
    phase_conv(P, C, hcT, S, I["e_dwT"], I["e_prm"], mixT)
    phase_attn(P, C, qT, kT, vtm, S, mixT, 512)
    phase_outproj(P, C, mixT, xa, xb, NBLK, I["e_w_out"], g[0, 1], b[0, 1])
    phase_ffn(P, C, xb, 0, xa, 0, NBLK, fi[0, 1], fo[0, 1], g[0, 2], b[0, 2])
    phase_ffn(P, C, xa, 0, xb, 0, NBLK, fi[1, 0], fo[1, 0], g[1, 0], b[1, 0])
    D = {k[2:]: v for k, v in I.items() if k.startswith("o_") or k.startswith("k_")}
    phase_odd_mixer(P, C, xb, NBLK, D, mixT)
    phase_outproj(P, C, mixT, xb, xa, NBLK, I["o_w_out"], g[1, 1], b[1, 1])
    phase_ffn(P, C, xa, 0, out, 0, NBLK, fi[1, 1], fo[1, 1], g[1, 2], b[1, 2])
    P.finish()
    es.close()
    return nc


ACTIVE_CORES = (0, 1, 4, 5)


def kernel(**inputs):
    inp = {k: np.asarray(v) for k, v in inputs.items()}
    x = np.asarray(inp["x"], np.float32)
    B, S, _ = x.shape
    shared = {
        "ffn_in": np.ascontiguousarray(inp["ffn_in"], np.float32),
        "ffn_out": np.ascontiguousarray(inp["ffn_out"], np.float32),
        "ln_g": np.ascontiguousarray(inp["ln_g"], np.float32),
        "ln_b": np.ascontiguousarray(inp["ln_b"], np.float32),
        "o_w_out": np.ascontiguousarray(inp["o_w_out"][0], np.float32),
    }
    shared.update(host_even_params(inp, 0))
    for k, v in host_odd_params(inp, 0).items():
        shared["o_" + k] = v
    for k, v in host_consts_odd().items():
        shared["k_" + k] = v
    for k, v in host_consts().items():
        shared["c_" + k] = v
    n_cores = 8
    assert B <= len(ACTIVE_CORES)
    in_maps = []
    zero_x = np.zeros((S, D_MODEL), np.float32)
    for c in range(n_cores):
        m = dict(shared)
        if c in ACTIVE_CORES and ACTIVE_CORES.index(c) < B:
            m["x"] = np.ascontiguousarray(x[ACTIVE_CORES.index(c)])
        else:
            m["x"] = zero_x
        in_maps.append(m)
    shapes = {k: v.shape for k, v in in_maps[0].items()}
    nc = build_program(S, shapes)
    res = run_bass_kernel_spmd(nc, in_maps, core_ids=list(range(n_cores)))
    out = np.stack([np.asarray(res.results[ACTIVE_CORES[bi]]["out"], np.float32) for bi in range(B)], axis=0)
    return out
```

```python
import numpy as np
from contextlib import ExitStack
import concourse.bass as bass
import concourse.mybir as mybir
from concourse.bass_utils import run_bass_kernel_spmd

F32 = mybir.dt.float32
BF16 = mybir.dt.bfloat16
AF = mybir.ActivationFunctionType
ALU = mybir.AluOpType
AX = mybir.AxisListType

D_MODEL = 1024
D_FF = 2816
DEPTH = 2
DN_ALPHA = (2.0 * DEPTH) ** 0.25
LN_EPS = 1e-5
ARENA_BYTES = 207 * 1024
SQ_DVE = "pool"


class Op:
    __slots__ = ("eng", "fn", "idx", "deps", "dma", "dslot", "dval", "signal",
                 "sig", "waits")

    def __init__(self, eng, fn, idx, dma):
        self.eng = eng
        self.fn = fn
        self.idx = idx
        self.dma = dma
        self.deps = set()
        self.dslot = -1
        self.dval = 0
        self.signal = False
        self.sig = 0
        self.waits = []


class Tile:
    def __init__(self, ap, name=""):
        self.ap = ap
        self.name = name

    def __getitem__(self, k):
        return self.ap[k]


class Prog:
    CENG = ("pe", "act", "dve", "pool")

    def __init__(self, nc, es, n_dma_sems=48):
        self.nc = nc
        self.eng = {"pe": nc.tensor, "act": nc.scalar, "dve": nc.vector,
                    "pool": nc.gpsimd, "sp": nc.sync}
        self.sem = {e: es.enter_context(nc.semaphore("sem_" + e)) for e in self.CENG}
        self.dsem = [es.enter_context(nc.semaphore("dsem%d" % i)) for i in range(n_dma_sems)]
        self.dsem_val = [0] * n_dma_sems
        self.dsem_last = [None] * n_dma_sems
        self.dnext = 0
        self.ops = []
        self.last_w = {}
        self.readers = {}
        self.count = {e: 0 for e in self.eng}
        self.last_op = {e: None for e in self.eng}
        self.arena = es.enter_context(nc.sbuf_tensor("arena", [128, ARENA_BYTES // 4], F32))
        self.cur = 0
        self.psum = [es.enter_context(nc.psum_tensor("ps%d" % i, [128, 512], F32)) for i in range(8)]
        self.ps_tiles = [Tile(p, "ps%d" % i) for i, p in enumerate(self.psum)]

    def alloc(self, free_shape, dtype, name=""):
        n = 1
        for s in free_shape:
            n *= s
        esz = 4 if dtype == F32 else 2
        nbytes = (n * esz + 63) // 64 * 64
        off = self.cur
        self.cur += nbytes
        assert self.cur <= ARENA_BYTES, "arena overflow %s: %d" % (name, self.cur)
        ap = self.arena[:, off // 4:(off + nbytes) // 4]
        if dtype != F32:
            ap = ap.bitcast(dtype)
        ap = ap[:, 0:n]
        if len(free_shape) == 2:
            ap = ap.rearrange("p (a b) -> p a b", a=free_shape[0])
        elif len(free_shape) == 3:
            ap = ap.rearrange("p (a b c) -> p a b c", a=free_shape[0], b=free_shape[1])
        return Tile(ap, name)

    def arena_reset(self, to=0):
        self.cur = to

    def op(self, eng, fn, reads=(), writes=(), dma=False):
        o = Op(eng, fn, self.count[eng], dma)
        self.count[eng] += 1
        deps = o.deps
        for t in reads:
            w = self.last_w.get(t)
            if w is not None:
                deps.add(w)
        for t in writes:
            w = self.last_w.get(t)
            if w is not None:
                deps.add(w)
            rd = self.readers.get(t)
            if rd:
                deps.update(rd.values())
        if dma:
            slot = self.dnext
            self.dnext = (self.dnext + 1) % len(self.dsem)
            prev = self.dsem_last[slot]
            if prev is not None:
                deps.add(prev)
            self.dsem_val[slot] += 16
            o.dslot = slot
            o.dval = self.dsem_val[slot]
            self.dsem_last[slot] = o
        for t in reads:
            key = ("d", id(o)) if dma else eng
            self.readers.setdefault(t, {})[key] = o
        for t in writes:
            self.last_w[t] = o
            self.readers[t] = {}
        self.ops.append(o)
        self.last_op[eng] = o
        return o

    def dma(self, eng, out, in_, reads=(), writes=()):
        return self.op(eng, lambda e: e.dma_start(out=out, in_=in_), reads, writes, dma=True)

    def mm(self, out, lhsT, rhs, start, stop, reads, writes):
        return self.op("pe", lambda e: e.matmul(out, lhsT, rhs, start=start, stop=stop), reads, writes)

    def tr(self, out, in_, ident, reads, writes):
        return self.op("pe", lambda e: e.transpose(out, in_, ident), reads, writes)

    def act(self, out, in_, func, reads, writes, bias=None, scale=None, eng="act", accum_out=None):
        kw = {}
        if bias is not None:
            kw["bias"] = bias
        if scale is not None:
            kw["scale"] = scale
        if accum_out is not None:
            kw["accum_out"] = accum_out
        return self.op(eng, lambda e: e.activation(out=out, in_=in_, func=func, **kw), reads, writes)

    def tt(self, eng, out, in0, in1, op, reads, writes):
        return self.op(eng, lambda e: e.tensor_tensor(out, in0, in1, op), reads, writes)

    def stt(self, eng, out, in0, scalar, in1, op0, op1, reads, writes):
        return self.op(eng, lambda e: e.scalar_tensor_tensor(out, in0, scalar, in1, op0, op1), reads, writes)

    def ts(self, eng, out, in0, s1, s2, op0, op1, reads, writes):
        if s2 is None:
            return self.op(eng, lambda e: e.tensor_scalar(out, in0, s1, None, op0), reads, writes)
        return self.op(eng, lambda e: e.tensor_scalar(out, in0, s1, s2, op0, op1), reads, writes)

    def copy(self, eng, out, in_, reads, writes):
        if eng == "act":
            return self.op(eng, lambda e: e.activation(out=out, in_=in_, func=AF.Copy), reads, writes)
        return self.op(eng, lambda e: e.tensor_copy(out, in_), reads, writes)

    def barrier(self):
        deps = set(o for o in self.last_op.values() if o is not None)
        deps.update(o for o in self.dsem_last if o is not None)
        for e in self.eng:
            o = Op(e, None, self.count[e], False)
            self.count[e] += 1
            o.deps = set(deps)
            self.ops.append(o)

    def finish(self):
        deps = set(o for o in self.last_op.values() if o is not None)
        deps.update(o for o in self.dsem_last if o is not None)
        o = Op("sp", None, self.count["sp"], False)
        o.deps = deps
        self.ops.append(o)
        self.emit()

    def emit(self):
        waited = {e: {} for e in self.eng}
        for o in self.ops:
            need = {}
            for d in o.deps:
                if d is o:
                    continue
                if d.dma:
                    key, val = ("d", d.dslot), d.dval
                else:
                    if d.eng == "pe" and o.eng == "pe":
                        continue
                    key, val = d.eng, d.idx
                if waited[o.eng].get(key, -1) >= val:
                    continue
                if key not in need or need[key][0] < val:
                    need[key] = (val, d)
            for key, (val, d) in need.items():
                waited[o.eng][key] = val
                if not d.dma:
                    d.signal = True
                o.waits.append(d)
        signum = {e: 0 for e in self.eng}
        n_inst = 0
        for o in self.ops:
            e = self.eng[o.eng]
            if o.signal and not o.dma:
                signum[o.eng] += 1
                o.sig = signum[o.eng]
            for d in o.waits:
                if d.dma:
                    e.wait_ge(self.dsem[d.dslot], d.dval)
                else:
                    e.wait_ge(self.sem[d.eng], d.sig)
                n_inst += 1
            if o.fn is None:
                continue
            inst = o.fn(e)
            n_inst += 1
            if o.dma:
                inst.then_inc(self.dsem[o.dslot], 16)
            elif o.signal:
                inst.then_inc(self.sem[o.eng], 1)
        self.n_inst = n_inst
        self.n_sig = dict(signum)
        self.ops = []


def load_w_bf16(P, dst, src_ap, kchunks, ncols, split=1):
    v = src_ap.rearrange("(k p) n -> p k n", p=128)
    step = max(1, kchunks // split)
    for k0 in range(0, kchunks, step):
        k1 = min(kchunks, k0 + step)
        P.dma("pool", dst[:, k0:k1, :], v[:, k0:k1, :], writes=[(dst, k) for k in range(k0, k1)])


def load_bcast(P, dst, vec_ap, n):
    P.dma("sp", dst[:, 0:n], vec_ap.partition_broadcast(128), writes=[dst])


def transpose_block(P, C, x_f32, xbf, xT, tcol, ps_t, nchunks=8, cast_eng="pool"):
    P.copy(cast_eng, xbf[:, 0:nchunks * 128], x_f32[:, 0:nchunks * 128], [x_f32], [xbf])
    psv = ps_t.ap.bitcast(BF16)
    for c in range(nchunks):
        P.tr(psv[:, c * 128:(c + 1) * 128], xbf[:, c * 128:(c + 1) * 128], C["ident"][:, :],
             [xbf, C["ident"]], [ps_t])
    P.copy("dve", xT[:, 0:nchunks, tcol:tcol + 128],
           psv[:, 0:nchunks * 128].rearrange("p (c t) -> p c t", c=nchunks), [ps_t], [xT])


def layer_norm_block(P, r, out, gbc, bbc, stats, mv, eps, D=1024):
    nch = D // 512
    for h in range(nch):
        P.op("dve", lambda e, h=h: e.bn_stats(stats[:, h, :], r[:, h * 512:(h + 1) * 512]),
             reads=[r], writes=[stats])
    P.op("dve", lambda e: e.bn_aggr(mv[:, 0:2], stats[:, 0:nch, :]), reads=[stats], writes=[mv])
    P.ts("dve", mv[:, 2:3], mv[:, 1:2], eps, None, ALU.add, None, [mv], [mv])
    P.act(mv[:, 2:3], mv[:, 2:3], AF.Sqrt, [mv], [mv])
    P.op("dve", lambda e: e.reciprocal(mv[:, 2:3], mv[:, 2:3]), reads=[mv], writes=[mv])
    P.stt("dve", mv[:, 3:4], mv[:, 0:1], -1.0, mv[:, 2:3], ALU.mult, ALU.mult, [mv], [mv])
    P.act(r[:, 0:D], r[:, 0:D], AF.Identity, [r, mv], [r], bias=mv[:, 3:4], scale=mv[:, 2:3])
    P.tt("pool", r[:, 0:D], r[:, 0:D], gbc[:, 0:D], ALU.mult, [r, gbc], [r])
    P.tt("pool", out[:, 0:D], r[:, 0:D], bbc[:, 0:D], ALU.add, [r, bbc], [out])


def tiles_of(nblocks, per):
    out = []
    b = 0
    while b < nblocks:
        n = min(per, nblocks - b)
        out.append((b, n))
        b += n
    return out


def phase_ffn(P, C, x_in, in_blk0, x_out, out_blk0, nblocks, w_in, w_out, ln_g, ln_b):
    P.barrier()
    P.arena_reset(C["arena_base"])
    NF = D_FF // 128
    Win = P.alloc([8, 2 * D_FF], BF16, "Win")
    Wout = P.alloc([NF, D_MODEL], BF16, "Wout")
    gbc = P.alloc([D_MODEL], F32, "gbc")
    bbc = P.alloc([D_MODEL], F32, "bbc")
    TB = C.get('TB', 4)
    xin1 = P.alloc([D_MODEL], F32, "xin")
    xbf1 = P.alloc([D_MODEL], BF16, "xbf")
    xres1 = P.alloc([D_MODEL], F32, "xres")
    xin = [xin1, xin1]
    xbf = [xbf1, xbf1]
    xres = [xres1, xres1]
    xT = [P.alloc([8, TB * 128], BF16, "xT%d" % i) for i in range(2)]
    gT = P.alloc([NF, TB * 128], BF16, "gT")
    sg = [P.alloc([TB * 128], F32, "sg%d" % i) for i in range(2)]
    rr = [P.alloc([D_MODEL], F32, "r%d" % i) for i in range(3)]
    stats = [P.alloc([2, 6], F32, "st%d" % i) for i in range(3)]
    mv = [P.alloc([4], F32, "mv%d" % i) for i in range(3)]

    load_w_bf16(P, Win, w_in, 8, 2 * D_FF, split=8)
    load_w_bf16(P, Wout, w_out, NF, D_MODEL, split=2)
    load_bcast(P, gbc, ln_g, D_MODEL)
    load_bcast(P, bbc, ln_b, D_MODEL)

    psG = [P.ps_tiles[0], P.ps_tiles[1]]
    psU = [P.ps_tiles[2], P.ps_tiles[3]]
    psY = [P.ps_tiles[4], P.ps_tiles[5]]
    psT = P.ps_tiles[6]
    cres = 0.5 / DN_ALPHA
    eps = LN_EPS / (DN_ALPHA * DN_ALPHA)

    tl = tiles_of(nblocks, TB)
    psYr = PsRot(P, [4, 5, 7])
    psv = psT.ap.bitcast(BF16)

    def pro_load(ti, j):
        b0, nb = tl[ti]
        g = b0 + j
        s = g % 2
        P.dma("sp", xin[s][:, :], x_in[(in_blk0 + g) * 128:(in_blk0 + g + 1) * 128, :], writes=[xin[s]])
        P.copy("pool", xbf[s][:, :], xin[s][:, :], [xin[s]], [xbf[s]])

    def pro_tr(ti, j):
        b0, nb = tl[ti]
        s = (b0 + j) % 2
        xt = xT[ti % 2]
        for c in range(8):
            P.tr(psv[:, c * 128:(c + 1) * 128], xbf[s][:, c * 128:(c + 1) * 128], C["ident"][:, :],
                 [xbf[s], C["ident"]], [psT])
        P.copy("dve", xt[:, 0:8, j * 128:(j + 1) * 128],
               psv[:, 0:1024].rearrange("p (c t) -> p c t", c=8), [psT], [xt])

    def epilogue_ln(g):
        s3 = g % 3
        layer_norm_block(P, rr[s3], rr[s3], gbc, bbc, stats[s3], mv[s3], eps)
        P.dma("pool", x_out[(out_blk0 + g) * 128:(out_blk0 + g + 1) * 128, :], rr[s3][:, :], reads=[rr[s3]])

    grp = 0
    for j in range(tl[0][1]):
        pro_load(0, j)
        pro_tr(0, j)
    pending_ln = None
    for ti, (b0, nb) in enumerate(tl):
        NT = nb * 128
        xt = xT[ti % 2]
        nxt = tl[ti + 1][1] if ti + 1 < len(tl) else 0
        for fp in range(NF):
            pg = psG[grp % 2]
            pu = psU[grp % 2]
            sgt = sg[grp % 2]
            grp += 1
            for kc in range(8):
                P.mm(pg[:, 0:NT], Win[:, kc, fp * 128:(fp + 1) * 128], xt[:, kc, 0:NT],
                     kc == 0, kc == 7, [(Win, kc), xt], [pg])
            for kc in range(8):
                P.mm(pu[:, 0:NT], Win[:, kc, D_FF + fp * 128:D_FF + (fp + 1) * 128], xt[:, kc, 0:NT],
                     kc == 0, kc == 7, [(Win, kc), xt], [pu])
            P.act(sgt[:, 0:NT], pg[:, 0:NT], AF.Silu, [pg], [sgt])
            P.tt("dve", gT[:, fp, 0:NT], sgt[:, 0:NT], pu[:, 0:NT], ALU.mult, [sgt, pu], [(gT, fp)])
            if fp >= 1 and (fp - 1) % 5 == 0 and (fp - 1) // 5 < nxt:
                pro_load(ti + 1, (fp - 1) // 5)
            if fp >= 4 and (fp - 4) % 5 == 0 and (fp - 4) // 5 < nxt:
                pro_tr(ti + 1, (fp - 4) // 5)
        for j in range(nb):
            g = b0 + j
            s = g % 2
            P.dma("sp", xres[s][:, :], x_in[(in_blk0 + g) * 128:(in_blk0 + g + 1) * 128, :],
                  writes=[xres[s]])
            for half in range(2):
                py = psYr.next()
                for fc in range(NF):
                    P.mm(py[:, 0:512], gT[:, fc, j * 128:(j + 1) * 128],
                         Wout[:, fc, half * 512:(half + 1) * 512], fc == 0, fc == NF - 1,
                         [(gT, fc), (Wout, fc)], [py])
                P.stt("dve", rr[g % 3][:, half * 512:(half + 1) * 512], py[:, 0:512], cres,
                      xres[s][:, half * 512:(half + 1) * 512], ALU.mult, ALU.add,
                      [py, xres[s]], [rr[g % 3]])
            if pending_ln is not None:
                epilogue_ln(pending_ln)
            pending_ln = g
    if pending_ln is not None:
        epilogue_ln(pending_ln)


def setup_consts(P, consts_dram):
    C = {}
    ident = P.alloc([128], BF16, "ident")
    P.dma("pool", ident[:, :], consts_dram["ident"][:, :], writes=[ident])
    C["ident"] = ident
    onesF = P.alloc([128], F32, "onesF")
    P.op("pool", lambda e: e.memset(onesF[:, :], 1.0), writes=[onesF])
    C["onesF"] = onesF
    onesB = P.alloc([128], BF16, "onesB")
    P.op("pool", lambda e: e.memset(onesB[:, :], 1.0), writes=[onesB])
    C["onesB"] = onesB
    trim = P.alloc([128], BF16, "trim")
    P.dma("pool", trim[:, :], consts_dram["trim"][:, :], writes=[trim])
    C["trim"] = trim
    C["dmask_dram"] = consts_dram["dmask"]
    C["arena_base"] = P.cur
    return C


def host_consts():
    k = np.arange(128)
    trim = (k[:, None] >= k[None, :]).astype(np.float32)
    dm = (k[:, None] < k[None, :]).astype(np.float32)
    return {"ident": np.eye(128, dtype=np.float32), "trim": trim,
            "dmask": np.ascontiguousarray(np.tile(dm, (1, 4)))}


class PsRot:
    def __init__(self, P, banks):
        self.t = [P.ps_tiles[b] for b in banks]
        self.i = 0

    def next(self):
        t = self.t[self.i % len(self.t)]
        self.i += 1
        return t


def load_xT_tile(P, C, x_dram, blk0, nb, xin, xbf, xt, psT, ctr):
    for j in range(nb):
        s = ctr[0] % 2
        ctr[0] += 1
        P.dma("sp", xin[s][:, :], x_dram[(blk0 + j) * 128:(blk0 + j + 1) * 128, :], writes=[xin[s]])
        transpose_block(P, C, xin[s], xbf[s], xt, j * 128, psT)


def phase_inproj_even(P, C, x_in, nblocks, w_in, hcT, qT, kT, vtm):
    P.barrier()
    P.arena_reset(C["arena_base"])
    NCOL = 2560
    We = P.alloc([8, NCOL], BF16, "We")
    wv = w_in.rearrange("(k p) n -> p k n", p=128)
    for k0 in range(0, 8, 2):
        P.dma("pool", We[:, k0:k0 + 2, 0:2048], wv[:, k0:k0 + 2, 0:2048],
              writes=[(We, k) for k in range(k0, k0 + 2)])
    wvv = wv[:, :, 2048:2560].rearrange("p k (h two d) -> p k two h d", two=2, d=64)
    for par in range(2):
        for k in range(8):
            P.dma("pool", We[:, k, 2048 + par * 256:2048 + (par + 1) * 256].rearrange(
                "p (h d) -> p h d", d=64), wvv[:, k, par], writes=[(We, ("v", par, k))])
    TB = 4
    xin = [P.alloc([D_MODEL], F32, "xin%d" % i) for i in range(2)]
    xbf = [P.alloc([D_MODEL], BF16, "xbf%d" % i) for i in range(2)]
    xT = [P.alloc([8, TB * 128], BF16, "xT%d" % i) for i in range(2)]
    sgm = [P.alloc([TB * 128], F32, "sgm%d" % i) for i in range(2)]
    ob = [P.alloc([TB * 128], BF16, "ob%d" % i) for i in range(4)]
    psr = PsRot(P, [0, 1, 2, 3, 4, 5])
    psT = P.ps_tiles[6]
    ctr = [0]
    oi = 0
    for ti, (b0, nb) in enumerate(tiles_of(nblocks, TB)):
        NT = nb * 128
        t0 = b0 * 128
        xt = xT[ti % 2]
        load_xT_tile(P, C, x_in, b0, nb, xin, xbf, xt, psT, ctr)
        for c in range(4):
            pa = psr.next()
            pg = psr.next()
            for kc in range(8):
                P.mm(pa[:, 0:NT], We[:, kc, c * 128:(c + 1) * 128], xt[:, kc, 0:NT],
                     kc == 0, kc == 7, [(We, kc), xt], [pa])
            for kc in range(8):
                P.mm(pg[:, 0:NT], We[:, kc, 512 + c * 128:512 + (c + 1) * 128], xt[:, kc, 0:NT],
                     kc == 0, kc == 7, [(We, kc), xt], [pg])
            sg = sgm[c % 2]
            P.act(sg[:, 0:NT], pg[:, 0:NT], AF.Sigmoid, [pg], [sg])
            o = ob[oi % 4]
            oi += 1
            P.tt("dve", o[:, 0:NT], sg[:, 0:NT], pa[:, 0:NT], ALU.mult, [sg, pa], [o])
            P.dma("sp", hcT[c * 128:(c + 1) * 128, t0:t0 + NT], o[:, 0:NT], reads=[o])
        for c in range(8):
            pq = psr.next()
            col = 1024 + c * 128
            for kc in range(8):
                P.mm(pq[:, 0:NT], We[:, kc, col:col + 128], xt[:, kc, 0:NT],
                     kc == 0, kc == 7, [(We, kc), xt], [pq])
            o = ob[oi % 4]
            oi += 1
            if c < 4:
                P.act(o[:, 0:NT], pq[:, 0:NT], AF.Copy, [pq], [o], scale=0.125)
                P.dma("sp", qT[c * 128:(c + 1) * 128, t0:t0 + NT], o[:, 0:NT], reads=[o])
            else:
                P.copy("dve", o[:, 0:NT], pq[:, 0:NT], [pq], [o])
                P.dma("sp", kT[(c - 4) * 128:(c - 3) * 128, t0:t0 + NT], o[:, 0:NT], reads=[o])
        for j in range(nb):
            pv = psr.next()
            for kc in range(8):
                P.mm(pv[:, 0:512], xt[:, kc, j * 128:(j + 1) * 128], We[:, kc, 2048:2560],
                     kc == 0, kc == 7, [(We, ("v", 0, kc)), (We, ("v", 1, kc)), xt], [pv])
            o = ob[oi % 4]
            oi += 1
            if j % 2 == 0:
                P.act(o[:, 0:512], pv[:, 0:512], AF.Copy, [pv], [o])
            else:
                P.copy("dve", o[:, 0:512], pv[:, 0:512], [pv], [o])
            P.dma("sp", vtm[(b0 + j) * 128:(b0 + j + 1) * 128, :], o[:, 0:512], reads=[o])


def phase_conv(P, C, hcT, S, w_dwT, cprm, mixT):
    P.barrier()
    P.arena_reset(C["arena_base"])
    KW = 31
    PAD = KW - 1
    hc = [P.alloc([PAD + S], BF16, "hc%d" % g) for g in range(4)]
    Dg = P.alloc([4, KW, 128], BF16, "Dg")
    wT = P.alloc([4, KW], F32, "wT")
    prm = P.alloc([4, 3], F32, "prm")
    cv = [P.alloc([512], F32, "cv%d" % g) for g in range(4)]
    sq = [P.alloc([512], F32, "sq%d" % g) for g in range(4)]
    mean_sb = P.alloc([512], F32, "mean")
    rstd = P.alloc([512], F32, "rstd")
    tmp = [P.alloc([512], F32, "tmp%d" % i) for i in range(2)]
    ob = [P.alloc([512], BF16, "ob%d" % i) for i in range(2)]
    P.dma("sp", wT[:, :, :], w_dwT[:, :, :], writes=[wT])
    P.dma("sp", prm[:, :, :], cprm[:, :, :], writes=[(prm, 0), (prm, 1), (prm, 2)])
    for g in range(4):
        P.op("pool", lambda e, g=g: e.memset(hc[g][:, 0:PAD], 0.0), writes=[(hc[g], "pad")])
        P.dma("sp", hc[g][:, PAD:PAD + S], hcT[g * 128:(g + 1) * 128, :], writes=[hc[g]])
        for k in range(KW):
            P.ts("dve", Dg[:, g, k, :], C["ident"][:, :], wT[:, g, k:k + 1], None, ALU.mult, None,
                 [C["ident"], wT], [(Dg, g)])
    psr = PsRot(P, [0, 1, 2, 3])
    psM = P.ps_tiles[4]
    psQ = P.ps_tiles[5]
    oi = 0
    for t0 in range(0, S, 512):
        N = min(512, S - t0)
        for g in range(4):
            pc = psr.next()
            for k in range(KW):
                P.mm(pc[:, 0:N], Dg[:, g, k, :], hc[g][:, t0 + k:t0 + k + N], k == 0, k == KW - 1,
                     [(Dg, g), hc[g], (hc[g], "pad")], [pc])
            P.act(cv[g][:, 0:N], pc[:, 0:N], AF.Identity, [pc, (prm, 0)], [cv[g]], bias=prm[:, g, 0:1])
            P.act(sq[g][:, 0:N], pc[:, 0:N], AF.Square, [pc, (prm, 0)], [sq[g]], bias=prm[:, g, 0:1])
        for g in range(4):
            P.mm(psM[:, 0:N], C["onesF"][:, :], cv[g][:, 0:N], g == 0, g == 3, [C["onesF"], cv[g]], [psM])
        for g in range(4):
            P.mm(psQ[:, 0:N], C["onesF"][:, :], sq[g][:, 0:N], g == 0, g == 3, [C["onesF"], sq[g]], [psQ])
        P.act(mean_sb[:, 0:N], psM[:, 0:N], AF.Copy, [psM], [mean_sb], scale=1.0 / 512)
        P.tt("dve", rstd[:, 0:N], mean_sb[:, 0:N], mean_sb[:, 0:N], ALU.mult, [mean_sb], [rstd])
        P.stt("dve", rstd[:, 0:N], psQ[:, 0:N], 1.0 / 512, rstd[:, 0:N], ALU.mult, ALU.subtract,
              [psQ, rstd], [rstd])
        P.ts("dve", rstd[:, 0:N], rstd[:, 0:N], LN_EPS, None, ALU.add, None, [rstd], [rstd])
        P.act(rstd[:, 0:N], rstd[:, 0:N], AF.Sqrt, [rstd], [rstd])
        P.op("dve", lambda e, N=N: e.reciprocal(rstd[:, 0:N], rstd[:, 0:N]), reads=[rstd], writes=[rstd])
        for g in range(4):
            tp = tmp[g % 2]
            eng = "dve" if g % 2 == 0 else "pool"
            P.tt(eng, tp[:, 0:N], cv[g][:, 0:N], mean_sb[:, 0:N], ALU.subtract, [cv[g], mean_sb], [tp])
            P.tt(eng, tp[:, 0:N], tp[:, 0:N], rstd[:, 0:N], ALU.mult, [tp, rstd], [tp])
            o = ob[oi % 2]
            oi += 1
            P.act(o[:, 0:N], tp[:, 0:N], AF.Silu, [tp, (prm, 1), (prm, 2)], [o],
                  scale=prm[:, g, 1:2], bias=prm[:, g, 2:3])
            P.dma("sp", mixT[g * 128:(g + 1) * 128, t0:t0 + N], o[:, 0:N], reads=[o])


def phase_outproj(P, C, mixT, x_res, x_out, nblocks, w_o, ln_g, ln_b):
    P.barrier()
    P.arena_reset(C["arena_base"])
    Wo = P.alloc([8, D_MODEL], BF16, "Wo")
    load_w_bf16(P, Wo, w_o, 8, D_MODEL, split=2)
    gbc = P.alloc([D_MODEL], F32, "gbc")
    bbc = P.alloc([D_MODEL], F32, "bbc")
    load_bcast(P, gbc, ln_g, D_MODEL)
    load_bcast(P, bbc, ln_b, D_MODEL)
    TB = 4
    mT = [P.alloc([8, TB * 128], BF16, "mT%d" % i) for i in range(2)]
    xres = [P.alloc([D_MODEL], F32, "xres%d" % i) for i in range(2)]
    rr = [P.alloc([D_MODEL], F32, "r%d" % i) for i in range(2)]
    stats = [P.alloc([2, 6], F32, "st%d" % i) for i in range(2)]
    mv = [P.alloc([4], F32, "mv%d" % i) for i in range(2)]
    psr = PsRot(P, [0, 1, 2, 3])
    cres = 1.0 / DN_ALPHA
    eps = LN_EPS / (DN_ALPHA * DN_ALPHA)
    for ti, (b0, nb) in enumerate(tiles_of(nblocks, TB)):
        NT = nb * 128
        mt = mT[ti % 2]
        P.dma("sp", mt[:, :, 0:NT], mixT[:, b0 * 128:b0 * 128 + NT].rearrange("(c p) t -> p c t", p=128),
              writes=[mt])
        for j in range(nb):
            g = b0 + j
            s = g % 2
            P.dma("sp", xres[s][:, :], x_res[g * 128:(g + 1) * 128, :], writes=[xres[s]])
            for half in range(2):
                py = psr.next()
                for fc in range(8):
                    P.mm(py[:, 0:512], mt[:, fc, j * 128:(j + 1) * 128],
                         Wo[:, fc, half * 512:(half + 1) * 512], fc == 0, fc == 7,
                         [mt, (Wo, fc)], [py])
                P.stt("dve", rr[s][:, half * 512:(half + 1) * 512], py[:, 0:512], cres,
                      xres[s][:, half * 512:(half + 1) * 512], ALU.mult, ALU.add,
                      [py, xres[s]], [rr[s]])
            layer_norm_block(P, rr[s], rr[s], gbc, bbc, stats[s], mv[s], eps)
            P.dma("pool", x_out[g * 128:(g + 1) * 128, :], rr[s][:, :], reads=[rr[s]])


ATT_WIN = 3


def phase_attn(P, C, qT, kT, vtm, S, mixT, row0):
    P.barrier()
    P.arena_reset(C["arena_base"])
    NBLK = S // 128
    dmask = P.alloc([512], F32, "dmask")
    P.dma("sp", dmask[:, :], C["dmask_dram"][:, :], writes=[dmask])
    C = dict(C)
    C["dmask"] = dmask
    kTs = P.alloc([4, S], BF16, "kTs")
    qTs = P.alloc([4, S], BF16, "qTs")
    vs = P.alloc([NBLK, 256], BF16, "vs")
    NB2 = 2
    ex = [[P.alloc([512], F32, "ex%d_%d" % (b, d)) for d in range(ATT_WIN)] for b in range(NB2)]
    spb = [[P.alloc([512], BF16, "sp%d_%d" % (b, d)) for d in range(ATT_WIN)] for b in range(NB2)]
    att = [[P.alloc([512], BF16, "att%d_%d" % (b, d)) for d in range(ATT_WIN)] for b in range(NB2)]
    wt = [P.alloc([512], F32, "w%d" % i) for i in range(2)]
    ot = [P.alloc([512], BF16, "ot%d" % i) for i in range(2)]
    psZ = PsRot(P, [0, 1, 2])
    psL = PsRot(P, [3, 4, 5])
    psO = PsRot(P, [6, 7])
    wi = 0
    for c in range(4):
        P.dma("sp", kTs[:, c, :], kT[c * 128:(c + 1) * 128, :], writes=[(kTs, c)])
        P.dma("sp", qTs[:, c, :], qT[c * 128:(c + 1) * 128, :], writes=[(qTs, c)])
    for par in range(2):
        pb = par * 64
        vsrc = vtm[:, par * 256:(par + 1) * 256].rearrange("(b p) f -> p b f", p=128)
        for b0 in range(0, NBLK, 8):
            b1 = min(NBLK, b0 + 8)
            P.dma("sp", vs[:, b0:b1, :], vsrc[:, b0:b1, :], writes=[(vs, b0 // 8)])
        mixv = mixT[row0:row0 + 512, :].rearrange("(h two d) t -> two d h t", two=2, d=64)[par]
        for i in range(NBLK):
            b = i % NB2
            ndk = min(ATT_WIN, i + 1)
            for dk in range(ndk):
                kb = i - dk
                pz = psZ.next()
                for hh in range(4):
                    P.mm(pz[:, hh * 128:(hh + 1) * 128], kTs[pb:pb + 64, hh, kb * 128:(kb + 1) * 128],
                         qTs[pb:pb + 64, hh, i * 128:(i + 1) * 128], True, True,
                         [(kTs, hh), (qTs, hh)], [pz])
                e_t = ex[b][dk]
                P.act(e_t[:, :], pz[:, 0:512], AF.Exp, [pz], [e_t])
                if dk == 0:
                    P.tt("pool", e_t[:, :], e_t[:, :], C["dmask"][:, :], ALU.mult, [e_t, C["dmask"]], [e_t])
                P.act(spb[b][dk][:, :], e_t[:, :], AF.Ln, [e_t], [spb[b][dk]], bias=1.0)
            for dk in range(ndk):
                pl = psL.next()
                P.mm(pl[:, 0:512], C["trim"][:, :], spb[b][dk][:, :], True, dk == 0,
                     [C["trim"], spb[b][dk]], [pl])
                for d2 in range(dk):
                    P.mm(pl[:, 0:512], C["onesB"][:, :], spb[b][d2][:, :], False, d2 == dk - 1,
                         [C["onesB"], spb[b][d2]], [pl])
                w = wt[wi % 2]
                wi += 1
                P.act(w[:, :], pl[:, 0:512], AF.Exp, [pl], [w], scale=-1.0)
                P.tt("dve", att[b][dk][:, :], ex[b][dk][:, :], w[:, :], ALU.mult, [ex[b][dk], w], [att[b][dk]])
            po = psO.next()
            for hh in range(4):
                for dk in range(ndk):
                    kb = i - dk
                    P.mm(po[0:64, hh * 128:(hh + 1) * 128], vs[:, kb, hh * 64:(hh + 1) * 64],
                         att[b][dk][:, hh * 128:(hh + 1) * 128], dk == 0, dk == ndk - 1,
                         [(vs, kb // 8), att[b][dk]], [po])
            o = ot[i % 2]
            P.copy("dve", o[0:64, :], po[0:64, 0:512], [po], [o])
            P.dma("sp", mixv[:, :, i * 128:(i + 1) * 128],
                  o[0:64, :].rearrange("d (h t) -> d h t", h=4), reads=[o])


LDC = float(np.exp(-0.5))
GN_EPS = 64 * 1e-5
NPRM1 = 42


def phase_odd_mixer(P, C, x_in, nblocks, D, mixT):
    P.barrier()
    P.arena_reset(C["arena_base"])
    A = P.alloc
    Wi = A([8, 2304], BF16, "Wi")
    load_w_bf16(P, Wi, D["w_in"], 8, 2304, split=4)
    wa2 = A([512], BF16, "wa2")
    g2s = A([512], BF16, "g2s")
    wp = A([4, 128], BF16, "wp")
    P.dma("pool", wa2[:, :], D["wa2"][:, :], writes=[wa2])
    P.dma("pool", g2s[:, :], D["g2"][:, :], writes=[g2s])
    P.dma("pool", wp[:, :, :], D["wpool"][:, :, :], writes=[wp])
    prm = A([NPRM1], F32, "prm1")
    P.dma("sp", prm[:, :], D["prm1"][:, :], writes=[prm])
    mGa = A([4, 256], F32, "mGa")
    mN = A([4, 128], F32, "mN")
    triu = A([128], F32, "triu")
    bd = A([128], F32, "bd")
    hsel = A([2], BF16, "hsel")
    icnt = A([4, 128], F32, "icnt")
    P.dma("sp", mGa[:, :, :], D["mGa"].rearrange("p (h t) -> p h t", h=4), writes=[mGa])
    P.dma("sp", mN[:, :, :], D["mN"].rearrange("p (h t) -> p h t", h=4), writes=[mN])
    P.dma("sp", triu[:, :], D["triu"][:, :], writes=[triu])
    P.dma("sp", bd[:, :], D["bd"][:, :], writes=[bd])
    P.dma("pool", hsel[:, :], D["hsel"][:, :], writes=[hsel])
    P.dma("sp", icnt[:, :, :], D["icnt"].rearrange("p (h t) -> p h t", h=4), writes=[icnt])
    cwin = A([4, 128], F32, "cwin")
    for g in range(4):
        P.op("pool", lambda e, g=g: e.memset(cwin[:, g, :], 1.0 / (2 << g)), writes=[cwin])
    w0tm = A([512], F32, "w0tm")
    lnxg = A([512], F32, "lnxg")
    lnxb = A([512], F32, "lnxb")
    load_bcast(P, w0tm, D["w0row"], 512)
    load_bcast(P, lnxg, D["lnxg"], 512)
    load_bcast(P, lnxb, D["lnxb"], 512)
    Ssh2 = [A([14, 128], F32, "Ssh%d" % i) for i in range(2)]
    ones3 = Ssh2[0]
    P.op("pool", lambda e: e.memset(ones3[:, :, :], 1.0), writes=[ones3])
    mu_bc = A([14, 128], F32, "mu_bc")
    P.tt("pool", mu_bc[:, :, :], ones3[:, :, :], prm[:, 0:14].unsqueeze(2).to_broadcast([128, 14, 128]),
         ALU.mult, [ones3, prm], [mu_bc])

    def bc4(col, name):
        t = A([4, 128], F32, name)
        P.tt("pool", t[:, :, :], ones3[:, 0:4, :],
             prm[:, col:col + 4].unsqueeze(2).to_broadcast([128, 4, 128]), ALU.mult, [ones3, prm], [t])
        return t

    w0f = bc4(14, "w0f")
    a0f = bc4(18, "a0f")
    kkb = bc4(22, "kkb")
    kab = bc4(26, "kab")
    rkb = bc4(30, "rkb")
    oma = A([4, 128], F32, "oma")
    P.ts("pool", oma[:, :, :], kab[:, :, :], -1.0, 1.0, ALU.mult, ALU.add, [kab], [oma])
    pbs = A([4], F32, "pbs")
    P.tt("pool", pbs[:, :], prm[:, 34:38], prm[:, 38:42], ALU.mult, [prm], [pbs])
    identB = C["ident"]
    identB4 = A([4, 128], BF16, "identB4")
    for h in range(4):
        P.copy("pool", identB4[:, h, :], identB[:, :], [identB], [identB4])
    identF = A([128], F32, "identF")
    P.copy("pool", identF[:, :], identB[:, :], [identB], [identF])

    pbuf = A([14, 129], F32, "pbuf")
    P.op("pool", lambda e: e.memset(pbuf[:, :, 0:1], 0.0), writes=[(pbuf, "h")])
    PADU = 16
    ubuf = A([4, PADU + 128], F32, "ubuf")
    P.op("pool", lambda e: e.memset(ubuf[:, :, 0:PADU], 0.0), writes=[(ubuf, "h")])
    Ssf = A([4, 64], F32, "Ssf")
    Sb = A([4, 64], BF16, "Sb")
    P.op("pool", lambda e: e.memset(Ssf[:, :, :], 0.0), writes=[Ssf])
    P.op("pool", lambda e: e.memset(Sb[:, :, :], 0.0), writes=[Sb])

    xin1 = A([D_MODEL], F32, "xin")
    xbf1 = A([D_MODEL], BF16, "xbf")
    xin = [xin1, xin1]
    xbf = [xbf1, xbf1]
    xT = [A([8, 128], BF16, "xT%d" % i) for i in range(2)]
    lor = A([128], BF16, "lor")
    sgb = A([128], BF16, "sgb")
    sgf = A([4, 128], F32, "sgf")
    icl = A([4, 128], F32, "icl")
    sgt = A([512], F32, "sgt")
    gate_sb = A([512], F32, "gate_sb")
    clsb = A([4, 128], F32, "clsb")
    cle = A([4, 128], F32, "cle")
    E1 = A([4, 128], F32, "E1")
    E2 = A([4, 128], F32, "E2")
    E3 = A([4, 128], F32, "E3")
    E4 = A([4, 128], F32, "E4")
    nbv = A([4], F32, "nbv")
    gC = A([4], F32, "gC")
    kk = A([4, 128], F32, "kk")
    sq = A([4, 128], F32, "sq")
    rn = A([4, 128], F32, "rn")
    t1 = A([4, 128], F32, "t1")
    kmod = A([4, 128], F32, "kmod")
    bvec = A([4, 128], F32, "bvec")
    ARf = A([4, 256], BF16, "ARf")
    ARz = [A([4, 256], BF16, "ARz%d" % p) for p in range(2)]
    Kt = A([4, 128], BF16, "Kt")
    Bt = A([4, 128], BF16, "Bt")
    Kh = A([4, 128], BF16, "Kh")
    Bh = A([4, 128], BF16, "Bh")
    rkr = A([4, 128], BF16, "rkr")
    Vtf = A([512], F32, "Vtf")
    Vtb = A([512], BF16, "Vtb")
    Kht = A([512], BF16, "Kht")
    Bht = A([512], BF16, "Bht")
    bon = A([8], F32, "bon")
    GkM = [A([4, 256], BF16, "GkM%d" % p) for p in range(2)]
    GbM = [A([4, 256], BF16, "GbM%d" % p) for p in range(2)]
    Xk = [[A([4, 128], BF16, "X%d_%d" % (p, i)) for i in range(2)] for p in range(2)]
    Nk = [[A([4, 128], BF16, "N%d_%d" % (p, i)) for i in range(2)] for p in range(2)]
    Pk = [[A([4, 128], BF16, "P%d_%d" % (p, i)) for i in range(2)] for p in range(2)]
    Qk = [[A([4, 128], BF16, "Q%d_%d" % (p, i)) for i in range(2)] for p in range(2)]
    RHSb = A([512], BF16, "RHSb")
    Ub = A([512], BF16, "Ub")
    ysb = A([512], F32, "ysb")
    ysq = A([512], F32, "ysq")
    st = A([4, 8], F32, "st")
    yob = A([512], BF16, "yob")
    yT = A([4, 128], BF16, "yT")
    pooled = A([4, 128], BF16, "pooled")
    ptmp = [A([4, PADU + 128], F32, "ptmp%d" % i) for i in range(2)]
    pout = A([4, 128], BF16, "pout")
    pm = A([2], F32, "pm")
    P.op("pool", lambda e: e.memset(pm[:, :], 0.0), writes=[pm])
    P.op("pool", lambda e: e.memset(pm[0:64, 0:1], 1.0), writes=[pm])
    P.op("pool", lambda e: e.memset(pm[64:128, 1:2], 1.0), writes=[pm])

    psrF = PsRot(P, [0, 1])
    psrR = PsRot(P, [2, 3, 4, 5])
    psrA = PsRot(P, [2, 3])
    psrB = PsRot(P, [4, 5])
    psT = P.ps_tiles[6]
    psS = P.ps_tiles[7]
    ctr = [0]

    def v3(ps, n=4, w=128):
        return ps[:, 0:n * w].rearrange("p (c t) -> p c t", c=n)

    def genA(blk):
        SshB = Ssh2[blk % 2]
        psr = psrF
        t0 = blk * 128
        xt = xT[blk % 2]
        load_xT_tile(P, C, x_in, blk, 1, xin, xbf, xt, psT, ctr)
        yield
        for grp in range(5):
            c0 = grp * 4
            ncg = min(4, 18 - c0)
            pp = psr.next()
            for ci in range(ncg):
                c = c0 + ci
                for kc in range(8):
                    P.mm(pp[:, ci * 128:(ci + 1) * 128], Wi[:, kc, c * 128:(c + 1) * 128], xt[:, kc, :],
                         kc == 0, kc == 7, [(Wi, kc), xt], [pp])
            if grp < 3:
                P.copy("dve", pbuf[:, c0:c0 + 4, 1:129], v3(pp), [pp], [pbuf])
            elif grp == 3:
                P.copy("dve", pbuf[:, 12:14, 1:129], v3(pp, 2), [pp], [pbuf])
                P.copy("dve", ubuf[:, 0:2, PADU:PADU + 128], pp[:, 256:512].rearrange("p (c t) -> p c t", c=2),
                       [pp], [ubuf])
            else:
                P.copy("dve", ubuf[:, 2:4, PADU:PADU + 128], v3(pp, 2), [pp], [ubuf])
            yield
        yield
        P.tt("dve", SshB[:, :, :], pbuf[:, :, 0:128], pbuf[:, :, 1:129], ALU.subtract,
             [pbuf, (pbuf, "h")], [SshB])
        P.tt("pool", SshB[:, :, :], SshB[:, :, :], mu_bc[:, :, :], ALU.mult, [SshB, mu_bc], [SshB])
        P.tt("dve", SshB[:, :, :], SshB[:, :, :], pbuf[:, :, 1:129], ALU.add, [SshB, pbuf], [SshB])
        P.copy("pool", pbuf[:, :, 0:1], pbuf[:, :, 128:129], [pbuf], [(pbuf, "h")])
        yield
        W = PADU + 128
        a_, b_ = ptmp
        P.tt("pool", a_[:, :, 1:W], ubuf[:, :, 1:W], ubuf[:, :, 0:W - 1], ALU.add, [ubuf, (ubuf, "h")], [a_])
        P.tt("pool", b_[:, 1:4, 3:W], a_[:, 1:4, 3:W], a_[:, 1:4, 1:W - 2], ALU.add, [a_], [b_])
        P.tt("pool", a_[:, 2:4, 7:W], b_[:, 2:4, 7:W], b_[:, 2:4, 3:W - 4], ALU.add, [b_, a_], [(a_, 1)])
        P.tt("pool", b_[:, 3:4, 15:W], a_[:, 3:4, 15:W], a_[:, 3:4, 7:W - 8], ALU.add, [a_, (a_, 1), b_], [(b_, 1)])
        srcs = [a_, b_, a_, b_]
        for g in range(4):
            win = 2 << g
            src = srcs[g]
            deps = [a_, b_, (a_, 1), (b_, 1), ubuf]
            ic = icnt if blk == 0 else cwin
            P.tt("dve", src[:, g, PADU:W], src[:, g, PADU:W], ic[:, g, :], ALU.mult, deps + [ic], [(src, "f%d" % g)])
            P.tt("dve", pooled[:, g, :], src[:, g, PADU:W], ubuf[:, g, PADU:W], ALU.subtract,
                 deps + [(src, "f%d" % g)], [(pooled, g)])
        P.copy("pool", ubuf[:, :, 0:PADU], ubuf[:, :, 128:128 + PADU], [ubuf, (pooled, 0), (pooled, 1), (pooled, 2), (pooled, 3)],
               [(ubuf, "h")])
        ppl = psr.next()
        for g in range(4):
            P.mm(ppl[:, g * 128:(g + 1) * 128], wp[:, g, :], pooled[:, g, :], True, True, [wp, (pooled, g)], [ppl])
        for g in range(4):
            P.act(pout[:, g, :], ppl[:, g * 128:(g + 1) * 128], AF.Identity, [ppl, prm, pbs], [pout],
                  scale=prm[:, 38 + g:39 + g], bias=pbs[:, g:g + 1])
        P.dma("sp", mixT[512:1024, t0:t0 + 128].rearrange("(c p) t -> p c t", p=128), pout[:, :, :], reads=[pout])

    def genR(blk):
        SshB = Ssh2[blk % 2]
        psr = psrR
        t0 = blk * 128
        r_ = SshB[:, 0:4, :]
        k_ = SshB[:, 4:8, :]
        v_ = SshB[:, 8:12, :]
        yield
        P.act(lor[0:64, :], SshB[0:64, 12, :], AF.Tanh, [SshB], [(lor, 0)])
        P.copy("dve", lor[64:128, :], SshB[64:128, 12, :], [SshB], [(lor, 1)])
        P.act(sgb[:, :], SshB[:, 13, :], AF.Sigmoid, [SshB], [sgb])
        pdw = psr.next()
        for c in range(4):
            P.mm(pdw[:, c * 128:(c + 1) * 128], wa2[0:64, c * 128:(c + 1) * 128], lor[0:64, :], True, True,
                 [wa2, (lor, 0)], [pdw])
        pda = psr.next()
        for c in range(4):
            P.mm(pda[:, c * 128:(c + 1) * 128], wa2[64:128, c * 128:(c + 1) * 128], lor[64:128, :], True, True,
                 [wa2, (lor, 1)], [pda])
        pdt = psr.next()
        P.mm(pdt[:, 0:512], lor[0:64, :], wa2[0:64, :], True, True, [wa2, (lor, 0)], [pdt])
        pgt = psr.next()
        P.mm(pgt[:, 0:512], sgb[:, :], g2s[:, :], True, True, [sgb, g2s], [pgt])
        P.tt("dve", sgf[:, :, :], v3(pdw), w0f[:, :, :], ALU.add, [pdw, w0f], [sgf])
        P.act(sgf[:, :, :], sgf[:, :, :], AF.Sigmoid, [sgf], [sgf])
        P.tt("dve", icl[:, :, :], v3(pda), a0f[:, :, :], ALU.add, [pda, a0f], [icl])
        P.act(icl[:, :, :], icl[:, :, :], AF.Sigmoid, [icl], [icl])
        P.tt("dve", sgt[:, :], pdt[:, 0:512], w0tm[:, :], ALU.add, [pdt, w0tm], [sgt])
        P.act(sgt[:, :], sgt[:, :], AF.Sigmoid, [sgt], [sgt])
        P.copy("act", gate_sb[:, :], pgt[:, 0:512], [pgt], [gate_sb])
        yield
        pcl = psr.next()
        for c in range(4):
            P.mm(pcl[:, c * 128:(c + 1) * 128], sgt[:, c * 128:(c + 1) * 128], triu[:, :], True, True,
                 [sgt, triu], [pcl])
        P.copy("dve", clsb[:, :, :], v3(pcl), [pcl], [clsb])
        P.tt("pool", cle[:, :, :], clsb[:, :, :], sgf[:, :, :], ALU.subtract, [clsb, sgf], [cle])
        P.ts("dve", nbv[:, :], clsb[:, :, 127], -LDC, None, ALU.mult, None, [clsb], [nbv])
        yield
        P.tt("pool", kk[:, :, :], k_, kkb[:, :, :], ALU.mult, [SshB, kkb], [kk])
        P.tt("pool", sq[:, :, :], kk[:, :, :], kk[:, :, :], ALU.mult, [kk], [sq])
        pss = psr.next()
        P.mm(pss[:, 0:512], bd[:, :], sq[:, :, :].rearrange("p c t -> p (c t)"), True, True, [bd, sq], [pss])
        P.ts("dve", rn[:, :, :], v3(pss), 1e-24, None, ALU.max, None, [pss], [rn])
        P.act(rn[:, :, :], rn[:, :, :], AF.Sqrt, [rn], [rn])
        P.op("dve", lambda e: e.reciprocal(rn[:, :, :], rn[:, :, :]), reads=[rn], writes=[rn])
        P.tt("dve", kk[:, :, :], kk[:, :, :], rn[:, :, :], ALU.mult, [kk, rn], [kk])
        P.tt("pool", t1[:, :, :], icl[:, :, :], kab[:, :, :], ALU.mult, [icl, kab], [t1])
        P.tt("pool", t1[:, :, :], t1[:, :, :], oma[:, :, :], ALU.add, [t1, oma], [t1])
        P.tt("dve", kmod[:, :, :], k_, t1[:, :, :], ALU.mult, [SshB, t1], [kmod])
        P.tt("pool", bvec[:, :, :], kk[:, :, :], icl[:, :, :], ALU.mult, [kk, icl], [bvec])
        yield
        P.act(E1[:, :, :], clsb[:, :, :], AF.Exp, [clsb], [E1], scale=-LDC)
        P.act(E2[:, :, :], clsb[:, :, :], AF.Exp, [clsb], [E2], scale=LDC)
        P.act(E3[:, :, :], cle[:, :, :], AF.Exp, [cle], [E3], scale=-LDC)
        for c in range(4):
            P.act(E4[:, c, :], clsb[:, c, :], AF.Exp, [clsb, nbv], [E4], scale=LDC, bias=nbv[:, c:c + 1])
        P.act(gC[:, :], nbv[:, :], AF.Exp, [nbv], [gC])
        P.stt("dve", ARf[:, :, 0:128], kk[:, :, :], -1.0, E3[:, :, :], ALU.mult, ALU.mult, [kk, E3], [(ARf, 0)])
        P.tt("pool", ARf[:, :, 128:256], r_, E1[:, :, :], ALU.mult, [SshB, E1], [(ARf, 1)])
        for p in range(2):
            P.ts("dve" if p == 0 else "pool", ARz[p][:, :, :], ARf[:, :, :], pm[:, p:p + 1], None, ALU.mult, None,
                 [(ARf, 0), (ARf, 1), pm], [ARz[p]])
        P.tt("dve", Kt[:, :, :], kmod[:, :, :], E2[:, :, :], ALU.mult, [kmod, E2], [Kt])
        P.tt("pool", Bt[:, :, :], bvec[:, :, :], E2[:, :, :], ALU.mult, [bvec, E2], [Bt])
        P.tt("dve", Kh[:, :, :], kmod[:, :, :], E4[:, :, :], ALU.mult, [kmod, E4], [Kh])
        P.tt("pool", Bh[:, :, :], bvec[:, :, :], E4[:, :, :], ALU.mult, [bvec, E4], [Bh])
        P.tt("pool", t1[:, :, :], r_, rkb[:, :, :], ALU.mult, [SshB, rkb], [t1])
        P.tt("dve", rkr[:, :, :], t1[:, :, :], kmod[:, :, :], ALU.mult, [t1, kmod], [rkr])
        yield
        pvt = psr.next()
        for c in range(4):
            P.tr(pvt[:, c * 128:(c + 1) * 128], SshB[:, 8 + c, :], identF[:, :], [SshB, identF], [pvt])
        P.copy("act", Vtf[:, :], pvt[:, 0:512], [pvt], [Vtf])
        P.copy("dve", Vtb[:, :], pvt[:, 0:512], [pvt], [Vtb])
        psv = psT.ap.bitcast(BF16)
        for c in range(4):
            P.tr(psv[:, c * 128:(c + 1) * 128], Kh[:, c, :], identB[:, :], [Kh, identB], [psT])
        for c in range(4):
            P.tr(psv[:, 512 + c * 128:512 + (c + 1) * 128], Bh[:, c, :], identB[:, :], [Bh, identB], [psT])
        P.copy("act", Kht[:, :], psv[:, 0:512], [psT], [Kht])
        P.copy("dve", Bht[:, :], psv[:, 512:1024], [psT], [Bht])
        pbn = psr.next()
        for c in range(4):
            P.mm(pbn[:, c * 2:(c + 1) * 2], rkr[:, c, :], hsel[:, :], True, True, [rkr, hsel], [pbn])
        P.copy("act", bon[:, :], pbn[:, 0:8], [pbn], [bon])
        yield
        TT = [None, None]

        def intra(par, pr):
            az = ARz[par]
            pgk = [pr.next(), pr.next()]
            for hh in range(4):
                P.mm(pgk[hh // 2][:, (hh % 2) * 256:(hh % 2 + 1) * 256], Kt[:, hh, :], az[:, hh, :], True, True,
                     [Kt, az], [pgk[hh // 2]])
            yield
            for i2 in range(2):
                P.tt("dve", GkM[par][:, 2 * i2:2 * i2 + 2, :], v3(pgk[i2], 2, 256), mGa[:, 2 * i2:2 * i2 + 2, :],
                     ALU.mult, [pgk[i2], mGa], [GkM[par]])
            pgb = [pr.next(), pr.next()]
            for hh in range(4):
                P.mm(pgb[hh // 2][:, (hh % 2) * 256:(hh % 2 + 1) * 256], Bt[:, hh, :], az[:, hh, :], True, True,
                     [Bt, az], [pgb[hh // 2]])
            yield
            for i2 in range(2):
                P.tt("dve", GbM[par][:, 2 * i2:2 * i2 + 2, :], v3(pgb[i2], 2, 256), mGa[:, 2 * i2:2 * i2 + 2, :],
                     ALU.mult, [pgb[i2], mGa], [GbM[par]])
            pn0 = pr.next()
            for hh in range(4):
                P.mm(pn0[:, hh * 128:(hh + 1) * 128], az[:, hh, 0:128], Bt[:, hh, :], True, True, [az, Bt], [pn0])
            yield
            X, N_, Pm, Q = Xk[par], Nk[par], Pk[par], Qk[par]
            P.tt("dve", N_[0][:, :, :], v3(pn0), mN[:, :, :], ALU.mult, [pn0, mN], [N_[0]])
            P.copy("pool", X[0][:, :, :], GbM[par][:, :, 0:128], [GbM[par]], [X[0]])
            P.tt("pool", Pm[0][:, :, :], X[0][:, :, :], identB4[:, :, :], ALU.add, [X[0], identB4], [Pm[0]])
            P.tt("pool", Q[0][:, :, :], N_[0][:, :, :], identB4[:, :, :], ALU.add, [N_[0], identB4], [Q[0]])
            yield
            NLEV = 6
            for lv in range(NLEV):
                a, b = lv % 2, (lv + 1) % 2
                last = lv == NLEV - 1
                px = pr.next()
                for hh in range(4):
                    P.mm(px[:, hh * 128:(hh + 1) * 128], N_[a][:, hh, :], X[a][:, hh, :], True, True,
                         [N_[a], X[a]], [px])
                if not last:
                    pn = pr.next()
                    for hh in range(4):
                        P.mm(pn[:, hh * 128:(hh + 1) * 128], X[a][:, hh, :], N_[a][:, hh, :], True, True,
                             [N_[a], X[a]], [pn])
                yield
                P.copy("act", X[b][:, :, :], v3(px), [px], [X[b]])
                if not last:
                    P.copy("dve", N_[b][:, :, :], v3(pn), [pn], [N_[b]])
                yield
                pp_ = pr.next()
                for hh in range(4):
                    P.mm(pp_[:, hh * 128:(hh + 1) * 128], Q[a][:, hh, :], X[b][:, hh, :], True, True,
                         [Q[a], X[b]], [pp_])
                if not last:
                    pq_ = pr.next()
                    for hh in range(4):
                        P.mm(pq_[:, hh * 128:(hh + 1) * 128], Pm[a][:, hh, :], N_[b][:, hh, :], True, True,
                             [Pm[a], N_[b]], [pq_])
                yield
                P.tt("dve", Pm[b][:, :, :], v3(pp_), Pm[a][:, :, :], ALU.add, [pp_, Pm[a]], [Pm[b]])
                if not last:
                    P.tt("dve", Q[b][:, :, :], v3(pq_), Q[a][:, :, :], ALU.add, [pq_, Q[a]], [Q[b]])
                yield
            TT[par] = Pm[NLEV % 2]

        gens = [intra(0, psrA), intra(1, psrB)]
        alive = [True, True]
        while any(alive):
            yield
            for gi in range(2):
                if alive[gi]:
                    try:
                        next(gens[gi])
                    except StopIteration:
                        alive[gi] = False
        yield
        prh = psr.next()
        for hh in range(4):
            for par in range(2):
                col = hh * 128 + par * 64
                P.mm(prh[:, col:col + 64], ARz[par][:, hh, 0:128], Sb[:, hh, :], True, False,
                     [ARz[par], Sb], [prh])
                P.mm(prh[:, col:col + 64], GkM[par][:, hh, 0:128], Vtb[:, col:col + 64], False, True,
                     [GkM[par], Vtb], [prh])
        P.copy("act", RHSb[:, :], prh[:, 0:512], [prh], [RHSb])
        pu = psr.next()
        for hh in range(4):
            for par in range(2):
                col = hh * 128 + par * 64
                P.mm(pu[:, col:col + 64], TT[par][:, hh, :], RHSb[:, col:col + 64], True, True,
                     [TT[par], RHSb], [pu])
        P.copy("dve", Ub[:, :], pu[:, 0:512], [pu], [Ub])
        py = psr.next()
        for hh in range(4):
            for par in range(2):
                col = hh * 128 + par * 64
                P.mm(py[:, col:col + 64], ARz[par][:, hh, 128:256], Sb[:, hh, :], True, False,
                     [ARz[par], Sb], [py])
                P.mm(py[:, col:col + 64], GkM[par][:, hh, 128:256], Vtb[:, col:col + 64], False, False,
                     [GkM[par], Vtb], [py])
                P.mm(py[:, col:col + 64], GbM[par][:, hh, 128:256], Ub[:, col:col + 64], False, True,
                     [GbM[par], Ub], [py])
        for hh in range(4):
            P.mm(psS[:, hh * 128:(hh + 1) * 128], Kht[:, hh * 128:(hh + 1) * 128], Vtb[:, hh * 128:(hh + 1) * 128],
                 True, False, [Kht, Vtb], [psS])
            P.mm(psS[:, hh * 128:(hh + 1) * 128], Bht[:, hh * 128:(hh + 1) * 128], Ub[:, hh * 128:(hh + 1) * 128],
                 False, True, [Bht, Ub], [psS])
        P.tt("pool", Ssf[:, :, :], Ssf[:, :, :], gC[:, :].unsqueeze(2).to_broadcast([128, 4, 64]), ALU.mult,
             [Ssf, gC], [Ssf])
        P.tt("dve", Ssf[0:64, :, :], Ssf[0:64, :, :], v3(psS)[0:64, :, 0:64], ALU.add, [Ssf, psS], [Ssf])
        P.tt("dve", Ssf[64:128, :, :], Ssf[64:128, :, :], v3(psS)[64:128, :, 64:128], ALU.add, [Ssf, psS], [Ssf])
        P.copy("pool", Sb[:, :, :], Ssf[:, :, :], [Ssf], [Sb])
        yield
        P.copy("act", ysb[:, :], py[:, 0:512], [py], [ysb])
        P.act(ysq[:, :], py[:, 0:512], AF.Square, [py], [ysq])
        y3 = ysb[:, :].rearrange("p (h i) -> p h i", h=8)
        P.op("dve", lambda e, y3=y3: e.tensor_reduce(st[:, 0, :], y3, AX.X, ALU.add), reads=[ysb], writes=[(st, 0)])
        P.op("dve", lambda e: e.tensor_reduce(st[:, 1, :], ysq[:, :].rearrange("p (h i) -> p h i", h=8), AX.X, ALU.add),
             reads=[ysq], writes=[(st, 1)])
        P.ts("dve", st[:, 0, :], st[:, 0, :], 1.0 / 64, None, ALU.mult, None, [(st, 0)], [(st, 0)])
        P.tt("dve", st[:, 3, :], st[:, 0, :], st[:, 0, :], ALU.mult, [(st, 0)], [(st, 3)])
        P.stt("dve", st[:, 2, :], st[:, 1, :], 1.0 / 64, st[:, 3, :], ALU.mult, ALU.subtract,
              [(st, 1), (st, 3)], [(st, 2)])
        P.ts("dve", st[:, 2, :], st[:, 2, :], GN_EPS, None, ALU.add, None, [(st, 2)], [(st, 2)])
        P.act(st[:, 2, :], st[:, 2, :], AF.Sqrt, [(st, 2)], [(st, 2)])
        P.op("dve", lambda e: e.reciprocal(st[:, 2, :], st[:, 2, :]), reads=[(st, 2)], writes=[(st, 2)])
        P.tt("dve", y3, y3, st[:, 0, :].unsqueeze(2).to_broadcast([128, 8, 64]), ALU.subtract, [ysb, (st, 0)], [ysb])
        P.tt("dve", y3, y3, st[:, 2, :].unsqueeze(2).to_broadcast([128, 8, 64]), ALU.mult, [ysb, (st, 2)], [ysb])
        P.tt("pool", ysb[:, :], ysb[:, :], lnxg[:, :], ALU.mult, [ysb, lnxg], [ysb])
        P.tt("pool", ysb[:, :], ysb[:, :], lnxb[:, :], ALU.add, [ysb, lnxb], [ysb])
        P.tt("pool", ysq[:, :].rearrange("p (h i) -> p h i", h=8), Vtf[:, :].rearrange("p (h i) -> p h i", h=8),
             bon[:, :].unsqueeze(2).to_broadcast([128, 8, 64]), ALU.mult, [Vtf, bon, ysq], [ysq])
        P.tt("pool", ysb[:, :], ysb[:, :], ysq[:, :], ALU.add, [ysb, ysq], [ysb])
        P.tt("dve", yob[:, :], ysb[:, :], gate_sb[:, :], ALU.mult, [ysb, gate_sb], [yob])
        for c in range(4):
            P.tr(psv[:, c * 128:(c + 1) * 128], yob[:, c * 128:(c + 1) * 128], identB[:, :], [yob, identB], [psT])
        P.copy("act", yT[:, :, :], psv[:, 0:512].rearrange("p (c t) -> p c t", c=4), [psT], [yT])
        P.dma("sp", mixT[0:512, t0:t0 + 128].rearrange("(c p) t -> p c t", p=128), yT[:, :, :], reads=[yT])


    def drive(gens):
        alive = [g is not None for g in gens]
        while any(alive):
            for gi, g in enumerate(gens):
                if alive[gi]:
                    try:
                        next(g)
                    except StopIteration:
                        alive[gi] = False

    drive([genA(0)])
    for blk in range(nblocks):
        drive([genR(blk), genA(blk + 1) if blk + 1 < nblocks else None])


def host_consts_odd():
    k = np.arange(128)
    strict = (k[:, None] < k[None, :]).astype(np.float32)
    incl = (k[:, None] <= k[None, :]).astype(np.float32)
    mGa = np.tile(np.concatenate([strict, incl], axis=1), (1, 4))
    mN = np.tile((k[None, :] < k[:, None]).astype(np.float32), (1, 4))
    bd = (k[:, None] // 64 == k[None, :] // 64).astype(np.float32)
    hsel = np.stack([(k < 64), (k >= 64)], axis=1).astype(np.float32)
    t = np.arange(128)
    icnt = np.concatenate([np.tile(1.0 / np.minimum(t + 1, 2 << g)[None, :], (128, 1)) for g in range(4)],
                          axis=1).astype(np.float32)
    return {"mGa": np.ascontiguousarray(mGa), "mN": np.ascontiguousarray(mN), "triu": incl, "bd": bd,
            "hsel": hsel, "icnt": np.ascontiguousarray(icnt)}


def host_odd_params(inp, i):
    def fm(v):
        return np.asarray(v, np.float32).reshape(4, 128).T
    prm1 = np.concatenate([
        np.asarray(inp["o_mu"][i], np.float32).reshape(14, 128).T,
        fm(inp["o_w0"][i]), fm(inp["o_a0"][i]), fm(inp["o_k_k"][i].reshape(512)),
        fm(inp["o_k_a"][i].reshape(512)), fm(inp["o_r_k"][i].reshape(512)),
        fm(inp["o_b_pool"][i].reshape(512)), fm(inp["o_pool_scale"][i])], axis=1)
    return {
        "w_in": np.ascontiguousarray(inp["o_w_in"][i], np.float32),
        "wa2": np.ascontiguousarray(np.concatenate([inp["o_w2"][i], inp["o_a2"][i]], axis=0), np.float32),
        "g2": np.ascontiguousarray(inp["o_g2"][i], np.float32),
        "wpool": np.ascontiguousarray(np.transpose(inp["o_w_pool"][i], (1, 0, 2)), np.float32),
        "prm1": np.ascontiguousarray(prm1, np.float32),
        "w0row": np.ascontiguousarray(inp["o_w0"][i], np.float32),
        "lnxg": np.ascontiguousarray(inp["o_lnx_g"][i].reshape(512), np.float32),
        "lnxb": np.ascontiguousarray(inp["o_lnx_b"][i].reshape(512), np.float32),
    }


def host_even_params(inp, i):
    return {
        "e_w_in": np.ascontiguousarray(inp["e_w_in"][i], np.float32),
        "e_w_out": np.ascontiguousarray(inp["e_w_out"][i], np.float32),
        "e_dwT": np.ascontiguousarray(np.asarray(inp["e_w_dw"][i], np.float32).T.reshape(4, 128, 31).transpose(1, 0, 2)),
        "e_prm": np.ascontiguousarray(np.stack([inp["e_b_dw"][i], inp["e_conv_g"][i], inp["e_conv_b"][i]], -1)
                                      .astype(np.float32).reshape(4, 128, 3).transpose(1, 0, 2)),
    }


def build_program(S, shapes):
    NBLK = S // 128
    nc = bass.Bass("TRN2", target_bir_lowering=False)
    I = {k: nc.dram_tensor(k, list(shp), F32, kind="ExternalInput").ap() for k, shp in shapes.items()}
    out = nc.dram_tensor("out", [S, D_MODEL], F32, kind="ExternalOutput").ap()

    def scratch(name, shape, dt):
        return nc.dram_tensor(name, shape, dt, kind="Internal").ap()

    xa = scratch("s_xa", [S, D_MODEL], F32)
    xb = scratch("s_xb", [S, D_MODEL], F32)
    hcT = scratch("s_hcT", [512, S], BF16)
    qT = scratch("s_qT", [512, S], BF16)
    kT = scratch("s_kT", [512, S], BF16)
    vtm = scratch("s_vtm", [S, 512], BF16)
    mixT = scratch("s_mixT", [1024, S], BF16)
    es = ExitStack()
    P = Prog(nc, es)
    C = setup_consts(P, {k[2:]: v for k, v in I.items() if k.startswith("c_")})
    g, b = I["ln_g"], I["ln_b"]
    fi, fo = I["ffn_in"], I["ffn_out"]
    phase_ffn(P, C, I["x"], 0, xa, 0, NBLK, fi[0, 0], fo[0, 0], g[0, 0], b[0, 0])
    phase_inproj_even(P, C, xa, NBLK, I["e_w_in"], hcT, qT, kT, vtm)
    phase_conv(P, C, hcT, S, I["e_dwT"], I["e_prm"], mixT)
    phase_attn(P, C, qT, kT, vtm, S, mixT, 512)
    phase_outproj(P, C, mixT, xa, xb, NBLK, I["e_w_out"], g[0, 1], b[0, 1])
    phase_ffn(P, C, xb, 0, xa, 0, NBLK, fi[0, 1], fo[0, 1], g[0, 2], b[0, 2])
    phase_ffn(P, C, xa, 0, xb, 0, NBLK, fi[1, 0], fo[1, 0], g[1, 0], b[1, 0])
    D = {k[2:]: v for k, v in I.items() if k.startswith("o_") or k.startswith("k_")}
    phase_odd_mixer(P, C, xb, NBLK, D, mixT)
    phase_outproj(P, C, mixT, xb, xa, NBLK, I["o_w_out"], g[1, 1], b[1, 1])
    phase_ffn(P, C, xa, 0, out, 0, NBLK, fi[1, 1], fo[1, 1], g[1, 2], b[1, 2])
    P.finish()
    es.close()
    return nc


ACTIVE_CORES = (0, 1, 4, 5)


def kernel(**inputs):
    inp = {k: np.asarray(v) for k, v in inputs.items()}
    x = np.asarray(inp["x"], np.float32)
    B, S, _ = x.shape
    shared = {
        "ffn_in": np.ascontiguousarray(inp["ffn_in"], np.float32),
        "ffn_out": np.ascontiguousarray(inp["ffn_out"], np.float32),
        "ln_g": np.ascontiguousarray(inp["ln_g"], np.float32),
        "ln_b": np.ascontiguousarray(inp["ln_b"], np.float32),
        "o_w_out": np.ascontiguousarray(inp["o_w_out"][0], np.float32),
    }
    shared.update(host_even_params(inp, 0))
    for k, v in host_odd_params(inp, 0).items():
        shared["o_" + k] = v
    for k, v in host_consts_odd().items():
        shared["k_" + k] = v
    for k, v in host_consts().items():
        shared["c_" + k] = v
    n_cores = 8
    assert B <= len(ACTIVE_CORES)
    in_maps = []
    zero_x = np.zeros((S, D_MODEL), np.float32)
    for c in range(n_cores):
        m = dict(shared)
        if c in ACTIVE_CORES and ACTIVE_CORES.index(c) < B:
            m["x"] = np.ascontiguousarray(x[ACTIVE_CORES.index(c)])
        else:
            m["x"] = zero_x
        in_maps.append(m)
    shapes = {k: v.shape for k, v in in_maps[0].items()}
    nc = build_program(S, shapes)
    res = run_bass_kernel_spmd(nc, in_maps, core_ids=list(range(n_cores)))
    out = np.stack([np.asarray(res.results[ACTIVE_CORES[bi]]["out"], np.float32) for bi in range(B)], axis=0)
    return out
```
